# Optimizing a Trainium2 kernel written in Bass

```python
import math
import jax, jax.numpy as jnp
from jax import lax
import numpy as np

D_MODEL = 1024
BATCH = 4
SEQ = 8192
DEPTH = 2

N_HEADS = 16
HEAD_DIM = D_MODEL // N_HEADS
HD = N_HEADS * HEAD_DIM
ROT_DIM = HEAD_DIM // 4
ROPE_THETA = 500000.0
Q_BLOCK = 128
N_KV_GROUPS = 2
HEADS_PER_GROUP = N_HEADS // N_KV_GROUPS
CMP_LEN = 32
CMP_STRIDE = 16
CMP_HIDDEN = 256
SLC_LEN = 64
SLC_TOPK = 16
WIN = 512
D_FF = 2816
CONV_W = 3
N_A_LAYERS = (DEPTH + 1) // 2
N_B_LAYERS = DEPTH // 2
RMS_EPS = 1e-6
NEG = -1e30
FORCED_SCORE = 1e6
FORGET_BIAS = 3.0

kernel_name = "yoco_fox_nsa_convffn"


def rms_norm(x, g):
    xf = x.astype(jnp.float32)
    y = xf * lax.rsqrt(jnp.mean(xf * xf, axis=-1, keepdims=True) + RMS_EPS)
    return (y * g.astype(jnp.float32)).astype(x.dtype)


def partial_rope(x, pos):
    half = ROT_DIM // 2
    inv = ROPE_THETA ** (-jnp.arange(half, dtype=jnp.float32) * (2.0 / ROT_DIM))
    ang = pos.astype(jnp.float32)[..., None] * inv
    cos = jnp.cos(ang)[..., None, :]
    sin = jnp.sin(ang)[..., None, :]
    xf = x.astype(jnp.float32)
    x1 = xf[..., :half]
    x2 = xf[..., half:ROT_DIM]
    out = jnp.concatenate([x1 * cos - x2 * sin, x2 * cos + x1 * sin, xf[..., ROT_DIM:]], axis=-1)
    return out.astype(x.dtype)


def forgetting_attention(xn, w_in, b_f, q_gain, k_gain, w_out):
    B, T, _ = xn.shape
    proj = xn @ w_in
    q = rms_norm(proj[..., :HD].reshape(B, T, N_HEADS, HEAD_DIM), q_gain)
    k = rms_norm(proj[..., HD:2 * HD].reshape(B, T, N_HEADS, HEAD_DIM), k_gain)
    v = proj[..., 2 * HD:3 * HD].reshape(B, T, N_HEADS, HEAD_DIM)
    log_f = jax.nn.log_sigmoid((proj[..., 3 * HD:] + b_f).astype(jnp.float32))
    c = jnp.cumsum(log_f, axis=1).transpose(0, 2, 1)
    q = q.transpose(0, 2, 1, 3)
    k = k.transpose(0, 2, 1, 3)
    v = v.transpose(0, 2, 1, 3)
    scale = HEAD_DIM ** -0.5
    kpos = jnp.arange(T)

    def block(i):
        q0 = i * Q_BLOCK
        qb = lax.dynamic_slice_in_dim(q, q0, Q_BLOCK, axis=2)
        cb = lax.dynamic_slice_in_dim(c, q0, Q_BLOCK, axis=2)
        s = jnp.einsum('bhqd,bhkd->bhqk', qb, k, preferred_element_type=jnp.float32) * scale
        s = s + cb[..., :, None] - c[..., None, :]
        qpos = q0 + jnp.arange(Q_BLOCK)
        s = jnp.where(kpos[None, :] <= qpos[:, None], s, -jnp.inf)
        p = jax.nn.softmax(s, axis=-1)
        return jnp.einsum('bhqk,bhkd->bqhd', p.astype(v.dtype), v)

    o = lax.map(block, jnp.arange(T // Q_BLOCK))
    o = jnp.moveaxis(o, 0, 1).reshape(B, T, HD)
    return o @ w_out


def conv_ffn(xn, w_up, conv_w, conv_b, w_down):
    T = xn.shape[1]
    u = xn @ w_up
    u_pad = jnp.pad(u, ((0, 0), (CONV_W - 1, 0), (0, 0)))
    c = sum((conv_w[j] * u_pad[:, j:j + T] for j in range(CONV_W)), conv_b)
    gate, val = jnp.split(c, 2, axis=-1)
    return (jax.nn.silu(gate) * val) @ w_down


def shared_kv(h, positions, kv_norm, kv_w, kc_pe, vc_pe, kc_w1, kc_w2, vc_w1, vc_w2, kc_gain, ks_gain, kw_gain):
    B, T, _ = h.shape
    hn = rms_norm(h, kv_norm)
    parts = (hn @ kv_w).reshape(B, T, 6, N_KV_GROUPS, HEAD_DIM)
    kc_raw, vc_raw = parts[:, :, 0], parts[:, :, 1]
    ks, vs = parts[:, :, 2], parts[:, :, 3]
    kw, vw = parts[:, :, 4], parts[:, :, 5]
    n_cmp = (T - CMP_LEN) // CMP_STRIDE + 1
    starts = jnp.arange(n_cmp) * CMP_STRIDE
    idx = starts[:, None] + jnp.arange(CMP_LEN)[None, :]

    def compress(raw, pe, w1, w2):
        blk = raw[:, idx] + pe[None, None, :, None, :]
        blk = jnp.moveaxis(blk, 3, 2).reshape(B, n_cmp, N_KV_GROUPS, CMP_LEN * HEAD_DIM)
        return jax.nn.gelu(blk @ w1) @ w2

    kc = compress(kc_raw, kc_pe, kc_w1, kc_w2)
    vc = compress(vc_raw, vc_pe, vc_w1, vc_w2)
    ends = starts + CMP_LEN - 1
    kc = partial_rope(rms_norm(kc, kc_gain), positions[:, ends])
    ks = partial_rope(rms_norm(ks, ks_gain), positions)
    kw = partial_rope(rms_norm(kw, kw_gain), positions)
    return kc, vc, ks, vs, kw, vw


def native_sparse_attention(xn, positions, kc, vc, ks, vs, kw, vw, w_in, b_gate, q_gain, w_out):
    B, T, _ = xn.shape
    G, HG, dh = N_KV_GROUPS, HEADS_PER_GROUP, HEAD_DIM
    proj = xn @ w_in
    q = partial_rope(rms_norm(proj[..., :HD].reshape(B, T, N_HEADS, dh), q_gain), positions)
    q = q.reshape(B, T, G, HG, dh).transpose(0, 2, 3, 1, 4)
    gates = jax.nn.sigmoid((proj[..., HD:] + b_gate).astype(jnp.float32)).reshape(B, T, 3, G, HG)
    n_cmp = kc.shape[1]
    n_slc = T // SLC_LEN
    top_k = min(SLC_TOPK, n_slc)
    kc_g = kc.transpose(0, 2, 1, 3)
    vc_g = vc.transpose(0, 2, 1, 3)
    ks_blk = ks.transpose(0, 2, 1, 3).reshape(B, G, n_slc, SLC_LEN * dh)
    vs_blk = vs.transpose(0, 2, 1, 3).reshape(B, G, n_slc, SLC_LEN * dh)
    pad = ((0, 0), (0, 0), (WIN, 0), (0, 0))
    kw_pad = jnp.pad(kw.transpose(0, 2, 1, 3), pad)
    vw_pad = jnp.pad(vw.transpose(0, 2, 1, 3), pad)
    cmp_start = jnp.arange(n_cmp) * CMP_STRIDE
    cmp_end = cmp_start + CMP_LEN - 1
    slc_start = jnp.arange(n_slc) * SLC_LEN
    overlap = jnp.maximum(jnp.minimum(cmp_start[:, None] + CMP_LEN, slc_start[None, :] + SLC_LEN)
                          - jnp.maximum(cmp_start[:, None], slc_start[None, :]), 0).astype(jnp.float32) / CMP_LEN
    scale = dh ** -0.5
    bidx = jnp.arange(B)[:, None, None]
    gidx = jnp.arange(G)[None, :, None]
    jb = jnp.arange(n_slc)

    def block(i):
        q0 = i * Q_BLOCK
        qpos = q0 + jnp.arange(Q_BLOCK)
        qb = lax.dynamic_slice_in_dim(q, q0, Q_BLOCK, axis=3)
        s_c = jnp.einsum('bghqd,bgkd->bghqk', qb, kc_g, preferred_element_type=jnp.float32) * scale
        m_c = cmp_end[None, :] <= qpos[:, None]
        s_c = jnp.where(m_c, s_c, NEG)
        e_c = jnp.exp(s_c - jnp.max(s_c, axis=-1, keepdims=True)) * m_c
        p_c = e_c / jnp.maximum(jnp.sum(e_c, axis=-1, keepdims=True), 1.0)
        o_c = jnp.einsum('bghqk,bgkd->bqghd', p_c.astype(vc_g.dtype), vc_g)
        imp = jnp.einsum('bghqk,kn->bgqn', p_c, overlap)
        cur = qpos // SLC_LEN
        forced = (jb[None, :] == 0) | (jb[None, :] == cur[:, None]) | (jb[None, :] == cur[:, None] - 1)
        eligible = slc_start[None, :] <= qpos[:, None]
        score = jnp.where(eligible, jnp.where(forced, FORCED_SCORE, imp), NEG)
        _, sel = lax.top_k(score, top_k)
        flat = sel.reshape(B, G, Q_BLOCK * top_k)
        k_sel = ks_blk[bidx, gidx, flat].reshape(B, G, Q_BLOCK, top_k * SLC_LEN, dh)
        v_sel = vs_blk[bidx, gidx, flat].reshape(B, G, Q_BLOCK, top_k * SLC_LEN, dh)
        tok = (sel[..., None] * SLC_LEN + jnp.arange(SLC_LEN)).reshape(B, G, Q_BLOCK, top_k * SLC_LEN)
        m_s = tok <= qpos[None, None, :, None]
        s_s = jnp.einsum('bghqd,bgqkd->bghqk', qb, k_sel, preferred_element_type=jnp.float32) * scale
        p_s = jax.nn.softmax(jnp.where(m_s[:, :, None], s_s, -jnp.inf), axis=-1)
        o_s = jnp.einsum('bghqk,bgqkd->bqghd', p_s.astype(v_sel.dtype), v_sel)
        k_win = lax.dynamic_slice_in_dim(kw_pad, q0, WIN + Q_BLOCK, axis=2)
        v_win = lax.dynamic_slice_in_dim(vw_pad, q0, WIN + Q_BLOCK, axis=2)
        kpos = q0 - WIN + jnp.arange(WIN + Q_BLOCK)
        dist = qpos[:, None] - kpos[None, :]
        m_w = (dist >= 0) & (dist < WIN) & (kpos[None, :] >= 0)
        s_w = jnp.einsum('bghqd,bgkd->bghqk', qb, k_win, preferred_element_type=jnp.float32) * scale
        p_w = jax.nn.softmax(jnp.where(m_w, s_w, -jnp.inf), axis=-1)
        o_w = jnp.einsum('bghqk,bgkd->bqghd', p_w.astype(v_win.dtype), v_win)
        g = lax.dynamic_slice_in_dim(gates, q0, Q_BLOCK, axis=1)[..., None]
        o = g[:, :, 0] * o_c + g[:, :, 1] * o_s + g[:, :, 2] * o_w
        return o.astype(xn.dtype)

    o = lax.map(block, jnp.arange(T // Q_BLOCK))
    o = jnp.moveaxis(o, 0, 1).reshape(B, T, HD)
    return o @ w_out


def setup_inputs(seed: int = 0) -> dict:
    key = jax.random.key(seed)
    ks = jax.random.split(key, 32)
    f32 = jnp.float32
    nrm = lambda k, shape, s: jax.random.normal(k, shape, f32) * s
    gain = lambda k, shape: 1.0 + 0.02 * jax.random.normal(k, shape, f32)
    D, F, G = D_MODEL, D_FF, N_KV_GROUPS
    return {
        "x": jax.random.normal(ks[0], (BATCH, SEQ, D), f32),
        "positions": jnp.broadcast_to(jnp.arange(SEQ, dtype=jnp.int32), (BATCH, SEQ)),
        "a_norm": gain(ks[1], (N_A_LAYERS, D)),
        "a_w_in": nrm(ks[2], (N_A_LAYERS, D, 3 * HD + N_HEADS), D ** -0.5),
        "a_b_f": FORGET_BIAS + 0.1 * jax.random.normal(ks[3], (N_A_LAYERS, N_HEADS), f32),
        "a_q_gain": gain(ks[4], (N_A_LAYERS, HEAD_DIM)),
        "a_k_gain": gain(ks[5], (N_A_LAYERS, HEAD_DIM)),
        "a_w_out": nrm(ks[6], (N_A_LAYERS, HD, D), HD ** -0.5),
        "kv_norm": gain(ks[7], (D,)),
        "kv_w": nrm(ks[8], (D, 6 * G * HEAD_DIM), D ** -0.5),
        "kc_pe": nrm(ks[9], (CMP_LEN, HEAD_DIM), 0.02),
        "vc_pe": nrm(ks[10], (CMP_LEN, HEAD_DIM), 0.02),
        "kc_w1": nrm(ks[11], (CMP_LEN * HEAD_DIM, CMP_HIDDEN), (CMP_LEN * HEAD_DIM) ** -0.5),
        "kc_w2": nrm(ks[12], (CMP_HIDDEN, HEAD_DIM), CMP_HIDDEN ** -0.5),
        "vc_w1": nrm(ks[13], (CMP_LEN * HEAD_DIM, CMP_HIDDEN), (CMP_LEN * HEAD_DIM) ** -0.5),
        "vc_w2": nrm(ks[14], (CMP_HIDDEN, HEAD_DIM), CMP_HIDDEN ** -0.5),
        "kc_gain": gain(ks[15], (HEAD_DIM,)),
        "ks_gain": gain(ks[16], (HEAD_DIM,)),
        "kw_gain": gain(ks[17], (HEAD_DIM,)),
        "b_norm": gain(ks[18], (N_B_LAYERS, D)),
        "b_w_in": nrm(ks[19], (N_B_LAYERS, D, HD + 3 * N_HEADS), D ** -0.5),
        "b_b_gate": nrm(ks[20], (N_B_LAYERS, 3 * N_HEADS), 0.01),
        "b_q_gain": gain(ks[21], (N_B_LAYERS, HEAD_DIM)),
        "b_w_out": nrm(ks[22], (N_B_LAYERS, HD, D), HD ** -0.5),
        "f_norm": gain(ks[23], (DEPTH, D)),
        "f_w_up": nrm(ks[24], (DEPTH, D, 2 * F), D ** -0.5),
        "f_conv_w": nrm(ks[25], (DEPTH, CONV_W, 2 * F), CONV_W ** -0.5),
        "f_conv_b": nrm(ks[26], (DEPTH, 2 * F), 0.01),
        "f_w_down": nrm(ks[27], (DEPTH, F, D), F ** -0.5),
    }


def reference(x, positions, a_norm, a_w_in, a_b_f, a_q_gain, a_k_gain, a_w_out, kv_norm, kv_w, kc_pe, vc_pe, kc_w1, kc_w2, vc_w1, vc_w2, kc_gain, ks_gain, kw_gain, b_norm, b_w_in, b_b_gate, b_q_gain, b_w_out, f_norm, f_w_up, f_conv_w, f_conv_b, f_w_down):
    h = x
    kv = None
    for layer in range(DEPTH):
        if layer < N_A_LAYERS:
            i = layer
            h = h + forgetting_attention(rms_norm(h, a_norm[i]), a_w_in[i], a_b_f[i], a_q_gain[i], a_k_gain[i], a_w_out[i])
        else:
            i = layer - N_A_LAYERS
            kc, vc, ks, vs, kw, vw = kv
            h = h + native_sparse_attention(rms_norm(h, b_norm[i]), positions, kc, vc, ks, vs, kw, vw,
                                            b_w_in[i], b_b_gate[i], b_q_gain[i], b_w_out[i])
        h = h + conv_ffn(rms_norm(h, f_norm[layer]), f_w_up[layer], f_conv_w[layer], f_conv_b[layer], f_w_down[layer])
        if layer == N_A_LAYERS - 1:
            kv = shared_kv(h, positions, kv_norm, kv_w, kc_pe, vc_pe, kc_w1, kc_w2, vc_w1, vc_w2, kc_gain, ks_gain, kw_gain)
    return h
```

```python
import contextlib
import numpy as np
import ml_dtypes
import concourse.bass as bass
import concourse.mybir as mybir
from concourse.bass_utils import run_bass_kernel_spmd

F32 = mybir.dt.float32
BF16 = mybir.dt.bfloat16
I32 = mybir.dt.int32
AF = mybir.ActivationFunctionType
ALU = mybir.AluOpType
AX = mybir.AxisListType

EPOCH = 8192
NDMASEM = 24

D = 1024
T = 8192
NH = 16
DH = 64
DFF = 2816
NFC = 44
EPS = 1e-6


class Sched:
    ENGS = ("pe", "act", "dve", "pool", "sp")

    def __init__(self, nc, tag=""):
        self.nc = nc
        self.tag = tag
        self.ops = {e: [] for e in self.ENGS}
        self.cnt = {e: 0 for e in self.ENGS}
        self.dcnt = {e: 0 for e in self.ENGS}
        self.lastw = {}
        self.readers = {}
        self.known = {e: {} for e in self.ENGS}
        self.sems = {}
        self.final_tokens = []

    def _tok_compute(self, eng):
        i = self.cnt[eng]
        self.cnt[eng] += 1
        return (("c", eng, i // EPOCH), i % EPOCH + 1)

    def _tok_dma(self, q):
        i = self.dcnt[q]
        self.dcnt[q] += 1
        return (("d", q, i % NDMASEM), 16 * (i // NDMASEM + 1))

    def _need(self, eng, waits, tok):
        sk, v = tok
        if self.known[eng].get(sk, 0) >= v:
            return
        waits[sk] = max(waits.get(sk, 0), v)

    def _deps(self, eng, reads, writes, is_dma):
        waits = {}

        def same_pe(t):
            return eng == "pe" and (not is_dma) and t[0][0] == "c" and t[0][1] == "pe"

        for r in reads:
            t = self.lastw.get(r)
            if t is not None and not same_pe(t):
                self._need(eng, waits, t)
        for w in writes:
            t = self.lastw.get(w)
            if t is not None and not same_pe(t):
                self._need(eng, waits, t)
            for t in self.readers.get(w, ()):
                if (not is_dma) and t[0][0] == "c" and t[0][1] == eng:
                    continue
                self._need(eng, waits, t)
        for sk, v in waits.items():
            self.known[eng][sk] = v
        return waits

    def _commit(self, tok, reads, writes):
        for r in reads:
            self.readers.setdefault(r, []).append(tok)
        for w in writes:
            self.lastw[w] = tok
            self.readers[w] = []

    def op(self, eng, emit, reads=(), writes=()):
        waits = self._deps(eng, reads, writes, False)
        tok = self._tok_compute(eng)
        self.ops[eng].append((waits, emit, tok))
        self._commit(tok, reads, writes)
        return tok

    def dma(self, q, emit, reads=(), writes=(), final=False):
        waits = self._deps(q, reads, writes, True)
        tok = self._tok_dma(q)
        sk, v = tok
        if v > 16 and self.known[q].get(sk, 0) < v - 16:
            waits[sk] = max(waits.get(sk, 0), v - 16)
            self.known[q][sk] = v - 16
        self.ops[q].append((waits, emit, tok))
        self._commit(tok, reads, writes)
        if final:
            self.final_tokens.append(tok)
        return tok

    def emit_all(self):
        nc = self.nc
        semkeys = set()
        for e in self.ENGS:
            for waits, emit, tok in self.ops[e]:
                semkeys.add(tok[0])
                semkeys.update(waits.keys())
        semkeys = sorted(semkeys, key=str)
        with contextlib.ExitStack() as st:
            for sk in semkeys:
                self.sems[sk] = nc.alloc_semaphore(name=self.tag + "s_" + "_".join(str(x) for x in sk))
            block = st.enter_context(nc.Block())
            sems = self.sems

            def run(engname, eng):
                for waits, emit, tok in self.ops[engname]:
                    for sk, v in waits.items():
                        eng.wait_ge(sems[sk], v)
                    ins = emit(eng)
                    ins.then_inc(sems[tok[0]], 16 if tok[0][0] == "d" else 1)
                if engname == "sp":
                    for sk, v in self.final_tokens:
                        eng.wait_ge(sems[sk], v)

            @block.tensor
            def _(eng):
                run("pe", eng)

            @block.scalar
            def _(eng):
                run("act", eng)

            @block.vector
            def _(eng):
                run("dve", eng)

            @block.gpsimd
            def _(eng):
                run("pool", eng)

            @block.sync
            def _(eng):
                run("sp", eng)


class Prog:
    def __init__(self, nc, tag, io):
        self.nc = nc
        self.tag = tag
        self.io = io
        self.st = contextlib.ExitStack()
        self.s = Sched(self.nc, tag)

    def din(self, name, shape, dt=F32):
        ap = self.io[name]
        assert list(ap.shape) == list(shape), (name, ap.shape, shape)
        return ap

    dout = din

    def sb(self, name, shape, dt):
        return self.st.enter_context(self.nc.sbuf_tensor(self.tag + name, list(shape), dt))

    def ps(self, name, shape, dt=F32):
        return self.st.enter_context(self.nc.psum_tensor(self.tag + name, list(shape), dt))

    def finish(self):
        import os
        if os.environ.get("KDEBUG"):
            print("phase", self.tag, "sbuf remaining", self.nc.sbuf_bytes_remaining, flush=True)
        self.s.emit_all()
        self.st.close()
        return self.nc


def make_ident(P, dt, name):
    s = P.s
    idf = P.sb(name + "_f", [128, 128], F32)
    ident = P.sb(name, [128, 128], dt)
    s.op("pool", lambda e: e.memset(idf[:], 1.0), writes=[name + "_f"])
    s.op("pool", lambda e: e.affine_select(out=idf[:], in_=idf[:], pattern=[[-1, 128]],
                                           compare_op=ALU.is_equal, fill=0.0, base=0, channel_multiplier=1),
         reads=[name + "_f"], writes=[name + "_f"])
    s.op("pool", lambda e: e.tensor_copy(out=ident[:], in_=idf[:]), reads=[name + "_f"], writes=[name])
    return ident


def build_ffn(P, with_kv, NT, hin_int):
    nc, s = P.nc, P.s
    NTO = NT - 128
    msk = P.din("msk", [128, 2])
    if hin_int:
        h_my = P.din("h_my", [NTO, D])
        hl_all = P.din("hl_all", [256, D])
    else:
        hin = P.din("hin", [NT, D])
    oT_l = P.io["oT_all_l"]
    w_out = P.din("w_out", [D, D])
    gl = P.din("gl", [128, 8])
    w_up = P.din("w_up", [D, 2 * DFF])
    cw = P.din("cw", [128, NFC * 4])
    w_dn = P.din("w_dn", [DFF, D])
    hout = P.dout("hout", [NTO, D])
    if with_kv:
        kvg = P.din("kvg", [128, 8])
        kv_w = P.din("kv_w", [D, 768])
        bg = P.din("bg", [128, 8])
        kv_send_l = P.io["kv_send_l"]
        xnT_my_l = P.io["xnT_my_l"]
        hl_send = P.dout("hl_send", [128, D])

    W = 512
    ident = make_ident(P, BF16, "ident")
    gl_sb = P.sb("gl_sb", [128, 8], F32)
    cw_sb = P.sb("cw_sb", [128, NFC * 4], F32)
    s.dma("sp", lambda q: q.dma_start(out=gl_sb[:], in_=gl), writes=["gl"])
    s.dma("sp", lambda q: q.dma_start(out=cw_sb[:], in_=cw), writes=["cw"])
    msk_sb = P.sb("msk_sb", [128, 2], F32)
    s.dma("sp", lambda q: q.dma_start(out=msk_sb[:], in_=msk), writes=["msk"])
    if with_kv:
        kvg_sb = P.sb("kvg_sb", [128, 8], F32)
        s.dma("sp", lambda q: q.dma_start(out=kvg_sb[:], in_=kvg), writes=["kvg"])
        bg_sb = P.sb("bg_sb", [128, 8], F32)
        s.dma("sp", lambda q: q.dma_start(out=bg_sb[:], in_=bg), writes=["bg"])

    oT_sb = P.sb("oT_sb", [128, 8, W], BF16)
    oT_a = P.sb("oT_a", [128, 8, W], BF16)
    wbig = P.sb("wbig", [128, 22 * 1024], BF16)
    wstage = [P.sb(f"wstage{i}", [128, 2048], F32) for i in range(2)]
    hin_sb = [P.sb(f"hin_sb{i}", [128, D], F32) for i in range(2)]
    hmid = P.sb("hmid", [128, 4, D], F32)
    junk = P.sb("junk", [128, D], BF16)
    stat = P.sb("stat", [128, 8], F32)
    xs = [P.sb(f"xs{i}", [128, D], BF16) for i in range(2)]
    hnT = P.sb("hnT", [128, 8, W], BF16)
    wup_bf = [P.sb(f"wup_bf{i}", [128, 8, 256], BF16) for i in range(2)]
    ubuf = [P.sb(f"ubuf{i}", [128, 2 + W], F32) for i in range(2)]
    halo = P.sb("halo", [128, NFC, 2], F32)
    ctmp = [P.sb(f"ctmp{i}", [128, W], F32) for i in range(4)]
    sg = P.sb("sg", [128, W], F32)
    gT = P.sb("gT", [128, 22, W], BF16)
    otile = [P.sb(f"otile{i}", [128, D], F32) for i in range(2)]
    if with_kv:
        hkT = P.sb("hkT", [128, 8, W], BF16)
        hbT = P.sb("hbT", [128, 8, W], BF16)
        kvtile = [P.sb(f"kvtile{i}", [128, 768], F32) for i in range(2)]

    ps_o = [P.ps(f"ps_o{i}", [128, 512]) for i in range(2)]
    ps_t = P.ps("ps_t", [128, 8, 128], BF16)
    ps_u = [P.ps(f"ps_u{i}", [128, 512]) for i in range(4)]

    s.op("pool", lambda e: e.memset(halo[:], 0.0), writes=["halo"])

    w_out_v = w_out.rearrange("(k p) n -> p k n", p=128)
    w_up_v = w_up.rearrange("(k p) n -> p k n", p=128)
    w_dn_v = w_dn.rearrange("(k p) n -> p k n", p=128)
    def oT_load(q, dst, gcol, Wc):
        ins = None
        for kc in range(8):
            ins = q.dma_start(out=dst[:, kc, 0:Wc], in_=oT_l[kc % 4][(kc // 4) * 128:(kc // 4 + 1) * 128, gcol:gcol + Wc])
        return ins
    if with_kv:
        kv_w_v = kv_w.rearrange("(k p) n -> p k n", p=128)

    stage_ctr = [0]
    wout_s, wup_s, wdn_s = P.io["wout_s"], P.io["wup_s"], P.io["wdn_s"]
    kvw_s = P.io["kvw_s"] if with_kv else None
    cbuf = [P.sb(f"cbuf{i}", [128, 2048], BF16) for i in range(3)]
    skeys = []
    cast_ctr = [0]

    def precast(src_ap, n, dst_ap, view=None):
        c = cast_ctr[0]
        cast_ctr[0] += 1
        i, ci = c % 2, c % 3
        st_t, cb = wstage[i], cbuf[ci]
        s.dma("sp", lambda q: q.dma_start(out=st_t[:, 0:n], in_=src_ap), writes=[f"wstage{i}"])
        eng = ("pool", "act", "dve")[c % 3]
        if eng == "act":
            s.op("act", lambda e: e.activation(out=cb[:, 0:n], in_=st_t[:, 0:n], func=AF.Copy), reads=[f"wstage{i}"], writes=[f"cbuf{ci}"])
        else:
            s.op(eng, lambda e: e.tensor_copy(out=cb[:, 0:n], in_=st_t[:, 0:n]), reads=[f"wstage{i}"], writes=[f"cbuf{ci}"])
        key = f"ws{c}"
        skeys.append(key)
        srcv = cb[:, 0:n] if view is None else view(cb[:, 0:n])
        s.dma("sp", lambda q: q.dma_start(out=dst_ap, in_=srcv), reads=[f"cbuf{ci}"], writes=[key])

    if not P.io.get("precast_done"):
        for kc in range(8):
            precast(w_out[kc * 128:(kc + 1) * 128, :], 1024, wout_s[kc * 128:(kc + 1) * 128, :])
        for kc in range(8):
            for cb_ in range(4):
                precast(w_up[kc * 128:(kc + 1) * 128, cb_ * 1408:(cb_ + 1) * 1408], 1408, wup_s[:, cb_ * 11:(cb_ + 1) * 11, kc, :],
                        view=lambda a: a.rearrange("p (f n) -> p f n", f=11))
        for j in range(22):
            precast(w_dn[j * 128:(j + 1) * 128, :], 1024, wdn_s[j * 128:(j + 1) * 128, :])
        if with_kv:
            for kc in range(8):
                precast(kv_w[kc * 128:(kc + 1) * 128, :], 768, kvw_s[kc * 128:(kc + 1) * 128, :])
    wout_sv = wout_s.rearrange("(k p) n -> p k n", p=128)
    wdn_sv = wdn_s.rearrange("(j p) n -> p j n", p=128)
    if with_kv:
        kvw_sv = kvw_s.rearrange("(k p) n -> p k n", p=128)

    def load_cast(dst_ap, src_ap, ncols, dstkey):
        i = stage_ctr[0] % 2
        stage_ctr[0] += 1
        st_t = wstage[i]
        s.dma("sp", lambda q: q.dma_start(out=st_t[:, 0:ncols], in_=src_ap), writes=[f"wstage{i}"])
        s.op("pool", lambda e: e.tensor_copy(out=dst_ap, in_=st_t[:, 0:ncols]), reads=[f"wstage{i}"], writes=[dstkey])

    def norm_transpose(src_tile, src_key, dstT, dst_key, col0, gain_sb, gain_key, par, second=None):
        x_s = xs[par]
        s.op("act", lambda e: e.activation(out=junk[:], in_=src_tile, func=AF.Square, accum_out=stat[:, 0:1]),
             reads=[src_key], writes=["junk", "stat0"])
        s.op("dve", lambda e: e.tensor_scalar(out=stat[:, 1:2], in0=stat[:, 0:1], scalar1=1.0 / D, scalar2=EPS,
                                               op0=ALU.mult, op1=ALU.add), reads=["stat0"], writes=["stat1"])
        s.op("act", lambda e: e.activation(out=stat[:, 2:3], in_=stat[:, 1:2], func=AF.Sqrt), reads=["stat1"], writes=["stat2"])
        s.op("dve", lambda e: e.reciprocal(out=stat[:, 3:4], in_=stat[:, 2:3]), reads=["stat2"], writes=["stat3"])
        s.op("dve", lambda e: e.tensor_scalar(out=x_s[:], in0=src_tile, scalar1=stat[:, 3:4], scalar2=None, op0=ALU.mult),
             reads=[src_key, "stat3"], writes=[f"xs{par}"])
        for kc in range(8):
            s.op("pe", lambda e, kc=kc: e.transpose(out=ps_t[:, kc, :], in_=x_s[:, kc * 128:(kc + 1) * 128], identity=ident[:]),
                 reads=[f"xs{par}", "ident"], writes=["ps_t"])
        for kc in range(8):
            s.op("act", lambda e, kc=kc: e.activation(out=dstT[:, kc, col0:col0 + 128], in_=ps_t[:, kc, :], func=AF.Copy,
                                                       scale=gain_sb[:, kc:kc + 1]),
                 reads=["ps_t", gain_key], writes=[dst_key])
        if second is not None:
            d2, k2, g2, gk2 = second
            for kc in range(8):
                s.op("act", lambda e, kc=kc: e.activation(out=d2[:, kc, col0:col0 + 128], in_=ps_t[:, kc, :], func=AF.Copy,
                                                           scale=g2[:, kc:kc + 1]),
                     reads=["ps_t", gk2], writes=[k2])

    chunks = [(0, 128)] + [(128 + 512 * i, 512) for i in range((NT - 128) // 512)]
    tile_ctr = 0
    for (c0, Wc) in chunks:
        nt = Wc // 128
        s.dma("sp", lambda q: q.dma_start(out=wbig[:, 0:8192].rearrange("p (k n) -> p k n", k=8), in_=wout_sv), reads=skeys, writes=["wbig"])
        g1 = 4096 - 128 + c0
        for kc in range(8):
            s.dma("sp", lambda q, g1=g1, Wc=Wc, kc=kc: q.dma_start(out=oT_sb[:, kc, 0:Wc],
                                                                 in_=oT_l[kc % 4][(kc // 4) * 128:(kc // 4 + 1) * 128, g1:g1 + Wc]), writes=["oT_sb"])
        s.op("dve", lambda e, Wc=Wc: e.tensor_scalar(out=oT_sb[:, :, 0:Wc], in0=oT_sb[:, :, 0:Wc], scalar1=msk_sb[:, 1:2], scalar2=None,
                                                     op0=ALU.mult), reads=["oT_sb", "msk"], writes=["oT_sb"])
        if c0 >= 128:
            g0 = c0 - 128
            for kc in range(8):
                s.dma("sp", lambda q, g0=g0, Wc=Wc, kc=kc: q.dma_start(out=oT_a[:, kc, 0:Wc],
                                                                     in_=oT_l[kc % 4][(kc // 4) * 128:(kc // 4 + 1) * 128, g0:g0 + Wc]), writes=["oT_a"])
            s.op("dve", lambda e, Wc=Wc: e.scalar_tensor_tensor(out=oT_sb[:, :, 0:Wc], in0=oT_a[:, :, 0:Wc], scalar=msk_sb[:, 0:1],
                                                                in1=oT_sb[:, :, 0:Wc], op0=ALU.mult, op1=ALU.add),
                 reads=["oT_a", "oT_sb", "msk"], writes=["oT_sb"])
        for i in range(nt):
            par = tile_ctr % 2
            tile_ctr += 1
            r0 = c0 + i * 128
            hs = hin_sb[par]
            if not hin_int:
                s.dma("sp", lambda q, hs=hs, r0=r0: q.dma_start(out=hs[:], in_=hin[r0:r0 + 128, :]), writes=[f"hin_sb{par}"])
            elif r0 < 128:
                s.dma("sp", lambda q, hs=hs: q.dma_start(out=hs[:], in_=hl_all[0:128, :]), writes=[f"hin_sb{par}"])
                s.op("dve", lambda e, hs=hs: e.tensor_scalar(out=hs[:], in0=hs[:], scalar1=msk_sb[:, 1:2], scalar2=None, op0=ALU.mult),
                     reads=[f"hin_sb{par}", "msk"], writes=[f"hin_sb{par}"])
            else:
                s.dma("sp", lambda q, hs=hs, r0=r0: q.dma_start(out=hs[:], in_=h_my[r0 - 128:r0, :]), writes=[f"hin_sb{par}"])
            for hf in range(2):
                for kc in range(8):
                    s.op("pe", lambda e, kc=kc, hf=hf, i=i: e.matmul(ps_o[hf][:], lhsT=oT_sb[:, kc, i * 128:(i + 1) * 128],
                                                                    rhs=wbig[:, kc * 1024 + hf * 512: kc * 1024 + hf * 512 + 512],
                                                                    start=(kc == 0), stop=(kc == 7)),
                         reads=["oT_sb", "wbig"], writes=[f"ps_o{hf}"])
                s.op("dve", lambda e, hf=hf, i=i, hs=hs: e.tensor_tensor(out=hmid[:, i, hf * 512:(hf + 1) * 512], in0=ps_o[hf][:],
                                                                          in1=hs[:, hf * 512:(hf + 1) * 512], op=ALU.add),
                     reads=[f"ps_o{hf}", f"hin_sb{par}"], writes=[f"hmid{i}"])
            norm_transpose(hmid[:, i, :], f"hmid{i}", hnT, "hnT", i * 128, gl_sb, "gl", par)
        for j0 in (0, 6, 12, 18):
            j1 = min(j0 + 6, 22)
            s.dma("sp", lambda q, j0=j0, j1=j1: q.dma_start(out=wbig[:, j0 * 1024:j1 * 1024].rearrange("p (j n) -> p j n", j=j1 - j0),
                                                          in_=wdn_sv[:, j0:j1, :]), reads=skeys, writes=["wbig"])
        for j in range(22):
            wb = wup_bf[j % 2]
            wkey = f"wup_bf{j % 2}"
            for part in range(2):
                s.dma("sp", lambda q, wb=wb, part=part, j=j: q.dma_start(out=wb[:, :, part * 128:(part + 1) * 128], in_=wup_s[:, part * 22 + j, :, :]),
                      reads=skeys, writes=[wkey + f"_{part}"])
            for part in range(2):
                idx = part * 22 + j
                pu = ps_u[(j % 2) * 2 + part]
                pukey = f"ps_u{(j % 2) * 2 + part}"
                ub = ubuf[part]
                ubk = f"ubuf{part}"
                for kc in range(8):
                    s.op("pe", lambda e, kc=kc, pu=pu, wb=wb, part=part, Wc=Wc: e.matmul(
                        pu[:, 0:Wc], lhsT=wb[:, kc, part * 128:(part + 1) * 128], rhs=hnT[:, kc, 0:Wc],
                        start=(kc == 0), stop=(kc == 7)),
                        reads=[wkey + f"_{part}", "hnT"], writes=[pukey])
                s.op("act", lambda e, ub=ub, pu=pu, Wc=Wc: e.activation(out=ub[:, 2:2 + Wc], in_=pu[:, 0:Wc], func=AF.Copy),
                     reads=[pukey], writes=[ubk])
                s.op("pool", lambda e, ub=ub, idx=idx: e.tensor_copy(out=ub[:, 0:2], in_=halo[:, idx, :]),
                     reads=["halo%d" % idx], writes=[ubk + "h"])
                s.op("pool", lambda e, ub=ub, idx=idx, Wc=Wc: e.tensor_copy(out=halo[:, idx, :], in_=ub[:, Wc:Wc + 2]),
                     reads=[ubk, ubk + "h"], writes=["halo%d" % idx])
                c1, c2, c3 = ctmp[part * 2], ctmp[part * 2 + 1], ctmp[part * 2]
                k1, k2 = f"ctmp{part * 2}", f"ctmp{part * 2 + 1}"
                s.op("dve", lambda e, ub=ub, c1=c1, idx=idx, Wc=Wc: e.tensor_scalar(
                    out=c1[:, 0:Wc], in0=ub[:, 2:2 + Wc], scalar1=cw_sb[:, idx * 4 + 2:idx * 4 + 3],
                    scalar2=cw_sb[:, idx * 4 + 3:idx * 4 + 4], op0=ALU.mult, op1=ALU.add),
                    reads=[ubk, "cw"], writes=[k1])
                s.op("dve", lambda e, ub=ub, c1=c1, c2=c2, idx=idx, Wc=Wc: e.scalar_tensor_tensor(
                    out=c2[:, 0:Wc], in0=ub[:, 1:1 + Wc], scalar=cw_sb[:, idx * 4 + 1:idx * 4 + 2], in1=c1[:, 0:Wc],
                    op0=ALU.mult, op1=ALU.add), reads=[ubk, ubk + "h", "cw", k1], writes=[k2])
                s.op("dve", lambda e, ub=ub, c2=c2, c3=c3, idx=idx, Wc=Wc: e.scalar_tensor_tensor(
                    out=c3[:, 0:Wc], in0=ub[:, 0:Wc], scalar=cw_sb[:, idx * 4:idx * 4 + 1], in1=c2[:, 0:Wc],
                    op0=ALU.mult, op1=ALU.add), reads=[ubk, ubk + "h", "cw", k2], writes=[k1])
            s.op("act", lambda e, Wc=Wc: e.activation(out=sg[:, 0:Wc], in_=ctmp[0][:, 0:Wc], func=AF.Silu),
                 reads=["ctmp0"], writes=["sg"])
            s.op("pool", lambda e, j=j, Wc=Wc: e.tensor_tensor(out=gT[:, j, 0:Wc], in0=sg[:, 0:Wc], in1=ctmp[2][:, 0:Wc], op=ALU.mult),
                 reads=["sg", "ctmp2"], writes=["gT"])
        for i in range(nt):
            par = tile_ctr % 2
            tile_ctr += 1
            r0 = c0 + i * 128
            ot = otile[par]
            for hf in range(2):
                for j in range(22):
                    s.op("pe", lambda e, j=j, hf=hf, i=i: e.matmul(ps_o[hf][:], lhsT=gT[:, j, i * 128:(i + 1) * 128],
                                                                  rhs=wbig[:, j * 1024 + hf * 512: j * 1024 + hf * 512 + 512],
                                                                  start=(j == 0), stop=(j == 21)),
                         reads=["gT", "wbig"], writes=[f"ps_o{hf}"])
                s.op("dve", lambda e, hf=hf, i=i, ot=ot: e.tensor_tensor(out=ot[:, hf * 512:(hf + 1) * 512], in0=ps_o[hf][:],
                                                                          in1=hmid[:, i, hf * 512:(hf + 1) * 512], op=ALU.add),
                     reads=[f"ps_o{hf}", f"hmid{i}"], writes=[f"otile{par}"])
            if r0 >= 128:
                s.dma("sp", lambda q, ot=ot, r0=r0: q.dma_start(out=hout[r0 - 128:r0, :], in_=ot[:]),
                      reads=[f"otile{par}"], final=True)
            if with_kv and r0 == NT - 128:
                s.dma("sp", lambda q, ot=ot: q.dma_start(out=hl_send[:, :], in_=ot[:]), reads=[f"otile{par}"], final=True)
            if with_kv and r0 >= 128:
                norm_transpose(ot[:], f"otile{par}", hkT, "hkT", i * 128, kvg_sb, "kvg", par, second=(hbT, "hbT", bg_sb, "bg"))
        if with_kv and c0 >= 128:
            for kc in range(8):
                s.dma("sp", lambda q, c0=c0, Wc=Wc, kc=kc: q.dma_start(
                    out=xnT_my_l[kc // 2][(kc % 2) * 128:(kc % 2 + 1) * 128, c0 - 128:c0 - 128 + Wc], in_=hbT[:, kc, 0:Wc]),
                    reads=["hbT"], final=True)
            s.dma("sp", lambda q: q.dma_start(out=wbig[:, 0:6144].rearrange("p (k n) -> p k n", k=8), in_=kvw_sv), reads=skeys, writes=["wbig"])
            for i in range(nt):
                par = tile_ctr % 2
                tile_ctr += 1
                r0 = c0 + i * 128
                kt = kvtile[par]
                for hf in range(2):
                    for kc in range(8):
                        s.op("pe", lambda e, kc=kc, hf=hf, i=i: e.matmul(ps_o[hf][:, 0:384], lhsT=hkT[:, kc, i * 128:(i + 1) * 128],
                                                                        rhs=wbig[:, kc * 768 + hf * 384: kc * 768 + hf * 384 + 384],
                                                                        start=(kc == 0), stop=(kc == 7)),
                             reads=["hkT", "wbig"], writes=[f"ps_o{hf}"])
                    s.op("act", lambda e, hf=hf, kt=kt: e.activation(out=kt[:, hf * 384:(hf + 1) * 384], in_=ps_o[hf][:, 0:384], func=AF.Copy),
                         reads=[f"ps_o{hf}"], writes=[f"kvtile{par}"])
                for g in range(2):
                    tk = r0 - 128
                    s.dma("sp", lambda q, kt=kt, tk=tk, g=g: q.dma_start(
                        out=kv_send_l[g * 4 + tk // 1024][tk % 1024:tk % 1024 + 128, :].rearrange("p (a d) -> p a d", a=6),
                        in_=kt[:].rearrange("p (a g d) -> p a g d", a=6, g=2)[:, :, g, :]), reads=[f"kvtile{par}"], final=True)
    return P.finish()


_FILL_REGS = {}


def fill_reg(e, val):
    key = (id(e), val)
    if key not in _FILL_REGS:
        _FILL_REGS[key] = e.to_reg(val)
    return _FILL_REGS[key]


def rms_rstd(s, P, ss_ap, out_ap, n, rkey, wkey, tmp_ap, tkey):
    s.op("act", lambda e: e.activation(out=tmp_ap, in_=ss_ap, func=AF.Ln, scale=1.0 / n, bias=EPS), reads=[rkey], writes=[tkey])
    s.op("act", lambda e: e.activation(out=out_ap, in_=tmp_ap, func=AF.Exp, scale=-0.5), reads=[tkey], writes=[wkey])


def norm_transpose_g(P, src_tile, src_key, dstT, dst_key, col0, gain_sb, gain_key, xs_t, xs_key, junk, stat, ps_t, ident):
    s = P.s
    s.op("act", lambda e: e.activation(out=junk[:], in_=src_tile, func=AF.Square, accum_out=stat[:, 0:1]),
         reads=[src_key], writes=["junk", "stat0"])
    rms_rstd(s, P, stat[:, 0:1], stat[:, 3:4], D, "stat0", "stat3", stat[:, 1:2], "stat1")
    s.op("dve", lambda e: e.tensor_scalar(out=xs_t[:], in0=src_tile, scalar1=stat[:, 3:4], scalar2=None, op0=ALU.mult),
         reads=[src_key, "stat3"], writes=[xs_key])
    for kc in range(8):
        s.op("pe", lambda e, kc=kc: e.transpose(out=ps_t[:, kc, :], in_=xs_t[:, kc * 128:(kc + 1) * 128], identity=ident[:]),
             reads=[xs_key, "ident"], writes=["ps_t"])
    for kc in range(8):
        s.op("act", lambda e, kc=kc: e.activation(out=dstT[:, kc, col0:col0 + 128], in_=ps_t[:, kc, :], func=AF.Copy,
                                                   scale=gain_sb[:, kc:kc + 1]),
             reads=["ps_t", gain_key], writes=[dst_key])


def build_fox(P, TT):
    nc, s = P.nc, P.s
    NTL = TT // 128
    NQC = TT // 512
    x = P.din("x", [TT, D])
    gl = P.din("gl", [128, 8])
    wA = P.din("wA", [2, D, 512])
    wB = P.din("wB", [2, D, 260])
    bfl = P.din("bfl", [128, 8])
    qg = P.din("qg", [128, 64])
    kg = P.din("kg", [128, 64])
    oT_l = P.io["oT_l"]

    ident = make_ident(P, BF16, "ident")
    identf = make_ident(P, F32, "identf")
    gl_sb = P.sb("gl_sb", [128, 8], F32)
    bfl_sb = P.sb("bfl_sb", [128, 8], F32)
    qg_sb = P.sb("qg_sb", [128, 64], F32)
    kg_sb = P.sb("kg_sb", [128, 64], F32)
    for t_, d_, k_ in ((gl_sb, gl, "gl"), (bfl_sb, bfl, "bfl"), (qg_sb, qg, "qg"), (kg_sb, kg, "kg")):
        s.dma("sp", lambda q, t_=t_, d_=d_: q.dma_start(out=t_[:], in_=d_), writes=[k_])
    G8 = P.sb("G8", [128, 8, 64], F32)
    for h in range(4):
        s.op("pool", lambda e, h=h: e.tensor_scalar(out=G8[:, h, :], in0=qg_sb[:], scalar1=0.125, scalar2=None, op0=ALU.mult),
             reads=["qg"], writes=["G8"])
        s.op("pool", lambda e, h=h: e.tensor_copy(out=G8[:, 4 + h, :], in_=kg_sb[:]), reads=["kg"], writes=["G8"])
    ones_bf = P.sb("ones_bf", [128, 64], BF16)
    s.op("pool", lambda e: e.memset(ones_bf[:], 1.0), writes=["ones_bf"])
    onesf = P.sb("onesf", [128, 128], F32)
    s.op("pool", lambda e: e.memset(onesf[:], 1.0), writes=["onesf"])
    tri = P.sb("tri", [128, 128], F32)
    s.op("pool", lambda e: e.memset(tri[:], 1.0), writes=["tri"])
    s.op("pool", lambda e: e.affine_select(out=tri[:], in_=tri[:], pattern=[[1, 128]], compare_op=ALU.is_ge, fill=0.0,
                                           base=0, channel_multiplier=-1), reads=["tri"], writes=["tri"])
    selneg = P.sb("selneg", [4, 4, 128], F32)
    s.op("pool", lambda e: e.memset(selneg[:], -1.0), writes=["selneg"])
    s.op("pool", lambda e: e.affine_select(out=selneg[:], in_=selneg[:], pattern=[[1, 4], [0, 128]], compare_op=ALU.is_equal,
                                           fill=0.0, base=0, channel_multiplier=-1), reads=["selneg"], writes=["selneg"])

    KT = P.sb("KT", [128, 2, TT], BF16)
    V = P.sb("V", [128, NTL, 4, 128], BF16)
    s.op("pool", lambda e: e.memset(V[:], 1.0), writes=["V"])
    wA_bf = P.sb("wA_bf", [128, 8, 512], BF16)
    wB_bf = P.sb("wB_bf", [128, 8, 260], BF16)
    wstage = [P.sb(f"wstage{i}", [128, 512], F32) for i in range(2)]
    x_sb = [P.sb(f"x_sb{i}", [128, D], F32) for i in range(2)]
    xs = [P.sb(f"xs{i}", [128, D], BF16) for i in range(2)]
    junk = P.sb("junk", [128, D], BF16)
    stat = P.sb("stat", [128, 8], F32)
    xnT = P.sb("xnT", [128, 8, 512], BF16)
    sq = P.sb("sq", [128, 512], F32)
    ss8 = P.sb("ss8", [128, 24], F32)
    t1 = P.sb("t1", [128, 8, 64], F32)
    qkn = P.sb("qkn", [128, 8, 64], BF16)
    QT = P.sb("QT", [128, 2, 512], BF16)
    zf = P.sb("zf", [128, 16], F32)
    Ltok = P.sb("Ltok", [128, NTL, 4], F32)
    R = P.sb("R", [128, 4], F32)
    LT = P.sb("LT", [4, 512], F32)
    nLb = [P.sb(f"nLb{h}", [128, 512], F32) for h in range(4)]
    tmp = [P.sb(f"tmp{i}", [128, 512], F32) for i in range(4)]
    E = [P.sb(f"E{i}", [128, 512], BF16) for i in range(5)]
    rden = P.sb("rden", [64, 512], F32)
    oTh = [P.sb(f"oTh{i}", [64, 512], BF16) for i in range(2)]

    ps_qk = P.ps("ps_qk", [128, 512])
    ps_t = P.ps("ps_t", [128, 8, 128], BF16)
    ps_s = [P.ps(f"ps_s{i}", [128, 512]) for i in range(4)]
    ps_n = P.ps("ps_n", [128, 512])
    ps_m = P.ps("ps_m", [128, 512])
    ps_vf = ps_m[:, 252:512]

    x_v = x.rearrange("(n p) d -> n p d", p=128)
    pair_ctr = 0
    bgw = list(P.io.get("bg_work", []))
    per_it = (len(bgw) + 2 * NQC - 3) // max(2 * NQC - 2, 1)
    for hp in range(2):
        for kc in range(8):
            i = kc % 2
            s.dma("sp", lambda q, i=i, kc=kc, hp=hp: q.dma_start(out=wstage[i][:, 0:512], in_=wA[hp, kc * 128:(kc + 1) * 128, :]),
                  writes=[f"wstage{i}"])
            s.op("pool", lambda e, i=i, kc=kc: e.tensor_copy(out=wA_bf[:, kc, :], in_=wstage[i][:, 0:512]),
                 reads=[f"wstage{i}"], writes=["wA_bf"])
        for kc in range(8):
            i = kc % 2
            s.dma("sp", lambda q, i=i, kc=kc, hp=hp: q.dma_start(out=wstage[i][:, 0:260], in_=wB[hp, kc * 128:(kc + 1) * 128, :]),
                  writes=[f"wstage{i}"])
            s.op("pool", lambda e, i=i, kc=kc: e.tensor_copy(out=wB_bf[:, kc, :], in_=wstage[i][:, 0:260]),
                 reads=[f"wstage{i}"], writes=["wB_bf"])
        s.op("pool", lambda e: e.memset(R[:], 0.0), writes=["R"])
        for qc in range(NQC):
            for i in range(4):
                tg = qc * 4 + i
                par = tg % 2
                s.dma("sp", lambda q, par=par, tg=tg: q.dma_start(out=x_sb[par][:], in_=x_v[tg]), writes=[f"x_sb{par}"])
                norm_transpose_g(P, x_sb[par][:], f"x_sb{par}", xnT, "xnT", i * 128, gl_sb, "gl", xs[par], f"xs{par}",
                                 junk, stat, ps_t, ident)
            for i in range(4):
                tg = qc * 4 + i
                tc_ = slice(i * 128, (i + 1) * 128)
                for kc in range(8):
                    s.op("pe", lambda e, kc=kc, tc_=tc_: e.matmul(ps_qk[:], lhsT=xnT[:, kc, tc_], rhs=wA_bf[:, kc, :],
                                                                  start=(kc == 0), stop=(kc == 7)),
                         reads=["xnT", "wA_bf"], writes=["ps_qk"])
                for kc in range(8):
                    s.op("pe", lambda e, kc=kc, tc_=tc_: e.matmul(ps_vf[:, 0:260], lhsT=xnT[:, kc, tc_], rhs=wB_bf[:, kc, :],
                                                                  start=(kc == 0), stop=(kc == 7)),
                         reads=["xnT", "wB_bf"], writes=["ps_m"])
                s.op("act", lambda e: e.activation(out=sq[:], in_=ps_qk[:], func=AF.Square), reads=["ps_qk"], writes=["sq"])
                s.op("dve", lambda e: e.tensor_reduce(out=ss8[:, 0:8], in_=sq[:].rearrange("p (h d) -> p h d", h=8),
                                                      axis=AX.X, op=ALU.add), reads=["sq"], writes=["ss8a"])
                rms_rstd(s, P, ss8[:, 0:8], ss8[:, 16:24], DH, "ss8a", "ss8c", ss8[:, 8:16], "ss8b")
                s.op("dve", lambda e: e.tensor_tensor(out=t1[:], in0=ps_qk[:].rearrange("p (h d) -> p h d", h=8),
                                                      in1=ss8[:, 16:24].unsqueeze(2).to_broadcast([128, 8, 64]), op=ALU.mult),
                     reads=["ps_qk", "ss8c"], writes=["t1"])
                s.op("pool", lambda e: e.tensor_tensor(out=qkn[:], in0=t1[:], in1=G8[:], op=ALU.mult),
                     reads=["t1", "G8"], writes=["qkn"])
                for pr in range(4):
                    s.op("pe", lambda e, pr=pr: e.transpose(out=ps_t[:, pr, :], in_=qkn[:, 2 * pr:2 * pr + 2, :].rearrange("p h d -> p (h d)"),
                                                            identity=ident[:]), reads=["qkn", "ident"], writes=["ps_t"])
                s.op("act", lambda e, tc_=tc_: e.activation(out=QT[:, :, tc_], in_=ps_t[:, 0:2, :], func=AF.Copy),
                     reads=["ps_t"], writes=["QT"])
                s.op("act", lambda e, tg=tg: e.activation(out=KT[:, :, tg * 128:(tg + 1) * 128], in_=ps_t[:, 2:4, :], func=AF.Copy),
                     reads=["ps_t"], writes=["KT"])
                s.op("act", lambda e, tg=tg: e.activation(out=V[:, tg, :, 0:64], in_=ps_vf[:, 0:256].rearrange("p (h d) -> p h d", h=4), func=AF.Copy),
                     reads=["ps_m"], writes=["V"])
                s.op("dve", lambda e, hp=hp: e.tensor_tensor(out=zf[:, 0:4], in0=ps_vf[:, 256:260], in1=bfl_sb[:, hp * 4:hp * 4 + 4], op=ALU.add),
                     reads=["ps_m", "bfl"], writes=["zf0"])
                s.op("act", lambda e: e.activation(out=zf[:, 4:8], in_=zf[:, 0:4], func=AF.Exp, scale=-1.0), reads=["zf0"], writes=["zf1"])
                s.op("act", lambda e: e.activation(out=zf[:, 8:12], in_=zf[:, 4:8], func=AF.Ln, bias=1.0), reads=["zf1"], writes=["zf2"])
                s.op("pe", lambda e: e.matmul(ps_m[:, 0:4], lhsT=tri[:], rhs=zf[:, 8:12], start=True, stop=True),
                     reads=["tri", "zf2"], writes=["ps_m"])
                s.op("pe", lambda e: e.matmul(ps_m[:, 4:8], lhsT=onesf[:], rhs=zf[:, 8:12], start=True, stop=True),
                     reads=["onesf", "zf2"], writes=["ps_m"])
                s.op("dve", lambda e, tg=tg: e.tensor_tensor(out=Ltok[:, tg, :], in0=ps_m[:, 0:4], in1=R[:], op=ALU.add),
                     reads=["ps_m", "R"], writes=[f"Ltok{tg}"])
                s.op("dve", lambda e: e.tensor_tensor(out=R[:], in0=ps_m[:, 4:8], in1=R[:], op=ALU.add),
                     reads=["ps_m", "R"], writes=["R"])
                s.op("pe", lambda e, tg=tg: e.transpose(out=ps_m[0:4, 16:144], in_=Ltok[:, tg, :], identity=identf[:]),
                     reads=[f"Ltok{tg}", "identf"], writes=["ps_m"])
                s.op("act", lambda e, tc_=tc_: e.activation(out=LT[:, tc_], in_=ps_m[0:4, 16:144], func=AF.Copy),
                     reads=["ps_m"], writes=["LT"])
            for h in range(4):
                s.op("pe", lambda e, h=h: e.matmul(ps_m[:], lhsT=selneg[:, h, :], rhs=LT[:], start=True, stop=True),
                     reads=["selneg", "LT"], writes=["ps_m"])
                s.op("act", lambda e, h=h: e.activation(out=nLb[h][:], in_=ps_m[:], func=AF.Copy), reads=["ps_m"], writes=[f"nLb{h}"])
            for _ in range(per_it):
                if bgw:
                    bgw.pop(0)()
            nkt = 4 * qc + 4
            items = [(h, kt) for h in range(4) for kt in range(nkt)]
            LOOK = 3
            binfo = {}

            def stage_a(ix):
                nonlocal pair_ctr
                h, kt = items[ix]
                pr, hb = h // 2, (h % 2) * 64
                r = kt - 4 * qc
                cs = 128 * r if r > 0 else 0
                cols = slice(cs, 512)
                pi, ti, ei = pair_ctr % 4, pair_ctr % 4, pair_ctr % 5
                pair_ctr += 1
                pss, tm, Et = ps_s[pi], tmp[ti], E[ei]
                binfo[ix] = (Et, ei, cols)
                s.op("pe", lambda e: e.matmul(pss[:, cols], lhsT=KT[hb:hb + 64, pr, kt * 128:(kt + 1) * 128], rhs=QT[hb:hb + 64, pr, cols],
                                              start=True, stop=True), reads=["KT", "QT"], writes=[f"ps_s{pi}"])
                s.op("dve", lambda e: e.tensor_tensor(out=tm[:, cols], in0=pss[:, cols], in1=nLb[h][:, cols], op=ALU.add),
                     reads=[f"ps_s{pi}", f"nLb{h}"], writes=[f"tmp{ti}"])
                if r >= 0:
                    s.op("pool", lambda e: e.affine_select(out=tm[:, cs:cs + 128], in_=tm[:, cs:cs + 128], pattern=[[1, 128]],
                                                           compare_op=ALU.is_ge, fill=fill_reg(e, -30000.0), base=0, channel_multiplier=-1),
                         reads=[f"tmp{ti}"], writes=[f"tmp{ti}"])
                s.op("act", lambda e: e.activation(out=Et[:, cols], in_=tm[:, cols], func=AF.Exp, bias=Ltok[:, kt, h:h + 1]),
                     reads=[f"tmp{ti}", f"Ltok{kt}"], writes=[f"E{ei}"])

            def stage_b(ix):
                h, kt = items[ix]
                Et, ei, cols = binfo.pop(ix)
                first, last, qc_ = (kt == 0), (kt == nkt - 1), qc
                s.op("pe", lambda e: e.matmul(ps_n[:, cols], lhsT=V[:, kt, h, :], rhs=Et[:, cols], start=first, stop=last),
                     reads=[f"E{ei}", "V"], writes=["ps_n"])
                if last:
                    oi = h % 2
                    s.op("act", lambda e: e.activation(out=rden[:], in_=ps_n[64:128, :], func=AF.Copy), reads=["ps_n"], writes=["rden"])
                    s.op("dve", lambda e: e.reciprocal(out=rden[:], in_=rden[:]), reads=["rden"], writes=["rden"])
                    s.op("dve", lambda e: e.tensor_tensor(out=oTh[oi][:], in0=ps_n[0:64, :], in1=rden[:], op=ALU.mult),
                         reads=["ps_n", "rden"], writes=[f"oTh{oi}"])
                    hg = hp * 4 + h
                    s.dma("sp", lambda q: q.dma_start(out=oT_l[hg // 2][(hg % 2) * 64:(hg % 2 + 1) * 64, qc_ * 512:(qc_ + 1) * 512], in_=oTh[oi][:]),
                          reads=[f"oTh{oi}"], final=True)

            for ix in range(len(items) + LOOK):
                if ix < len(items):
                    stage_a(ix)
                if ix >= LOOK:
                    stage_b(ix - LOOK)
    while bgw:
        bgw.pop(0)()
    return P.finish()


def rope_tm(s, eng2, src, src_key, dst, dst_key, nh, cos_ap, sin_ap, tabkey, rt, rtkey):
    cb = cos_ap.unsqueeze(1).to_broadcast([128, nh, 8])
    sb_ = sin_ap.unsqueeze(1).to_broadcast([128, nh, 8])
    n8 = nh * 8

    def v(i):
        return rt[:, i, 0:n8].rearrange("p (h d) -> p h d", h=nh)

    x1 = src[:, :, 0:8]
    x2 = src[:, :, 8:16]
    s.op("dve", lambda e: e.tensor_copy(out=dst, in_=src), reads=[src_key], writes=[dst_key])
    s.op("dve", lambda e: e.tensor_tensor(out=v(0), in0=x1, in1=cb, op=ALU.mult), reads=[src_key, tabkey], writes=[rtkey + "0"])
    s.op("dve", lambda e: e.tensor_tensor(out=v(1), in0=x2, in1=sb_, op=ALU.mult), reads=[src_key, tabkey], writes=[rtkey + "1"])
    s.op("dve", lambda e: e.tensor_tensor(out=dst[:, :, 0:8], in0=v(0), in1=v(1), op=ALU.subtract),
         reads=[rtkey + "0", rtkey + "1"], writes=[dst_key])
    s.op("dve", lambda e: e.tensor_tensor(out=v(2), in0=x2, in1=cb, op=ALU.mult), reads=[src_key, tabkey], writes=[rtkey + "2"])
    s.op("dve", lambda e: e.tensor_tensor(out=v(3), in0=x1, in1=sb_, op=ALU.mult), reads=[src_key, tabkey], writes=[rtkey + "3"])
    s.op("dve", lambda e: e.tensor_tensor(out=dst[:, :, 8:16], in0=v(2), in1=v(3), op=ALU.add),
         reads=[rtkey + "2", rtkey + "3"], writes=[dst_key])


def sincos_tab(P, posf, n, invf_sb, cosT, sinT, name):
    s = P.s
    ang = P.sb(name + "_ang", [128, n, 8], F32)
    ki = P.sb(name + "_ki", [128, n, 8], I32)
    kf = P.sb(name + "_kf", [128, n, 8], F32)
    TWO_PI = 2.0 * np.pi
    for i in range(8):
        s.op("dve", lambda e, i=i: e.tensor_scalar(out=ang[:, :, i], in0=posf, scalar1=invf_sb[:, i:i + 1], scalar2=None, op0=ALU.mult),
             reads=[name + "_posf", "invf"], writes=[name + "_ang"])
    s.op("dve", lambda e: e.tensor_scalar(out=kf[:], in0=ang[:], scalar1=1.0 / TWO_PI, scalar2=None, op0=ALU.mult),
         reads=[name + "_ang"], writes=[name + "_kf"])
    s.op("dve", lambda e: e.tensor_copy(out=ki[:], in_=kf[:]), reads=[name + "_kf"], writes=[name + "_ki"])
    s.op("dve", lambda e: e.tensor_copy(out=kf[:], in_=ki[:]), reads=[name + "_ki"], writes=[name + "_kf"])
    s.op("dve", lambda e: e.scalar_tensor_tensor(out=ang[:], in0=kf[:], scalar=-TWO_PI, in1=ang[:], op0=ALU.mult, op1=ALU.add),
         reads=[name + "_kf", name + "_ang"], writes=[name + "_ang"])
    s.op("dve", lambda e: e.tensor_scalar(out=kf[:], in0=ang[:], scalar1=float(np.pi), scalar2=None, op0=ALU.is_gt),
         reads=[name + "_ang"], writes=[name + "_kf"])
    s.op("dve", lambda e: e.scalar_tensor_tensor(out=ang[:], in0=kf[:], scalar=-TWO_PI, in1=ang[:], op0=ALU.mult, op1=ALU.add),
         reads=[name + "_kf", name + "_ang"], writes=[name + "_ang"])
    s.op("dve", lambda e: e.tensor_scalar(out=kf[:], in0=ang[:], scalar1=-float(np.pi), scalar2=None, op0=ALU.is_lt),
         reads=[name + "_ang"], writes=[name + "_kf"])
    s.op("dve", lambda e: e.scalar_tensor_tensor(out=ang[:], in0=kf[:], scalar=TWO_PI, in1=ang[:], op0=ALU.mult, op1=ALU.add),
         reads=[name + "_kf", name + "_ang"], writes=[name + "_ang"])
    s.op("dve", lambda e: e.tensor_scalar(out=ang[:], in0=ang[:], scalar1=3.1415925, scalar2=-3.1415925, op0=ALU.min, op1=ALU.max),
         reads=[name + "_ang"], writes=[name + "_ang"])
    s.op("act", lambda e: e.activation(out=sinT[:], in_=ang[:], func=AF.Sin), reads=[name + "_ang"], writes=[name + "_tab"])
    s.op("dve", lambda e: e.tensor_scalar(out=kf[:], in0=ang[:], scalar1=-1.0, scalar2=None, op0=ALU.mult),
         reads=[name + "_ang"], writes=[name + "_kf"])
    s.op("dve", lambda e: e.tensor_tensor(out=kf[:], in0=kf[:], in1=ang[:], op=ALU.max),
         reads=[name + "_ang", name + "_kf"], writes=[name + "_kf"])
    s.op("dve", lambda e: e.tensor_scalar(out=kf[:], in0=kf[:], scalar1=-1.0, scalar2=float(np.pi / 2), op0=ALU.mult, op1=ALU.add),
         reads=[name + "_kf"], writes=[name + "_kf"])
    s.op("act", lambda e: e.activation(out=cosT[:], in_=kf[:], func=AF.Sin), reads=[name + "_kf"], writes=[name + "_tab"])


def build_nsa(P, TT):
    nc, s = P.nc, P.s
    NTL = TT // 128
    NQC = TT // 512
    NCB = TT // 16
    NCT = NCB // 128
    NVB = (TT - 32) // 16 + 1
    xnT_all_l = P.io["xnT_all_l"]
    msk = P.din("msk", [128, 2])
    wq = P.din("wq", [D, 512])
    wg = P.din("wg", [D, 24])
    bgl = P.din("bgl", [128, 24])
    qg = P.din("qg", [128, 64])
    kv_all_l = P.io["kv_all_l"]
    pos = P.din("pos", [128, NTL], I32)
    pose = P.din("pose", [128, NCT], I32)
    kgains = P.din("kgains", [128, 3, 64])
    invf = P.din("invf", [128, 8])
    peT = P.din("peT", [128, 32])
    w1 = P.din("w1", [128, 32, 256])
    w2 = P.din("w2", [128, 2, 2, 64])
    oT_l = P.io["oT_l"]

    ident = make_ident(P, BF16, "ident")
    identf = make_ident(P, F32, "identf")
    small = {}
    for nm, ap_, shp, dt in (("msk", msk, [128, 2], F32), ("bgl", bgl, [128, 24], F32), ("qg", qg, [128, 64], F32),
                             ("kgains", kgains, [128, 3, 64], F32), ("invf", invf, [128, 8], F32),
                             ("pos", pos, [128, NTL], I32), ("pose", pose, [128, NCT], I32), ("peT", peT, [128, 32], F32),
                             ("w2", w2, [128, 2, 2, 64], F32)):
        t_ = P.sb(nm + "_sb", shp, dt)
        s.dma("sp", lambda q, t_=t_, ap_=ap_: q.dma_start(out=t_[:], in_=ap_), writes=[nm])
        small[nm] = t_
    msk_sb, bgl_sb, qg_sb, kg_sb, invf_sb = small["msk"], small["bgl"], small["qg"], small["kgains"], small["invf"]
    G8 = P.sb("G8", [128, 8, 64], F32)
    for h in range(8):
        s.op("pool", lambda e, h=h: e.tensor_scalar(out=G8[:, h, :], in0=qg_sb[:], scalar1=0.125, scalar2=None, op0=ALU.mult),
             reads=["qg"], writes=["G8"])
    ones_bf = P.sb("ones_bf", [128, 128], BF16)
    s.op("pool", lambda e: e.memset(ones_bf[:], 1.0), writes=["ones_bf"])
    posf = P.sb("posf", [128, NTL], F32)
    s.op("dve", lambda e: e.tensor_copy(out=posf[:], in_=small["pos"][:]), reads=["pos"], writes=["tq_posf"])
    cosT = P.sb("cosT", [128, NTL, 8], F32)
    sinT = P.sb("sinT", [128, NTL, 8], F32)
    sincos_tab(P, posf[:], NTL, invf_sb, cosT, sinT, "tq")
    posef = P.sb("posef", [128, NCT], F32)
    s.op("dve", lambda e: e.tensor_copy(out=posef[:], in_=small["pose"][:]), reads=["pose"], writes=["te_posf"])
    cosE = P.sb("cosE", [128, NCT, 8], F32)
    sinE = P.sb("sinE", [128, NCT, 8], F32)
    sincos_tab(P, posef[:], NCT, invf_sb, cosE, sinE, "te")
    Gx = P.sb("Gx", [128, TT], BF16)
    s.op("pool", lambda e: e.memset(Gx[:], 1.0), writes=["Gx"])
    s.op("pool", lambda e: e.affine_select(out=Gx[:], in_=Gx[:], pattern=[[1, TT]], compare_op=ALU.is_ge, fill=0.0,
                                           base=0, channel_multiplier=-64), reads=["Gx"], writes=["Gx"])
    s.op("pool", lambda e: e.affine_select(out=Gx[:], in_=Gx[:], pattern=[[-1, TT]], compare_op=ALU.is_ge, fill=0.0,
                                           base=63, channel_multiplier=64), reads=["Gx"], writes=["Gx"])
    ovl = P.sb("ovl", [128, NCT, 128], BF16)
    oa = P.sb("oa", [128, NCT, 128], F32)
    ob = P.sb("ob", [128, NCT, 128], F32)
    oc = P.sb("oc", [128, NCT, 128], F32)
    s.op("pool", lambda e: e.iota(out=oa[:], pattern=[[2048, NCT], [0, 128]], base=0, channel_multiplier=16,
                                  allow_small_or_imprecise_dtypes=True), writes=["oa"])
    s.op("pool", lambda e: e.iota(out=ob[:], pattern=[[0, NCT], [64, 128]], base=0, channel_multiplier=0,
                                  allow_small_or_imprecise_dtypes=True), writes=["ob"])
    s.op("dve", lambda e: e.tensor_tensor(out=oc[:], in0=oa[:], in1=ob[:], op=ALU.max), reads=["oa", "ob"], writes=["oc"])
    s.op("dve", lambda e: e.tensor_scalar(out=oa[:], in0=oa[:], scalar1=32.0, scalar2=None, op0=ALU.add), reads=["oa"], writes=["oa"])
    s.op("dve", lambda e: e.tensor_scalar(out=ob[:], in0=ob[:], scalar1=64.0, scalar2=None, op0=ALU.add), reads=["ob"], writes=["ob"])
    s.op("dve", lambda e: e.tensor_tensor(out=oa[:], in0=oa[:], in1=ob[:], op=ALU.min), reads=["oa", "ob"], writes=["oa"])
    s.op("dve", lambda e: e.tensor_tensor(out=oa[:], in0=oa[:], in1=oc[:], op=ALU.subtract), reads=["oa", "oc"], writes=["oa"])
    s.op("dve", lambda e: e.tensor_scalar(out=ovl[:], in0=oa[:], scalar1=0.0, scalar2=1.0 / 32, op0=ALU.max, op1=ALU.mult),
         reads=["oa"], writes=["ovl"])
    selg = P.sb("selg", [24, 24, 64], F32)
    s.op("pool", lambda e: e.memset(selg[:], 1.0), writes=["selg"])
    s.op("pool", lambda e: e.affine_select(out=selg[:], in_=selg[:], pattern=[[1, 24], [0, 64]], compare_op=ALU.is_equal,
                                           fill=0.0, base=0, channel_multiplier=-1), reads=["selg"], writes=["selg"])

    ksT = P.sb("ksT", [128, TT], BF16)
    kwT = P.sb("kwT", [128, TT], BF16)
    VS = P.sb("VS", [128, NTL, 64], BF16)
    VW = P.sb("VW", [128, NTL, 64], BF16)
    kcT = P.sb("kcT", [128, NCB], BF16)
    VC = P.sb("VC", [128, NCT, 64], BF16)
    shared2 = P.sb("shared2", [128, 16384], BF16)
    rawT = shared2[:, 0:TT]
    w1_sb = shared2[:, 8192:16384].rearrange("p (l n) -> p l n", l=32)
    w2_sb = P.sb("w2_bf", [128, 2, 2, 64], BF16)
    peT_bf = P.sb("peT_bf", [128, 32], BF16)
    hidT = P.sb("hidT", [128, 2, NCB], BF16)
    wstage = [P.sb(f"wstage{i}", [128, 1024], F32) for i in range(2)]
    kv_sb = [P.sb(f"kv_sb{i}", [128, 6, 64], F32) for i in range(2)]
    kv_b = [P.sb(f"kv_b{i}", [128, 6, 64], F32) for i in range(2)]
    xs = [shared2[:, 4096 + 1024 * i:4096 + 1024 * (i + 1)] for i in range(2)]
    junk = shared2[:, 6144:7168]
    stat = P.sb("stat", [128, 8], F32)
    xnT = shared2[:, 0:4096].rearrange("p (k t) -> p k t", k=8)
    wq_bf = P.sb("wq_bf", [128, 8, 512], BF16)
    wg_bf = P.sb("wg_bf", [128, 8, 24], BF16)
    sq = P.sb("sq", [128, 512], F32)
    ss8 = P.sb("ss8", [128, 24], F32)
    t1 = P.sb("t1", [128, 8, 64], F32)
    t2 = P.sb("t2", [128, 8, 64], F32)
    rt = P.sb("rt", [128, 4, 64], F32)
    qkn = P.sb("qkn", [128, 8, 64], BF16)
    kdup = P.sb("kdup", [128, 2, 2, 64], BF16)
    QT = shared2[:, 7168:9216].rearrange("p (k t) -> p k t", k=4)
    zg = P.sb("zg", [128, 3, 24], F32)
    gT = P.sb("gT", [24, 512], F32)
    Ebuf = [shared2[:, 9216 + 512 * i:9216 + 512 * (i + 1)] for i in range(4)]
    Em = [shared2[:, 11264 + 512 * i:11264 + 512 * (i + 1)] for i in range(3)]
    rdenb = P.sb("rdenb", [128, 512], F32)
    rden = P.sb("rden", [64, 512], F32)
    fgate = P.sb("fgate", [64, 512], F32)
    contrib = P.sb("contrib", [64, 512], F32)
    shared1 = P.sb("shared1", [128, 8, 512], F32)
    oacc = [shared1[0:64, h, :] for h in range(8)]
    oTh = [P.sb(f"oTh{i}", [64, 512], BF16) for i in range(2)]
    impT = P.sb("impT", [128, 512], F32)
    Esum = [P.sb(f"Esum{i}", [128, 512], F32) for i in range(2)]
    Esb = P.sb("Esb", [128, 512], BF16)
    sc = P.sb("sc", [128, 128], F32)
    sc2 = P.sb("sc2", [128, 128], F32)
    m8 = P.sb("m8", [128, 16], F32)
    selb = P.sb("selb", [128, 128], BF16)
    selT = P.sb("selT", [128, 512], BF16)
    gx = shared1
    biasH = P.sb("biasH", [128, 4], F32)

    ps_s = [P.ps(f"ps_s{i}", [128, 512]) for i in range(3)]
    ps_M = [P.ps(f"ps_M{i}", [128, 512]) for i in range(2)]
    ps_t = ps_s[2][:].bitcast(BF16).rearrange("p (k t) -> p k t", k=8)
    ps_n = P.ps("ps_n", [64, 512])
    ps_d = P.ps("ps_d", [128, 512])
    ps_m = P.ps("ps_m", [128, 512])

    for kc in range(8):
        i = kc % 2
        s.dma("sp", lambda q, i=i, kc=kc: q.dma_start(out=wstage[i][:, 0:512], in_=wq[kc * 128:(kc + 1) * 128, :]), writes=[f"wstage{i}"])
        s.op("pool", lambda e, i=i, kc=kc: e.tensor_copy(out=wq_bf[:, kc, :], in_=wstage[i][:, 0:512]), reads=[f"wstage{i}"], writes=["wq_bf"])
    for kc in range(8):
        i = kc % 2
        s.dma("sp", lambda q, i=i, kc=kc: q.dma_start(out=wstage[i][:, 0:24], in_=wg[kc * 128:(kc + 1) * 128, :]), writes=[f"wstage{i}"])
        s.op("pool", lambda e, i=i, kc=kc: e.tensor_copy(out=wg_bf[:, kc, :], in_=wstage[i][:, 0:24]), reads=[f"wstage{i}"], writes=["wg_bf"])
    for l4 in range(8):
        i = l4 % 2
        s.dma("sp", lambda q, i=i, l4=l4: q.dma_start(out=wstage[i][:, 0:1024].rearrange("p (l n) -> p l n", l=4), in_=w1[:, l4 * 4:(l4 + 1) * 4, :]),
              writes=[f"wstage{i}"])
        s.op("pool", lambda e, i=i, l4=l4: e.tensor_copy(out=w1_sb[:, l4 * 4:(l4 + 1) * 4, :],
                                                       in_=wstage[i][:, 0:1024].rearrange("p (l n) -> p l n", l=4)),
             reads=[f"wstage{i}"], writes=["w1"])
    s.op("pool", lambda e: e.tensor_copy(out=w2_sb[:], in_=small["w2"][:]), reads=["w2"], writes=["w2b"])
    s.op("pool", lambda e: e.tensor_copy(out=peT_bf[:], in_=small["peT"][:]), reads=["peT"], writes=["peTb"])
    s.op("pool", lambda e: e.memset(hidT[:], 0.0), writes=["hidT"])

    def head_norm(src_ps, src_key, nslots, gain_ap, gain_key, dst, dst_key):
        w = nslots * 64
        s.op("act", lambda e: e.activation(out=sq[:, 0:w], in_=src_ps, func=AF.Square), reads=[src_key], writes=["sq"])
        s.op("dve", lambda e: e.tensor_reduce(out=ss8[:, 0:nslots], in_=sq[:, 0:w].rearrange("p (h d) -> p h d", h=nslots),
                                              axis=AX.X, op=ALU.add), reads=["sq"], writes=["ss8a"])
        rms_rstd(s, P, ss8[:, 0:nslots], ss8[:, 16:16 + nslots], DH, "ss8a", "ss8c", ss8[:, 8:8 + nslots], "ss8b")
        s.op("dve", lambda e: e.tensor_tensor(out=t1[:, 0:nslots, :], in0=src_ps.rearrange("p (h d) -> p h d", h=nslots),
                                              in1=ss8[:, 16:16 + nslots].unsqueeze(2).to_broadcast([128, nslots, 64]), op=ALU.mult),
             reads=[src_key, "ss8c"], writes=["t1"])
        s.op("pool", lambda e: e.tensor_tensor(out=dst, in0=t1[:, 0:nslots, :], in1=gain_ap, op=ALU.mult),
             reads=["t1", gain_key], writes=[dst_key])

    HT = NTL // 2
    QT_ = HT // 4

    def kv_src(g, rk, tl):
        r0_ = rk * (QT_ * 128) + (tl % QT_) * 128
        return kv_all_l[g * 4 + tl // QT_][r0_:r0_ + 128, :]
    for tg in range(NTL):
        par = tg % 2
        kvt = kv_sb[par]
        rk, tl = tg // HT, tg % HT
        kvb = kv_b[par]
        s.dma("sp", lambda q, kvt=kvt, rk=rk, tl=tl: q.dma_start(out=kvt[:].rearrange("p a d -> p (a d)"), in_=kv_src(0, rk, tl)),
              writes=[f"kv_sb{par}"])
        s.dma("sp", lambda q, kvb=kvb, rk=rk, tl=tl: q.dma_start(out=kvb[:].rearrange("p a d -> p (a d)"), in_=kv_src(1, rk, tl)),
              writes=[f"kv_b{par}"])
        s.op("dve", lambda e, kvt=kvt: e.tensor_scalar(out=kvt[:], in0=kvt[:], scalar1=msk_sb[:, 0:1], scalar2=None, op0=ALU.mult),
             reads=[f"kv_sb{par}", "msk"], writes=[f"kv_sb{par}"])
        s.op("dve", lambda e, kvt=kvt, kvb=kvb: e.scalar_tensor_tensor(out=kvt[:], in0=kvb[:], scalar=msk_sb[:, 1:2], in1=kvt[:],
                                                                     op0=ALU.mult, op1=ALU.add),
             reads=[f"kv_sb{par}", f"kv_b{par}", "msk"], writes=[f"kv_sb{par}"])
        for wi, part, gi in ((0, 2, 1), (1, 4, 2)):
            head_norm(kvt[:, part, :], f"kv_sb{par}", 1, kg_sb[:, gi:gi + 1, :], "kgains", t2[:, 0:1, :], "t2")
            rope_tm(s, None, t2[:, 0:1, :], "t2", kdup[:, wi, 0:1, :], f"kdup{wi}", 1, cosT[:, tg, :], sinT[:, tg, :], "tq_tab", rt, "rt")
            s.op("pool", lambda e, wi=wi: e.tensor_copy(out=kdup[:, wi, 1, :], in_=kdup[:, wi, 0, :]), reads=[f"kdup{wi}"], writes=[f"kdup{wi}"])
            s.op("pe", lambda e, wi=wi: e.transpose(out=ps_t[:, wi, :], in_=kdup[:, wi, :, :].rearrange("p a d -> p (a d)"), identity=ident[:]),
                 reads=[f"kdup{wi}", "ident"], writes=["ps_s2"])
        s.op("act", lambda e, tg=tg: e.activation(out=ksT[:, tg * 128:(tg + 1) * 128], in_=ps_t[:, 0, :], func=AF.Copy), reads=["ps_s2"], writes=["ksT"])
        s.op("act", lambda e, tg=tg: e.activation(out=kwT[:, tg * 128:(tg + 1) * 128], in_=ps_t[:, 1, :], func=AF.Copy), reads=["ps_s2"], writes=["kwT"])
        s.op("pool", lambda e, kvt=kvt, tg=tg: e.tensor_copy(out=VS[:, tg, :], in_=kvt[:, 3, :]), reads=[f"kv_sb{par}"], writes=["VS"])
        s.op("pool", lambda e, kvt=kvt, tg=tg: e.tensor_copy(out=VW[:, tg, :], in_=kvt[:, 5, :]), reads=[f"kv_sb{par}"], writes=["VW"])
        s.op("pool", lambda e, kvt=kvt: e.tensor_copy(out=qkn[:, 0:2, :], in_=kvt[:, 0:2, :]), reads=[f"kv_sb{par}"], writes=["qkn"])
        s.op("pe", lambda e: e.transpose(out=ps_t[:, 2, :], in_=qkn[:, 0:2, :].rearrange("p a d -> p (a d)"), identity=ident[:]),
             reads=["qkn", "ident"], writes=["ps_s2"])
        s.op("act", lambda e, tg=tg: e.activation(out=rawT[:, tg * 128:(tg + 1) * 128], in_=ps_t[:, 2, :], func=AF.Copy), reads=["ps_s2"], writes=["rawT"])

    for which in range(2):
        pb = which * 64
        for hc in range(2):
            for l in range(32):
                s.op("pe", lambda e, l=l, hc=hc, pb=pb, which=which: e.matmul(
                    ps_m[:, which * 2 + hc:which * 2 + hc + 1], lhsT=w1_sb[pb:pb + 64, l, hc * 128:(hc + 1) * 128],
                    rhs=peT_bf[pb:pb + 64, l:l + 1], start=(l == 0), stop=(l == 31)), reads=["w1", "peTb"], writes=["ps_m"])
            s.op("act", lambda e, which=which, hc=hc: e.activation(out=biasH[:, which * 2 + hc:which * 2 + hc + 1],
                                                                   in_=ps_m[:, which * 2 + hc:which * 2 + hc + 1], func=AF.Copy),
                 reads=["ps_m"], writes=["biasH"])
        for hc in range(2):
            pss = ps_s[hc]
            for l in range(32):
                s.op("pe", lambda e, l=l, hc=hc, pb=pb, pss=pss: e.matmul(
                    pss[:, 0:NVB], lhsT=w1_sb[pb:pb + 64, l, hc * 128:(hc + 1) * 128],
                    rhs=rawT[pb:pb + 64, l:l + 16 * (NVB - 1) + 1:16], start=(l == 0), stop=(l == 31)),
                    reads=["w1", "rawT"], writes=[f"ps_s{hc}"])
            xg, x2g, ug, thg = gx[:, 0, 0:NVB], gx[:, 1, 0:NVB], gx[:, 2, 0:NVB], gx[:, 3, 0:NVB]
            s.op("act", lambda e, pss=pss, which=which, hc=hc, xg=xg: e.activation(
                out=xg, in_=pss[:, 0:NVB], func=AF.Identity, bias=biasH[:, which * 2 + hc:which * 2 + hc + 1]),
                reads=[f"ps_s{hc}", "biasH"], writes=["gx0"])
            s.op("dve", lambda e, xg=xg, x2g=x2g: e.tensor_tensor(out=x2g, in0=xg, in1=xg, op=ALU.mult), reads=["gx0"], writes=["gx1"])
            s.op("dve", lambda e, x2g=x2g: e.tensor_scalar(out=x2g, in0=x2g, scalar1=0.044715, scalar2=1.0, op0=ALU.mult, op1=ALU.add),
                 reads=["gx1"], writes=["gx1"])
            s.op("dve", lambda e, xg=xg, x2g=x2g, ug=ug: e.tensor_tensor(out=ug, in0=x2g, in1=xg, op=ALU.mult), reads=["gx0", "gx1"], writes=["gx2"])
            s.op("act", lambda e, ug=ug, thg=thg: e.activation(out=thg, in_=ug, func=AF.Tanh, scale=0.7978845608028654), reads=["gx2"], writes=["gx3"])
            s.op("dve", lambda e, thg=thg: e.tensor_scalar(out=thg, in0=thg, scalar1=0.5, scalar2=0.5, op0=ALU.mult, op1=ALU.add),
                 reads=["gx3"], writes=["gx3"])
            s.op("dve", lambda e, thg=thg, xg=xg, hc=hc: e.tensor_tensor(out=hidT[:, hc, 0:NVB], in0=thg, in1=xg, op=ALU.mult),
                 reads=["gx3", "gx0"], writes=["hidT"])
        for ct in range(NCT):
            for hc in range(2):
                s.op("pe", lambda e, ct=ct, hc=hc, which=which: e.matmul(
                    ps_m[:, 64:128], lhsT=hidT[:, hc, ct * 128:(ct + 1) * 128], rhs=w2_sb[:, which, hc, :],
                    start=(hc == 0), stop=(hc == 1)), reads=["hidT", "w2b"], writes=["ps_m"])
            if which == 0:
                head_norm(ps_m[:, 64:128], "ps_m", 1, kg_sb[:, 0:1, :], "kgains", t2[:, 0:1, :], "t2")
                rope_tm(s, None, t2[:, 0:1, :], "t2", kdup[:, 0, 0:1, :], "kdup0", 1, cosE[:, ct, :], sinE[:, ct, :], "te_tab", rt, "rt")
                s.op("pool", lambda e: e.tensor_copy(out=kdup[:, 0, 1, :], in_=kdup[:, 0, 0, :]), reads=["kdup0"], writes=["kdup0"])
                s.op("pe", lambda e: e.transpose(out=ps_t[:, 0, :], in_=kdup[:, 0, :, :].rearrange("p a d -> p (a d)"), identity=ident[:]),
                     reads=["kdup0", "ident"], writes=["ps_s2"])
                s.op("act", lambda e, ct=ct: e.activation(out=kcT[:, ct * 128:(ct + 1) * 128], in_=ps_t[:, 0, :], func=AF.Copy),
                     reads=["ps_s2"], writes=["kcT"])
            else:
                s.op("act", lambda e, ct=ct: e.activation(out=VC[:, ct, :], in_=ps_m[:, 64:128], func=AF.Copy), reads=["ps_m"], writes=["VC"])

    s.op("pool", lambda e: e.memset(biasH[:, 0:1], 0.0), writes=["biasH", "rawT", "w1", "gx0", "gx1", "gx2", "gx3", "xnT", "xs0", "xs1", "junk", "QT"]
         + [f"E{i}" for i in range(4)] + [f"Em{i}" for i in range(3)] + [f"oacc{h}" for h in range(8)])
    pc = [0]

    binfo = {}

    def attn_a(ix, h, kT, ktile, kkey, cols, mask_fn, sel_kt=None, first=False, grp=0):
        pr, hb = h // 2, (h % 2) * 64
        pi = pc[0] % 3
        mi = pc[0] % 2
        ei = pc[0] % 3
        pc[0] += 1
        pss, Et = ps_s[pi], Em[ei]
        binfo[ix] = (Et, ei)
        s.op("pe", lambda e: e.matmul(pss[:, cols], lhsT=kT[hb:hb + 64, ktile * 128:(ktile + 1) * 128], rhs=QT[hb:hb + 64, pr, cols],
                                      start=True, stop=True), reads=[kkey, "QT"], writes=[f"ps_s{pi}"])
        s.op("act", lambda e: e.activation(out=Et[:, cols], in_=pss[:, cols], func=AF.Exp), reads=[f"ps_s{pi}"], writes=[f"Em{ei}"])
        if sel_kt is not None:
            psM = ps_M[mi]
            s.op("pe", lambda e: e.matmul(psM[:, cols], lhsT=Gx[:, sel_kt * 128:(sel_kt + 1) * 128], rhs=selT[:, cols], start=True, stop=True),
                 reads=["Gx", "selT"], writes=[f"ps_M{mi}"])
            s.op("dve", lambda e: e.tensor_tensor(out=Et[:, cols], in0=Et[:, cols], in1=psM[:, cols], op=ALU.mult),
                 reads=[f"Em{ei}", f"ps_M{mi}"], writes=[f"Em{ei}"])
        if mask_fn is not None:
            mask_fn(Et, f"Em{ei}")
        Es = Esum[grp % 2]
        if first:
            s.op("pool", lambda e: e.tensor_copy(out=Es[:, cols], in_=Et[:, cols]), reads=[f"Em{ei}"], writes=[f"Esum{grp % 2}"])
        else:
            s.op("pool", lambda e: e.tensor_tensor(out=Es[:, cols], in0=Es[:, cols], in1=Et[:, cols], op=ALU.add),
                 reads=[f"Em{ei}", f"Esum{grp % 2}"], writes=[f"Esum{grp % 2}"])

    def attn_b(ix, ktile, Vt, vkey, cols, first, last):
        Et, ei = binfo.pop(ix)
        s.op("pe", lambda e: e.matmul(ps_n[:, cols], lhsT=Vt[:, ktile, :], rhs=Et[:, cols], start=first, stop=last),
             reads=[f"Em{ei}", vkey], writes=["ps_n"])

    def finish_branch(h, br, first_branch, den_ap, den_key):
        idx = br * 8 + h
        s.op("dve", lambda e: e.tensor_scalar(out=rden[:], in0=den_ap, scalar1=1e-30, scalar2=None, op0=ALU.max), reads=[den_key], writes=["rden"])
        s.op("dve", lambda e: e.reciprocal(out=rden[:], in_=rden[:]), reads=["rden"], writes=["rden"])
        s.op("pe", lambda e: e.matmul(ps_m[0:64, :], lhsT=selg[:, idx, :], rhs=gT[:], start=True, stop=True), reads=["selg", "gT"], writes=["ps_m"])
        s.op("dve", lambda e: e.tensor_tensor(out=fgate[:], in0=ps_m[0:64, :], in1=rden[:], op=ALU.mult), reads=["ps_m", "rden"], writes=["fgate"])
        if first_branch:
            s.op("dve", lambda e: e.tensor_tensor(out=oacc[h][:], in0=ps_n[:], in1=fgate[:], op=ALU.mult), reads=["ps_n", "fgate"], writes=[f"oacc{h}"])
        else:
            s.op("dve", lambda e: e.tensor_tensor(out=contrib[:], in0=ps_n[:], in1=fgate[:], op=ALU.mult), reads=["ps_n", "fgate"], writes=["contrib"])
            s.op("pool", lambda e: e.tensor_tensor(out=oacc[h][:], in0=oacc[h][:], in1=contrib[:], op=ALU.add),
                 reads=[f"oacc{h}", "contrib"], writes=[f"oacc{h}"])

    def causal_mask(cs):
        def f(Et, ekey):
            s.op("pool", lambda e: e.affine_select(out=Et[:, cs:cs + 128], in_=Et[:, cs:cs + 128], pattern=[[1, 128]], compare_op=ALU.is_ge,
                                                   fill=0.0, base=0, channel_multiplier=-1), reads=[ekey], writes=[ekey])
        return f

    def winlow_mask(cs):
        def f(Et, ekey):
            s.op("pool", lambda e: e.affine_select(out=Et[:, cs:cs + 128], in_=Et[:, cs:cs + 128], pattern=[[-1, 128]], compare_op=ALU.is_ge,
                                                   fill=0.0, base=-1, channel_multiplier=1), reads=[ekey], writes=[ekey])
        return f

    for qc in range(NQC):
        t0 = qc * 512
        qh = NQC // 2
        for kc in range(8):
            rb = (qc // qh) * 256 + (kc % 2) * 128
            s.dma("sp", lambda q, qc=qc, qh=qh, kc=kc, rb=rb: q.dma_start(
                out=xnT[:, kc, :], in_=xnT_all_l[kc // 2][rb:rb + 128, (qc % qh) * 512:(qc % qh + 1) * 512]), writes=["xnT"])
        for i in range(4):
            tg = qc * 4 + i
            tc_ = slice(i * 128, (i + 1) * 128)
            for kc in range(8):
                s.op("pe", lambda e, kc=kc, tc_=tc_: e.matmul(ps_s[0][:], lhsT=xnT[:, kc, tc_], rhs=wq_bf[:, kc, :], start=(kc == 0), stop=(kc == 7)),
                     reads=["xnT", "wq_bf"], writes=["ps_s0"])
            for kc in range(8):
                s.op("pe", lambda e, kc=kc, tc_=tc_: e.matmul(ps_s[1][:, 0:24], lhsT=xnT[:, kc, tc_], rhs=wg_bf[:, kc, :], start=(kc == 0), stop=(kc == 7)),
                     reads=["xnT", "wg_bf"], writes=["ps_s1"])
            head_norm(ps_s[0][:], "ps_s0", 8, G8[:], "G8", t2[:], "t2")
            rope_tm(s, None, t2[:], "t2", qkn[:], "qkn", 8, cosT[:, tg, :], sinT[:, tg, :], "tq_tab", rt, "rt")
            for pr in range(4):
                s.op("pe", lambda e, pr=pr: e.transpose(out=ps_t[:, pr, :], in_=qkn[:, 2 * pr:2 * pr + 2, :].rearrange("p h d -> p (h d)"),
                                                        identity=ident[:]), reads=["qkn", "ident"], writes=["ps_s2"])
            s.op("act", lambda e, tc_=tc_: e.activation(out=QT[:, :, tc_], in_=ps_t[:, 0:4, :], func=AF.Copy), reads=["ps_s2"], writes=["QT"])
            s.op("dve", lambda e: e.tensor_tensor(out=zg[:, 0, :], in0=ps_s[1][:, 0:24], in1=bgl_sb[:], op=ALU.add), reads=["ps_s1", "bgl"], writes=["zg0"])
            s.op("act", lambda e: e.activation(out=zg[:, 1, :], in_=zg[:, 0, :], func=AF.Exp, scale=-1.0), reads=["zg0"], writes=["zg1"])
            s.op("dve", lambda e: e.tensor_scalar(out=zg[:, 1, :], in0=zg[:, 1, :], scalar1=1.0, scalar2=None, op0=ALU.add), reads=["zg1"], writes=["zg1"])
            s.op("dve", lambda e: e.reciprocal(out=zg[:, 2, :], in_=zg[:, 1, :]), reads=["zg1"], writes=["zg2"])
            s.op("pe", lambda e: e.transpose(out=ps_m[0:24, 0:128], in_=zg[:, 2, :], identity=identf[:]), reads=["zg2", "identf"], writes=["ps_m"])
            s.op("act", lambda e, tc_=tc_: e.activation(out=gT[:, tc_], in_=ps_m[0:24, 0:128], func=AF.Copy), reads=["ps_m"], writes=["gT"])
        kts = [kt for kt in range(NCT) if 2048 * kt + 31 <= t0 + 511]
        nimp = len(kts) * 8
        impi = 0
        for h in range(8):
            pr, hb = h // 2, (h % 2) * 64
            for ii, kt in enumerate(kts):
                pi = pc[0] % 2
                pc[0] += 1
                pss, Et = ps_s[pi], Ebuf[ii]
                s.op("pe", lambda e, pss=pss, kt=kt, pr=pr, hb=hb: e.matmul(pss[:], lhsT=kcT[hb:hb + 64, kt * 128:(kt + 1) * 128], rhs=QT[hb:hb + 64, pr, :],
                                                                        start=True, stop=True), reads=["kcT", "QT"], writes=[f"ps_s{pi}"])
                s.op("act", lambda e, pss=pss, Et=Et: e.activation(out=Et[:], in_=pss[:], func=AF.Exp), reads=[f"ps_s{pi}"], writes=[f"E{ii}"])
                if not (2048 * kt + 2063 <= t0):
                    s.op("pool", lambda e, Et=Et, kt=kt, t0=t0: e.affine_select(out=Et[:], in_=Et[:], pattern=[[1, 512]], compare_op=ALU.is_ge, fill=0.0,
                                                                       base=t0 - 2048 * kt - 31, channel_multiplier=-16),
                         reads=[f"E{ii}"], writes=[f"E{ii}"])
                s.op("pe", lambda e, Et=Et, kt=kt, ii=ii, nk=len(kts): e.matmul(ps_n[:], lhsT=VC[:, kt, :], rhs=Et[:], start=(ii == 0), stop=(ii == nk - 1)),
                     reads=[f"E{ii}", "VC"], writes=["ps_n"])
                s.op("pe", lambda e, Et=Et, ii=ii, nk=len(kts): e.matmul(ps_d[:], lhsT=ones_bf[:], rhs=Et[:], start=(ii == 0), stop=(ii == nk - 1)),
                     reads=[f"E{ii}", "ones_bf"], writes=["ps_d"])
            s.op("dve", lambda e: e.tensor_scalar(out=rdenb[:], in0=ps_d[:], scalar1=1e-30, scalar2=None, op0=ALU.max), reads=["ps_d"], writes=["rdenb"])
            s.op("dve", lambda e: e.reciprocal(out=rdenb[:], in_=rdenb[:]), reads=["rdenb"], writes=["rdenb"])
            for ii, kt in enumerate(kts):
                Et = Ebuf[ii]
                s.op("dve", lambda e, Et=Et: e.tensor_tensor(out=Et[:], in0=Et[:], in1=rdenb[:], op=ALU.mult), reads=[f"E{ii}", "rdenb"], writes=[f"E{ii}"])
                s.op("pe", lambda e, Et=Et, kt=kt, impi=impi, nimp=nimp: e.matmul(ps_M[0][:], lhsT=ovl[:, kt, :], rhs=Et[:], start=(impi == 0), stop=(impi == nimp - 1)),
                     reads=[f"E{ii}", "ovl"], writes=["ps_M0"])
                impi += 1
            finish_branch(h, 0, True, ps_d[0:64, :], "ps_d")
        s.op("act", lambda e: e.activation(out=impT[:], in_=ps_M[0][:], func=AF.Copy), reads=["ps_M0"], writes=["impT"])
        for i in range(4):
            tb = t0 + 128 * i
            s.op("pe", lambda e, i=i: e.transpose(out=ps_m[:, 128:256], in_=impT[:, i * 128:(i + 1) * 128], identity=identf[:]),
                 reads=["impT", "identf"], writes=["ps_m"])
            s.op("act", lambda e: e.activation(out=sc[:], in_=ps_m[:, 128:256], func=AF.Copy), reads=["ps_m"], writes=["sc"])
            s.op("pool", lambda e, tb=tb: e.affine_select(out=sc[:], in_=sc[:], pattern=[[-64, 128]], compare_op=ALU.is_ge,
                                                          fill=fill_reg(e, 1e6), base=tb - 128, channel_multiplier=1), reads=["sc"], writes=["sc"])
            s.op("pool", lambda e, tb=tb: e.affine_select(out=sc[:], in_=sc[:], pattern=[[-64, 128]], compare_op=ALU.is_ge,
                                                          fill=fill_reg(e, -1e30), base=tb, channel_multiplier=1), reads=["sc"], writes=["sc"])
            s.op("pool", lambda e: e.memset(sc[:, 0:1], 1e6), reads=["sc"], writes=["sc"])
            s.op("dve", lambda e: e.max(out=m8[:, 0:8], in_=sc[:]), reads=["sc"], writes=["m8a"])
            s.op("dve", lambda e: e.match_replace(out=sc2[:], in_to_replace=m8[:, 0:8], in_values=sc[:], imm_value=-3e38),
                 reads=["sc", "m8a"], writes=["sc2"])
            s.op("dve", lambda e: e.max(out=m8[:, 8:16], in_=sc2[:]), reads=["sc2"], writes=["m8b"])
            s.op("dve", lambda e: e.tensor_scalar(out=selb[:], in0=sc[:], scalar1=m8[:, 15:16], scalar2=None, op0=ALU.is_ge),
                 reads=["sc", "m8b"], writes=["selb"])
            s.op("pe", lambda e: e.transpose(out=ps_t[:, 0, :], in_=selb[:], identity=ident[:]), reads=["selb", "ident"], writes=["ps_s2"])
            s.op("act", lambda e, i=i: e.activation(out=selT[:, i * 128:(i + 1) * 128], in_=ps_t[:, 0, :], func=AF.Copy), reads=["ps_s2"], writes=["selT"])
        items = []
        nkt = 4 * qc + 4
        grp_ctr = [0]
        for h in range(8):
            for kt in range(nkt):
                r = kt - 4 * qc
                cs = 128 * r if r > 0 else 0
                items.append(("pair", h, ksT, "ksT", VS, "VS", kt, slice(cs, 512), kt == 0, kt == nkt - 1,
                              causal_mask(cs) if r >= 0 else None, kt, grp_ctr[0]))
            items.append(("fin", h, 1, grp_ctr[0]))
            grp_ctr[0] += 1
            kt_lo = max(0, 4 * qc - 4)
            kt_first = 4 * qc - 1 if qc >= 1 else 0
            worder = [kt_first] + [k_ for k_ in range(kt_lo, nkt) if k_ != kt_first]
            for wi_, kt in enumerate(worder):
                r = kt - 4 * qc
                if r >= 0:
                    cs = 128 * r
                    cols, mf = slice(cs, 512), causal_mask(cs)
                else:
                    ce = 128 * (r + 5)
                    cols, mf = slice(0, ce), winlow_mask(ce - 128)
                items.append(("pair", h, kwT, "kwT", VW, "VW", kt, cols, wi_ == 0, wi_ == len(worder) - 1, mf, None, grp_ctr[0]))
            items.append(("fin", h, 2, grp_ctr[0]))
            grp_ctr[0] += 1
            items.append(("out", h, qc))
        LOOK = 2

        def do_a(ix):
            it = items[ix]
            if it[0] == "pair":
                _, h, kT, kkey, Vt, vkey, kt, cols, first, last, mf, selkt, grp = it
                attn_a(ix, h, kT, kt, kkey, cols, mf, sel_kt=selkt, first=first, grp=grp)

        def do_b(ix):
            it = items[ix]
            if it[0] == "pair":
                _, h, kT, kkey, Vt, vkey, kt, cols, first, last, mf, selkt, grp = it
                attn_b(ix, kt, Vt, vkey, cols, first, last)
            elif it[0] == "fin":
                g_ = it[3] % 2
                s.op("pool", lambda e: e.tensor_copy(out=Esb[:], in_=Esum[g_][:]), reads=[f"Esum{g_}"], writes=["Esb"])
                s.op("pe", lambda e: e.matmul(ps_d[0:64, :], lhsT=ones_bf[:, 0:64], rhs=Esb[:], start=True, stop=True),
                     reads=["Esb", "ones_bf"], writes=["ps_d"])
                finish_branch(it[1], it[2], False, ps_d[0:64, :], "ps_d")
            else:
                h, qc_ = it[1], it[2]
                oi = h % 2
                s.op("act", lambda e: e.activation(out=oTh[oi][:], in_=oacc[h][:], func=AF.Copy), reads=[f"oacc{h}"], writes=[f"oTh{oi}"])
                s.dma("sp", lambda q: q.dma_start(out=oT_l[h // 2][(h % 2) * 64:(h % 2 + 1) * 64, qc_ * 512:(qc_ + 1) * 512], in_=oTh[oi][:]),
                      reads=[f"oTh{oi}"], final=True)

        for ix in range(len(items) + LOOK):
            if ix < len(items):
                do_a(ix)
            if ix >= LOOK:
                do_b(ix - LOOK)
    return P.finish()


def _lay_cw(conv_w, conv_b):
    a = np.concatenate([conv_w, conv_b[None]], 0)
    a = a.reshape(4, NFC, 128).transpose(2, 1, 0)
    return np.ascontiguousarray(a.reshape(128, NFC * 4)).astype(np.float32)


def _lay_g(g):
    return np.ascontiguousarray(np.asarray(g, np.float32).reshape(8, 128).T)


def _rep(v, n=128):
    v = np.asarray(v, np.float32)
    return np.ascontiguousarray(np.broadcast_to(v[None], (n,) + v.shape)).astype(np.float32)


def _fox_inputs(z, b, hh):
    w_in = z["a_w_in"][0]
    wA = np.zeros((2, D, 512), np.float32)
    wB = np.zeros((2, D, 260), np.float32)
    for hp in range(2):
        h0 = hh * 8 + hp * 4
        wA[hp, :, 0:256] = w_in[:, h0 * 64:(h0 + 4) * 64]
        wA[hp, :, 256:512] = w_in[:, 1024 + h0 * 64:1024 + (h0 + 4) * 64]
        wB[hp, :, 0:256] = w_in[:, 2048 + h0 * 64:2048 + (h0 + 4) * 64]
        wB[hp, :, 256:260] = w_in[:, 3072 + h0:3072 + h0 + 4]
    return {"x": np.ascontiguousarray(z["x"][b]), "gl": _lay_g(z["a_norm"][0]), "wA": wA, "wB": wB,
            "bfl": _rep(z["a_b_f"][0][hh * 8:hh * 8 + 8]), "qg": _rep(z["a_q_gain"][0]), "kg": _rep(z["a_k_gain"][0])}


_ROPE_INV = (500000.0 ** (-np.arange(8, dtype=np.float32) * (2.0 / 16))).astype(np.float32)


def _nsa_inputs(z, h1_b, kvp_b, pos_b, g, TT=T):
    w_in = z["b_w_in"][0]
    NTL = TT // 128
    NCB = TT // 16
    NCT = NCB // 128
    wq = np.ascontiguousarray(w_in[:, g * 512:(g + 1) * 512])
    gcols = [1024 + br * 16 + g * 8 + hl for br in range(3) for hl in range(8)]
    wg = np.ascontiguousarray(w_in[:, gcols])
    bgl = _rep(z["b_b_gate"][0][[c - 1024 for c in gcols]])
    kv = None if kvp_b is None else np.ascontiguousarray(kvp_b.reshape(TT, 6, 2, 64)[:, :, g, :].reshape(TT, 384))
    pos = np.ascontiguousarray(pos_b.reshape(NTL, 128).T).astype(np.int32)
    ends = np.minimum(np.arange(NCB) * 16 + 31, TT - 1)
    pose = np.ascontiguousarray(pos_b[ends].reshape(NCT, 128).T).astype(np.int32)
    kgains = np.ascontiguousarray(np.broadcast_to(np.stack([z["kc_gain"], z["ks_gain"], z["kw_gain"]])[None], (128, 3, 64))).astype(np.float32)
    peT = np.concatenate([z["kc_pe"].T, z["vc_pe"].T], 0).astype(np.float32)
    w1 = np.concatenate([z["kc_w1"].reshape(32, 64, 256).transpose(1, 0, 2), z["vc_w1"].reshape(32, 64, 256).transpose(1, 0, 2)], 0)
    w2 = np.stack([z["kc_w2"].reshape(2, 128, 64).transpose(1, 0, 2), z["vc_w2"].reshape(2, 128, 64).transpose(1, 0, 2)], 1)
    return {"h1": None if h1_b is None else np.ascontiguousarray(h1_b), "gl": _lay_g(z["b_norm"][0]), "wq": wq, "wg": wg, "bgl": bgl, "qg": _rep(z["b_q_gain"][0]),
            "kvp": kv, "pos": pos, "pose": pose, "kgains": kgains, "invf": _rep(_ROPE_INV), "peT": np.ascontiguousarray(peT),
            "w1": np.ascontiguousarray(w1.astype(np.float32)), "w2": np.ascontiguousarray(w2.astype(np.float32))}


def make_precast_work(P, ios):
    s = P.s
    stg = [P.sb(f"pc_stage{i}", [128, 1408], F32) for i in range(2)]
    cbf = [P.sb(f"pc_cb{i}", [128, 1408], BF16) for i in range(2)]
    ctr = [0]
    work = []

    def piece(src_ap, n, dst_ap, view=None):
        def emit():
            i = ctr[0] % 2
            ctr[0] += 1
            st_t, cb = stg[i], cbf[i]
            s.dma("sp", lambda q: q.dma_start(out=st_t[:, 0:n], in_=src_ap), writes=[f"pc_stage{i}"])
            s.op("pool", lambda e: e.tensor_copy(out=cb[:, 0:n], in_=st_t[:, 0:n]), reads=[f"pc_stage{i}"], writes=[f"pc_cb{i}"])
            srcv = cb[:, 0:n] if view is None else view(cb[:, 0:n])
            s.dma("sp", lambda q: q.dma_start(out=dst_ap, in_=srcv), reads=[f"pc_cb{i}"], final=True)
        work.append(emit)

    for io_ in ios:
        w_out, w_up, w_dn = io_["w_out"], io_["w_up"], io_["w_dn"]
        for kc in range(8):
            piece(w_out[kc * 128:(kc + 1) * 128, :], 1024, io_["wout_s"][kc * 128:(kc + 1) * 128, :])
        for kc in range(8):
            for cb_ in range(4):
                piece(w_up[kc * 128:(kc + 1) * 128, cb_ * 1408:(cb_ + 1) * 1408], 1408, io_["wup_s"][:, cb_ * 11:(cb_ + 1) * 11, kc, :],
                      view=lambda a: a.rearrange("p (f n) -> p f n", f=11))
        for j in range(22):
            piece(w_dn[j * 128:(j + 1) * 128, :], 1024, io_["wdn_s"][j * 128:(j + 1) * 128, :])
        if "kv_w" in io_:
            for kc in range(8):
                piece(io_["kv_w"][kc * 128:(kc + 1) * 128, :], 768, io_["kvw_s"][kc * 128:(kc + 1) * 128, :])
        io_["precast_done"] = True
    return work


GROUPS = [[0, 1], [2, 3], [4, 5], [6, 7]]
NT_FFN = 4096 + 128


def _cc_block(nc, tag, pairs):
    with nc.cleanup_on_exit():
        sem = nc.alloc_semaphore(name=tag + "cc")
        with nc.Block() as block:
            @block.gpsimd
            def _(g):
                for i, (a, b) in enumerate(pairs):
                    g.collective_compute("AllGather", ALU.bypass, replica_groups=GROUPS,
                                         ins=[a.ap().opt()], outs=[b.ap().opt()]).then_inc(sem)
                    g.wait_ge(sem, i + 1)


def _dump_block(nc, tag, src, dst):
    with nc.cleanup_on_exit():
        sem = nc.alloc_semaphore(name=tag + "dump")
        with nc.Block() as block:
            @block.sync
            def _(q):
                q.dma_start(out=dst.ap(), in_=src.ap()).then_inc(sem, 16)
                q.wait_ge(sem, 16)


def build_fused(upto=99):
    nc = bass.Bass("TRN2", target_bir_lowering=False)
    ext = {}

    def ein(name, shape, dt=F32):
        ext[name] = nc.dram_tensor(name, list(shape), dt, kind="ExternalInput")
        return ext[name].ap()

    NTL, NCT = T // 128, T // 16 // 128
    ioA = {"x": ein("A_x", [T, D]), "gl": ein("A_gl", [128, 8]), "wA": ein("A_wA", [2, D, 512]), "wB": ein("A_wB", [2, D, 260]),
           "bfl": ein("A_bfl", [128, 8]), "qg": ein("A_qg", [128, 64]), "kg": ein("A_kg", [128, 64])}
    msk = ein("msk", [128, 2])
    ioB = {"msk": msk, "hin": ein("B_hin", [NT_FFN, D]), "w_out": ein("B_w_out", [D, D]), "gl": ein("B_gl", [128, 8]),
           "w_up": ein("B_w_up", [D, 2 * DFF]), "cw": ein("B_cw", [128, NFC * 4]), "w_dn": ein("B_w_dn", [DFF, D]),
           "kvg": ein("B_kvg", [128, 8]), "kv_w": ein("B_kv_w", [D, 768]), "bg": ein("B_bg", [128, 8])}
    ioC = {"msk": msk, "wq": ein("C_wq", [D, 512]), "wg": ein("C_wg", [D, 24]), "bgl": ein("C_bgl", [128, 24]), "qg": ein("C_qg", [128, 64]),
           "pos": ein("C_pos", [128, NTL], I32), "pose": ein("C_pose", [128, NCT], I32), "kgains": ein("C_kgains", [128, 3, 64]),
           "invf": ein("C_invf", [128, 8]), "peT": ein("C_peT", [128, 32]), "w1": ein("C_w1", [128, 32, 256]), "w2": ein("C_w2", [128, 2, 2, 64])}
    ioD = {"msk": msk, "w_out": ein("D_w_out", [D, D]), "gl": ein("D_gl", [128, 8]), "w_up": ein("D_w_up", [D, 2 * DFF]),
           "cw": ein("D_cw", [128, NFC * 4]), "w_dn": ein("D_w_dn", [DFF, D])}
    out = nc.dram_tensor("out", [4096, D], F32, kind="ExternalOutput")
    oT1_my = [nc.dram_tensor(f"oT1_my{k}", [128, T], BF16) for k in range(4)]
    oT1_all = [nc.dram_tensor(f"oT1_all{k}", [256, T], BF16) for k in range(4)]
    h1_my = nc.dram_tensor("h1_my", [4096, D], F32)
    hl_send = nc.dram_tensor("hl_send", [128, D], F32)
    hl_all = nc.dram_tensor("hl_all", [256, D], F32)
    xnT_my = [nc.dram_tensor(f"xnT_my{k}", [256, 4096], BF16) for k in range(4)]
    xnT_all = [nc.dram_tensor(f"xnT_all{k}", [512, 4096], BF16) for k in range(4)]
    kv_send = [nc.dram_tensor(f"kv_send{k}", [1024, 384], F32) for k in range(8)]
    kv_all = [nc.dram_tensor(f"kv_all{k}", [2048, 384], F32) for k in range(8)]
    oT2_my = [nc.dram_tensor(f"oT2_my{k}", [128, T], BF16) for k in range(4)]
    oT2_all = [nc.dram_tensor(f"oT2_all{k}", [256, T], BF16) for k in range(4)]
    aps = lambda l: [t_.ap() for t_ in l]

    for pre, io_ in (("B", ioB), ("D", ioD)):
        io_["wout_s"] = nc.dram_tensor(pre + "_wout_s", [D, D], BF16).ap()
        io_["wup_s"] = nc.dram_tensor(pre + "_wup_s", [128, NFC, 8, 128], BF16).ap()
        io_["wdn_s"] = nc.dram_tensor(pre + "_wdn_s", [DFF, D], BF16).ap()
    ioB["kvw_s"] = nc.dram_tensor("B_kvw_s", [D, 768], BF16).ap()
    ioA["oT_l"] = aps(oT1_my)
    ioB.update({"oT_all_l": aps(oT1_all), "hout": h1_my.ap(), "kv_send_l": aps(kv_send), "xnT_my_l": aps(xnT_my), "hl_send": hl_send.ap()})
    ioC.update({"xnT_all_l": aps(xnT_all), "kv_all_l": aps(kv_all), "oT_l": aps(oT2_my)})
    ioD.update({"oT_all_l": aps(oT2_all), "h_my": h1_my.ap(), "hl_all": hl_all.ap(), "hout": out.ap()})

    with nc.cleanup_on_exit():
        PA = Prog(nc, "A_", ioA)
        ioA["bg_work"] = make_precast_work(PA, [ioB, ioD])
        build_fox(PA, T)
    if upto == 0:
        dbg = nc.dram_tensor("dbg", [128, T], BF16, kind="ExternalOutput")
        _dump_block(nc, "d0_", oT1_my[0], dbg)
        return nc
    _cc_block(nc, "e1_", list(zip(oT1_my, oT1_all)))
    if upto == 1:
        dbg = nc.dram_tensor("dbg", [256, T], BF16, kind="ExternalOutput")
        _dump_block(nc, "d1_", oT1_all[3], dbg)
        return nc
    with nc.cleanup_on_exit():
        build_ffn(Prog(nc, "B_", ioB), True, NT_FFN, False)
    if upto == 2:
        dbg = nc.dram_tensor("dbg", [4096, D], F32, kind="ExternalOutput")
        _dump_block(nc, "d2_", h1_my, dbg)
        return nc
    _cc_block(nc, "e2_", [(hl_send, hl_all)] + list(zip(xnT_my, xnT_all)) + list(zip(kv_send, kv_all)))
    with nc.cleanup_on_exit():
        build_nsa(Prog(nc, "C_", ioC), T)
    _cc_block(nc, "e3_", list(zip(oT2_my, oT2_all)))
    with nc.cleanup_on_exit():
        build_ffn(Prog(nc, "D_", ioD), False, NT_FFN, True)
    return nc


def _core_inputs(z, c):
    b, r = c // 2, c % 2
    d = {}
    for k, v in _fox_inputs(z, b, r).items():
        d["A_" + k] = v
    m = np.zeros((128, 2), np.float32)
    m[:, r] = 1.0
    d["msk"] = m
    hin = np.zeros((NT_FFN, D), np.float32)
    if r == 0:
        hin[128:] = z["x"][b, 0:4096]
    else:
        hin[:] = z["x"][b, 4096 - 128:8192]
    d["B_hin"] = hin
    for L, pre, w_out in ((0, "B_", z["a_w_out"][0]), (1, "D_", z["b_w_out"][0])):
        d[pre + "w_out"] = np.ascontiguousarray(w_out)
        d[pre + "gl"] = _lay_g(z["f_norm"][L])
        d[pre + "w_up"] = np.ascontiguousarray(z["f_w_up"][L])
        d[pre + "cw"] = _lay_cw(z["f_conv_w"][L], z["f_conv_b"][L])
        d[pre + "w_dn"] = np.ascontiguousarray(z["f_w_down"][L])
    d["B_kvg"] = _lay_g(z["kv_norm"])
    d["B_kv_w"] = np.ascontiguousarray(z["kv_w"])
    d["B_bg"] = _lay_g(z["b_norm"][0])
    ni = _nsa_inputs(z, None, None, z["positions"][b], r)
    for k in ("wq", "wg", "bgl", "qg", "pos", "pose", "kgains", "invf", "peT", "w1", "w2"):
        d["C_" + k] = ni[k]
    return d


def kernel(**inputs):
    z = {k: np.asarray(v) for k, v in inputs.items()}
    B = z["x"].shape[0]
    cores = list(range(8))
    nc = build_fused()
    res = run_bass_kernel_spmd(nc, [_core_inputs(z, c) for c in cores], core_ids=cores)
    out = np.stack([np.concatenate([res.results[2 * b]["out"], res.results[2 * b + 1]["out"]], 0) for b in range(B)], 0)
    return out.astype(np.float32)
```

```python
import contextlib
import numpy as np
import ml_dtypes
import concourse.bass as bass
import concourse.mybir as mybir
from concourse.bass_utils import run_bass_kernel_spmd

F32 = mybir.dt.float32
BF16 = mybir.dt.bfloat16
I32 = mybir.dt.int32
AF = mybir.ActivationFunctionType
ALU = mybir.AluOpType
AX = mybir.AxisListType

EPOCH = 8192
NDMASEM = 24

D = 1024
T = 8192
NH = 16
DH = 64
DFF = 2816
NFC = 44
EPS = 1e-6


class Sched:
    ENGS = ("pe", "act", "dve", "pool", "sp")

    def __init__(self, nc, tag=""):
        self.nc = nc
        self.tag = tag
        self.ops = {e: [] for e in self.ENGS}
        self.cnt = {e: 0 for e in self.ENGS}
        self.dcnt = {e: 0 for e in self.ENGS}
        self.lastw = {}
        self.readers = {}
        self.known = {e: {} for e in self.ENGS}
        self.sems = {}
        self.final_tokens = []

    def _tok_compute(self, eng):
        i = self.cnt[eng]
        self.cnt[eng] += 1
        return (("c", eng, i // EPOCH), i % EPOCH + 1)

    def _tok_dma(self, q):
        i = self.dcnt[q]
        self.dcnt[q] += 1
        return (("d", q, i % NDMASEM), 16 * (i // NDMASEM + 1))

    def _need(self, eng, waits, tok):
        sk, v = tok
        if self.known[eng].get(sk, 0) >= v:
            return
        waits[sk] = max(waits.get(sk, 0), v)

    def _deps(self, eng, reads, writes, is_dma):
        waits = {}

        def same_pe(t):
            return eng == "pe" and (not is_dma) and t[0][0] == "c" and t[0][1] == "pe"

        for r in reads:
            t = self.lastw.get(r)
            if t is not None and not same_pe(t):
                self._need(eng, waits, t)
        for w in writes:
            t = self.lastw.get(w)
            if t is not None and not same_pe(t):
                self._need(eng, waits, t)
            for t in self.readers.get(w, ()):
                if (not is_dma) and t[0][0] == "c" and t[0][1] == eng:
                    continue
                self._need(eng, waits, t)
        for sk, v in waits.items():
            self.known[eng][sk] = v
        return waits

    def _commit(self, tok, reads, writes):
        for r in reads:
            self.readers.setdefault(r, []).append(tok)
        for w in writes:
            self.lastw[w] = tok
            self.readers[w] = []

    def op(self, eng, emit, reads=(), writes=()):
        waits = self._deps(eng, reads, writes, False)
        tok = self._tok_compute(eng)
        self.ops[eng].append((waits, emit, tok))
        self._commit(tok, reads, writes)
        return tok

    def dma(self, q, emit, reads=(), writes=(), final=False):
        waits = self._deps(q, reads, writes, True)
        tok = self._tok_dma(q)
        sk, v = tok
        if v > 16 and self.known[q].get(sk, 0) < v - 16:
            waits[sk] = max(waits.get(sk, 0), v - 16)
            self.known[q][sk] = v - 16
        self.ops[q].append((waits, emit, tok))
        self._commit(tok, reads, writes)
        if final:
            self.final_tokens.append(tok)
        return tok

    def emit_all(self):
        nc = self.nc
        semkeys = set()
        for e in self.ENGS:
            for waits, emit, tok in self.ops[e]:
                semkeys.add(tok[0])
                semkeys.update(waits.keys())
        semkeys = sorted(semkeys, key=str)
        with contextlib.ExitStack() as st:
            for sk in semkeys:
                self.sems[sk] = nc.alloc_semaphore(name=self.tag + "s_" + "_".join(str(x) for x in sk))
            block = st.enter_context(nc.Block())
            sems = self.sems

            def run(engname, eng):
                for waits, emit, tok in self.ops[engname]:
                    for sk, v in waits.items():
                        eng.wait_ge(sems[sk], v)
                    ins = emit(eng)
                    ins.then_inc(sems[tok[0]], 16 if tok[0][0] == "d" else 1)
                if engname == "sp":
                    for sk, v in self.final_tokens:
                        eng.wait_ge(sems[sk], v)

            @block.tensor
            def _(eng):
                run("pe", eng)

            @block.scalar
            def _(eng):
                run("act", eng)

            @block.vector
            def _(eng):
                run("dve", eng)

            @block.gpsimd
            def _(eng):
                run("pool", eng)

            @block.sync
            def _(eng):
                run("sp", eng)


class Prog:
    def __init__(self, nc, tag, io):
        self.nc = nc
        self.tag = tag
        self.io = io
        self.st = contextlib.ExitStack()
        self.s = Sched(self.nc, tag)

    def din(self, name, shape, dt=F32):
        ap = self.io[name]
        assert list(ap.shape) == list(shape), (name, ap.shape, shape)
        return ap

    dout = din

    def sb(self, name, shape, dt):
        return self.st.enter_context(self.nc.sbuf_tensor(self.tag + name, list(shape), dt))

    def ps(self, name, shape, dt=F32):
        return self.st.enter_context(self.nc.psum_tensor(self.tag + name, list(shape), dt))

    def finish(self):
        import os
        if os.environ.get("KDEBUG"):
            print("phase", self.tag, "sbuf remaining", self.nc.sbuf_bytes_remaining, flush=True)
        self.s.emit_all()
        self.st.close()
        return self.nc


def make_ident(P, dt, name):
    s = P.s
    idf = P.sb(name + "_f", [128, 128], F32)
    ident = P.sb(name, [128, 128], dt)
    s.op("pool", lambda e: e.memset(idf[:], 1.0), writes=[name + "_f"])
    s.op("pool", lambda e: e.affine_select(out=idf[:], in_=idf[:], pattern=[[-1, 128]],
                                           compare_op=ALU.is_equal, fill=0.0, base=0, channel_multiplier=1),
         reads=[name + "_f"], writes=[name + "_f"])
    s.op("pool", lambda e: e.tensor_copy(out=ident[:], in_=idf[:]), reads=[name + "_f"], writes=[name])
    return ident


def build_ffn(P, with_kv, NT, hin_int):
    nc, s = P.nc, P.s
    NTO = NT - 128
    msk = P.din("msk", [128, 2])
    if hin_int:
        h_my = P.din("h_my", [NTO, D])
        hl_all = P.din("hl_all", [256, D])
    else:
        hin = P.din("hin", [NT, D])
    oT_l = P.io["oT_all_l"]
    w_out = P.din("w_out", [D, D])
    gl = P.din("gl", [128, 8])
    w_up = P.din("w_up", [D, 2 * DFF])
    cw = P.din("cw", [128, NFC * 4])
    w_dn = P.din("w_dn", [DFF, D])
    hout = P.dout("hout", [NTO, D])
    if with_kv:
        kvg = P.din("kvg", [128, 8])
        kv_w = P.din("kv_w", [D, 768])
        bg = P.din("bg", [128, 8])
        kv_send_l = P.io["kv_send_l"]
        xnT_my_l = P.io["xnT_my_l"]
        hl_send = P.dout("hl_send", [128, D])

    W = 512
    ident = make_ident(P, BF16, "ident")
    gl_sb = P.sb("gl_sb", [128, 8], F32)
    cw_sb = P.sb("cw_sb", [128, NFC * 4], F32)
    s.dma("sp", lambda q: q.dma_start(out=gl_sb[:], in_=gl), writes=["gl"])
    s.dma("sp", lambda q: q.dma_start(out=cw_sb[:], in_=cw), writes=["cw"])
    msk_sb = P.sb("msk_sb", [128, 2], F32)
    s.dma("sp", lambda q: q.dma_start(out=msk_sb[:], in_=msk), writes=["msk"])
    if with_kv:
        kvg_sb = P.sb("kvg_sb", [128, 8], F32)
        s.dma("sp", lambda q: q.dma_start(out=kvg_sb[:], in_=kvg), writes=["kvg"])
        bg_sb = P.sb("bg_sb", [128, 8], F32)
        s.dma("sp", lambda q: q.dma_start(out=bg_sb[:], in_=bg), writes=["bg"])

    oT_sb = P.sb("oT_sb", [128, 8, W], BF16)
    oT_a = P.sb("oT_a", [128, 8, W], BF16)
    wbig = P.sb("wbig", [128, 22 * 1024], BF16)
    wstage = [P.sb(f"wstage{i}", [128, 2048], F32) for i in range(2)]
    hin_sb = [P.sb(f"hin_sb{i}", [128, D], F32) for i in range(2)]
    hmid = P.sb("hmid", [128, 4, D], F32)
    junk = P.sb("junk", [128, D], BF16)
    stat = P.sb("stat", [128, 8], F32)
    xs = [P.sb(f"xs{i}", [128, D], BF16) for i in range(2)]
    hnT = P.sb("hnT", [128, 8, W], BF16)
    wup_bf = [P.sb(f"wup_bf{i}", [128, 8, 256], BF16) for i in range(2)]
    ubuf = [P.sb(f"ubuf{i}", [128, 2 + W], F32) for i in range(2)]
    halo = P.sb("halo", [128, NFC, 2], F32)
    ctmp = [P.sb(f"ctmp{i}", [128, W], F32) for i in range(4)]
    sg = P.sb("sg", [128, W], F32)
    gT = P.sb("gT", [128, 22, W], BF16)
    otile = [P.sb(f"otile{i}", [128, D], F32) for i in range(2)]
    if with_kv:
        hkT = P.sb("hkT", [128, 8, W], BF16)
        hbT = P.sb("hbT", [128, 8, W], BF16)
        kvtile = [P.sb(f"kvtile{i}", [128, 768], F32) for i in range(2)]

    ps_o = [P.ps(f"ps_o{i}", [128, 512]) for i in range(2)]
    ps_t = P.ps("ps_t", [128, 8, 128], BF16)
    ps_u = [P.ps(f"ps_u{i}", [128, 512]) for i in range(4)]

    s.op("pool", lambda e: e.memset(halo[:], 0.0), writes=["halo"])

    w_out_v = w_out.rearrange("(k p) n -> p k n", p=128)
    w_up_v = w_up.rearrange("(k p) n -> p k n", p=128)
    w_dn_v = w_dn.rearrange("(k p) n -> p k n", p=128)
    def oT_load(q, dst, gcol, Wc):
        ins = None
        for kc in range(8):
            ins = q.dma_start(out=dst[:, kc, 0:Wc], in_=oT_l[kc % 4][(kc // 4) * 128:(kc // 4 + 1) * 128, gcol:gcol + Wc])
        return ins
    if with_kv:
        kv_w_v = kv_w.rearrange("(k p) n -> p k n", p=128)

    stage_ctr = [0]
    wout_s, wup_s, wdn_s = P.io["wout_s"], P.io["wup_s"], P.io["wdn_s"]
    kvw_s = P.io["kvw_s"] if with_kv else None
    cbuf = [P.sb(f"cbuf{i}", [128, 2048], BF16) for i in range(3)]
    skeys = []
    cast_ctr = [0]

    def precast(src_ap, n, dst_ap, view=None):
        c = cast_ctr[0]
        cast_ctr[0] += 1
        i, ci = c % 2, c % 3
        st_t, cb = wstage[i], cbuf[ci]
        s.dma("sp", lambda q: q.dma_start(out=st_t[:, 0:n], in_=src_ap), writes=[f"wstage{i}"])
        eng = ("pool", "act", "dve")[c % 3]
        if eng == "act":
            s.op("act", lambda e: e.activation(out=cb[:, 0:n], in_=st_t[:, 0:n], func=AF.Copy), reads=[f"wstage{i}"], writes=[f"cbuf{ci}"])
        else:
            s.op(eng, lambda e: e.tensor_copy(out=cb[:, 0:n], in_=st_t[:, 0:n]), reads=[f"wstage{i}"], writes=[f"cbuf{ci}"])
        key = f"ws{c}"
        skeys.append(key)
        srcv = cb[:, 0:n] if view is None else view(cb[:, 0:n])
        s.dma("sp", lambda q: q.dma_start(out=dst_ap, in_=srcv), reads=[f"cbuf{ci}"], writes=[key])

    if not P.io.get("precast_done"):
        for kc in range(8):
            precast(w_out[kc * 128:(kc + 1) * 128, :], 1024, wout_s[kc * 128:(kc + 1) * 128, :])
        for kc in range(8):
            for cb_ in range(4):
                precast(w_up[kc * 128:(kc + 1) * 128, cb_ * 1408:(cb_ + 1) * 1408], 1408, wup_s[:, cb_ * 11:(cb_ + 1) * 11, kc, :],
                        view=lambda a: a.rearrange("p (f n) -> p f n", f=11))
        for j in range(22):
            precast(w_dn[j * 128:(j + 1) * 128, :], 1024, wdn_s[j * 128:(j + 1) * 128, :])
        if with_kv:
            for kc in range(8):
                precast(kv_w[kc * 128:(kc + 1) * 128, :], 768, kvw_s[kc * 128:(kc + 1) * 128, :])
    wout_sv = wout_s.rearrange("(k p) n -> p k n", p=128)
    wdn_sv = wdn_s.rearrange("(j p) n -> p j n", p=128)
    if with_kv:
        kvw_sv = kvw_s.rearrange("(k p) n -> p k n", p=128)

    def load_cast(dst_ap, src_ap, ncols, dstkey):
        i = stage_ctr[0] % 2
        stage_ctr[0] += 1
        st_t = wstage[i]
        s.dma("sp", lambda q: q.dma_start(out=st_t[:, 0:ncols], in_=src_ap), writes=[f"wstage{i}"])
        s.op("pool", lambda e: e.tensor_copy(out=dst_ap, in_=st_t[:, 0:ncols]), reads=[f"wstage{i}"], writes=[dstkey])

    def norm_transpose(src_tile, src_key, dstT, dst_key, col0, gain_sb, gain_key, par, second=None):
        x_s = xs[par]
        s.op("act", lambda e: e.activation(out=junk[:], in_=src_tile, func=AF.Square, accum_out=stat[:, 0:1]),
             reads=[src_key], writes=["junk", "stat0"])
        s.op("dve", lambda e: e.tensor_scalar(out=stat[:, 1:2], in0=stat[:, 0:1], scalar1=1.0 / D, scalar2=EPS,
                                               op0=ALU.mult, op1=ALU.add), reads=["stat0"], writes=["stat1"])
        s.op("act", lambda e: e.activation(out=stat[:, 2:3], in_=stat[:, 1:2], func=AF.Sqrt), reads=["stat1"], writes=["stat2"])
        s.op("dve", lambda e: e.reciprocal(out=stat[:, 3:4], in_=stat[:, 2:3]), reads=["stat2"], writes=["stat3"])
        s.op("dve", lambda e: e.tensor_scalar(out=x_s[:], in0=src_tile, scalar1=stat[:, 3:4], scalar2=None, op0=ALU.mult),
             reads=[src_key, "stat3"], writes=[f"xs{par}"])
        for kc in range(8):
            s.op("pe", lambda e, kc=kc: e.transpose(out=ps_t[:, kc, :], in_=x_s[:, kc * 128:(kc + 1) * 128], identity=ident[:]),
                 reads=[f"xs{par}", "ident"], writes=["ps_t"])
        for kc in range(8):
            s.op("act", lambda e, kc=kc: e.activation(out=dstT[:, kc, col0:col0 + 128], in_=ps_t[:, kc, :], func=AF.Copy,
                                                       scale=gain_sb[:, kc:kc + 1]),
                 reads=["ps_t", gain_key], writes=[dst_key])
        if second is not None:
            d2, k2, g2, gk2 = second
            for kc in range(8):
                s.op("act", lambda e, kc=kc: e.activation(out=d2[:, kc, col0:col0 + 128], in_=ps_t[:, kc, :], func=AF.Copy,
                                                           scale=g2[:, kc:kc + 1]),
                     reads=["ps_t", gk2], writes=[k2])

    chunks = [(0, 128)] + [(128 + 512 * i, 512) for i in range((NT - 128) // 512)]
    tile_ctr = 0
    for (c0, Wc) in chunks:
        nt = Wc // 128
        s.dma("sp", lambda q: q.dma_start(out=wbig[:, 0:8192].rearrange("p (k n) -> p k n", k=8), in_=wout_sv), reads=skeys, writes=["wbig"])
        g1 = 4096 - 128 + c0
        for kc in range(8):
            s.dma("sp", lambda q, g1=g1, Wc=Wc, kc=kc: q.dma_start(out=oT_sb[:, kc, 0:Wc],
                                                                 in_=oT_l[kc % 4][(kc // 4) * 128:(kc // 4 + 1) * 128, g1:g1 + Wc]), writes=["oT_sb"])
        s.op("dve", lambda e, Wc=Wc: e.tensor_scalar(out=oT_sb[:, :, 0:Wc], in0=oT_sb[:, :, 0:Wc], scalar1=msk_sb[:, 1:2], scalar2=None,
                                                     op0=ALU.mult), reads=["oT_sb", "msk"], writes=["oT_sb"])
        if c0 >= 128:
            g0 = c0 - 128
            for kc in range(8):
                s.dma("sp", lambda q, g0=g0, Wc=Wc, kc=kc: q.dma_start(out=oT_a[:, kc, 0:Wc],
                                                                     in_=oT_l[kc % 4][(kc // 4) * 128:(kc // 4 + 1) * 128, g0:g0 + Wc]), writes=["oT_a"])
            s.op("dve", lambda e, Wc=Wc: e.scalar_tensor_tensor(out=oT_sb[:, :, 0:Wc], in0=oT_a[:, :, 0:Wc], scalar=msk_sb[:, 0:1],
                                                                in1=oT_sb[:, :, 0:Wc], op0=ALU.mult, op1=ALU.add),
                 reads=["oT_a", "oT_sb", "msk"], writes=["oT_sb"])
        for i in range(nt):
            par = tile_ctr % 2
            tile_ctr += 1
            r0 = c0 + i * 128
            hs = hin_sb[par]
            if not hin_int:
                s.dma("sp", lambda q, hs=hs, r0=r0: q.dma_start(out=hs[:], in_=hin[r0:r0 + 128, :]), writes=[f"hin_sb{par}"])
            elif r0 < 128:
                s.dma("sp", lambda q, hs=hs: q.dma_start(out=hs[:], in_=hl_all[0:128, :]), writes=[f"hin_sb{par}"])
                s.op("dve", lambda e, hs=hs: e.tensor_scalar(out=hs[:], in0=hs[:], scalar1=msk_sb[:, 1:2], scalar2=None, op0=ALU.mult),
                     reads=[f"hin_sb{par}", "msk"], writes=[f"hin_sb{par}"])
            else:
                s.dma("sp", lambda q, hs=hs, r0=r0: q.dma_start(out=hs[:], in_=h_my[r0 - 128:r0, :]), writes=[f"hin_sb{par}"])
            for kc in range(8):
                for hf in range(2):
                    s.op("pe", lambda e, kc=kc, hf=hf, i=i: e.matmul(ps_o[hf][:], lhsT=oT_sb[:, kc, i * 128:(i + 1) * 128],
                                                                    rhs=wbig[:, kc * 1024 + hf * 512: kc * 1024 + hf * 512 + 512],
                                                                    start=(kc == 0), stop=(kc == 7)),
                         reads=["oT_sb", "wbig"], writes=[f"ps_o{hf}"])
            for hf in range(2):
                s.op("dve", lambda e, hf=hf, i=i, hs=hs: e.tensor_tensor(out=hmid[:, i, hf * 512:(hf + 1) * 512], in0=ps_o[hf][:],
                                                                          in1=hs[:, hf * 512:(hf + 1) * 512], op=ALU.add),
                     reads=[f"ps_o{hf}", f"hin_sb{par}"], writes=[f"hmid{i}"])
            norm_transpose(hmid[:, i, :], f"hmid{i}", hnT, "hnT", i * 128, gl_sb, "gl", par)
        for j0 in (0, 6, 12, 18):
            j1 = min(j0 + 6, 22)
            s.dma("sp", lambda q, j0=j0, j1=j1: q.dma_start(out=wbig[:, j0 * 1024:j1 * 1024].rearrange("p (j n) -> p j n", j=j1 - j0),
                                                          in_=wdn_sv[:, j0:j1, :]), reads=skeys, writes=["wbig"])
        for j in range(22):
            wb = wup_bf[j % 2]
            wkey = f"wup_bf{j % 2}"
            for part in range(2):
                s.dma("sp", lambda q, wb=wb, part=part, j=j: q.dma_start(out=wb[:, :, part * 128:(part + 1) * 128], in_=wup_s[:, part * 22 + j, :, :]),
                      reads=skeys, writes=[wkey + f"_{part}"])
            for kc in range(8):
                for part in range(2):
                    pu = ps_u[(j % 2) * 2 + part]
                    pukey = f"ps_u{(j % 2) * 2 + part}"
                    s.op("pe", lambda e, kc=kc, pu=pu, wb=wb, part=part, Wc=Wc: e.matmul(
                        pu[:, 0:Wc], lhsT=wb[:, kc, part * 128:(part + 1) * 128], rhs=hnT[:, kc, 0:Wc],
                        start=(kc == 0), stop=(kc == 7)),
                        reads=[wkey + f"_{part}", "hnT"], writes=[pukey])
            for part in range(2):
                idx = part * 22 + j
                pu = ps_u[(j % 2) * 2 + part]
                pukey = f"ps_u{(j % 2) * 2 + part}"
                ub = ubuf[part]
                ubk = f"ubuf{part}"
                s.op("act", lambda e, ub=ub, pu=pu, Wc=Wc: e.activation(out=ub[:, 2:2 + Wc], in_=pu[:, 0:Wc], func=AF.Copy),
                     reads=[pukey], writes=[ubk])
                s.op("pool", lambda e, ub=ub, idx=idx: e.tensor_copy(out=ub[:, 0:2], in_=halo[:, idx, :]),
                     reads=["halo%d" % idx], writes=[ubk + "h"])
                s.op("pool", lambda e, ub=ub, idx=idx, Wc=Wc: e.tensor_copy(out=halo[:, idx, :], in_=ub[:, Wc:Wc + 2]),
                     reads=[ubk, ubk + "h"], writes=["halo%d" % idx])
                c1, c2, c3 = ctmp[part * 2], ctmp[part * 2 + 1], ctmp[part * 2]
                k1, k2 = f"ctmp{part * 2}", f"ctmp{part * 2 + 1}"
                s.op("dve", lambda e, ub=ub, c1=c1, idx=idx, Wc=Wc: e.tensor_scalar(
                    out=c1[:, 0:Wc], in0=ub[:, 2:2 + Wc], scalar1=cw_sb[:, idx * 4 + 2:idx * 4 + 3],
                    scalar2=cw_sb[:, idx * 4 + 3:idx * 4 + 4], op0=ALU.mult, op1=ALU.add),
                    reads=[ubk, "cw"], writes=[k1])
                s.op("dve", lambda e, ub=ub, c1=c1, c2=c2, idx=idx, Wc=Wc: e.scalar_tensor_tensor(
                    out=c2[:, 0:Wc], in0=ub[:, 1:1 + Wc], scalar=cw_sb[:, idx * 4 + 1:idx * 4 + 2], in1=c1[:, 0:Wc],
                    op0=ALU.mult, op1=ALU.add), reads=[ubk, ubk + "h", "cw", k1], writes=[k2])
                s.op("dve", lambda e, ub=ub, c2=c2, c3=c3, idx=idx, Wc=Wc: e.scalar_tensor_tensor(
                    out=c3[:, 0:Wc], in0=ub[:, 0:Wc], scalar=cw_sb[:, idx * 4:idx * 4 + 1], in1=c2[:, 0:Wc],
                    op0=ALU.mult, op1=ALU.add), reads=[ubk, ubk + "h", "cw", k2], writes=[k1])
            s.op("act", lambda e, Wc=Wc: e.activation(out=sg[:, 0:Wc], in_=ctmp[0][:, 0:Wc], func=AF.Silu),
                 reads=["ctmp0"], writes=["sg"])
            s.op("pool", lambda e, j=j, Wc=Wc: e.tensor_tensor(out=gT[:, j, 0:Wc], in0=sg[:, 0:Wc], in1=ctmp[2][:, 0:Wc], op=ALU.mult),
                 reads=["sg", "ctmp2"], writes=["gT"])
        for i in range(nt):
            par = tile_ctr % 2
            tile_ctr += 1
            r0 = c0 + i * 128
            ot = otile[par]
            for j in range(22):
                for hf in range(2):
                    s.op("pe", lambda e, j=j, hf=hf, i=i: e.matmul(ps_o[hf][:], lhsT=gT[:, j, i * 128:(i + 1) * 128],
                                                                  rhs=wbig[:, j * 1024 + hf * 512: j * 1024 + hf * 512 + 512],
                                                                  start=(j == 0), stop=(j == 21)),
                         reads=["gT", "wbig"], writes=[f"ps_o{hf}"])
            for hf in range(2):
                s.op("dve", lambda e, hf=hf, i=i, ot=ot: e.tensor_tensor(out=ot[:, hf * 512:(hf + 1) * 512], in0=ps_o[hf][:],
                                                                          in1=hmid[:, i, hf * 512:(hf + 1) * 512], op=ALU.add),
                     reads=[f"ps_o{hf}", f"hmid{i}"], writes=[f"otile{par}"])
            if r0 >= 128:
                s.dma("sp", lambda q, ot=ot, r0=r0: q.dma_start(out=hout[r0 - 128:r0, :], in_=ot[:]),
                      reads=[f"otile{par}"], final=True)
            if with_kv and r0 == NT - 128:
                s.dma("sp", lambda q, ot=ot: q.dma_start(out=hl_send[:, :], in_=ot[:]), reads=[f"otile{par}"], final=True)
            if with_kv and r0 >= 128:
                norm_transpose(ot[:], f"otile{par}", hkT, "hkT", i * 128, kvg_sb, "kvg", par, second=(hbT, "hbT", bg_sb, "bg"))
        if with_kv and c0 >= 128:
            for kc in range(8):
                s.dma("sp", lambda q, c0=c0, Wc=Wc, kc=kc: q.dma_start(
                    out=xnT_my_l[kc // 2][(kc % 2) * 128:(kc % 2 + 1) * 128, c0 - 128:c0 - 128 + Wc], in_=hbT[:, kc, 0:Wc]),
                    reads=["hbT"], final=True)
            s.dma("sp", lambda q: q.dma_start(out=wbig[:, 0:6144].rearrange("p (k n) -> p k n", k=8), in_=kvw_sv), reads=skeys, writes=["wbig"])
            for i in range(nt):
                par = tile_ctr % 2
                tile_ctr += 1
                r0 = c0 + i * 128
                kt = kvtile[par]
                for hf in range(2):
                    for kc in range(8):
                        s.op("pe", lambda e, kc=kc, hf=hf, i=i: e.matmul(ps_o[hf][:, 0:384], lhsT=hkT[:, kc, i * 128:(i + 1) * 128],
                                                                        rhs=wbig[:, kc * 768 + hf * 384: kc * 768 + hf * 384 + 384],
                                                                        start=(kc == 0), stop=(kc == 7)),
                             reads=["hkT", "wbig"], writes=[f"ps_o{hf}"])
                    s.op("act", lambda e, hf=hf, kt=kt: e.activation(out=kt[:, hf * 384:(hf + 1) * 384], in_=ps_o[hf][:, 0:384], func=AF.Copy),
                         reads=[f"ps_o{hf}"], writes=[f"kvtile{par}"])
                for g in range(2):
                    tk = r0 - 128
                    s.dma("sp", lambda q, kt=kt, tk=tk, g=g: q.dma_start(
                        out=kv_send_l[g * 4 + tk // 1024][tk % 1024:tk % 1024 + 128, :].rearrange("p (a d) -> p a d", a=6),
                        in_=kt[:].rearrange("p (a g d) -> p a g d", a=6, g=2)[:, :, g, :]), reads=[f"kvtile{par}"], final=True)
    return P.finish()


_FILL_REGS = {}


def fill_reg(e, val):
    key = (id(e), val)
    if key not in _FILL_REGS:
        _FILL_REGS[key] = e.to_reg(val)
    return _FILL_REGS[key]


def rms_rstd(s, P, ss_ap, out_ap, n, rkey, wkey, tmp_ap, tkey):
    s.op("act", lambda e: e.activation(out=tmp_ap, in_=ss_ap, func=AF.Ln, scale=1.0 / n, bias=EPS), reads=[rkey], writes=[tkey])
    s.op("act", lambda e: e.activation(out=out_ap, in_=tmp_ap, func=AF.Exp, scale=-0.5), reads=[tkey], writes=[wkey])


def norm_transpose_g(P, src_tile, src_key, dstT, dst_key, col0, gain_sb, gain_key, xs_t, xs_key, junk, stat, ps_t, ident):
    s = P.s
    s.op("act", lambda e: e.activation(out=junk[:], in_=src_tile, func=AF.Square, accum_out=stat[:, 0:1]),
         reads=[src_key], writes=["junk", "stat0"])
    rms_rstd(s, P, stat[:, 0:1], stat[:, 3:4], D, "stat0", "stat3", stat[:, 1:2], "stat1")
    s.op("dve", lambda e: e.tensor_scalar(out=xs_t[:], in0=src_tile, scalar1=stat[:, 3:4], scalar2=None, op0=ALU.mult),
         reads=[src_key, "stat3"], writes=[xs_key])
    for kc in range(8):
        s.op("pe", lambda e, kc=kc: e.transpose(out=ps_t[:, kc, :], in_=xs_t[:, kc * 128:(kc + 1) * 128], identity=ident[:]),
             reads=[xs_key, "ident"], writes=["ps_t"])
    for kc in range(8):
        s.op("act", lambda e, kc=kc: e.activation(out=dstT[:, kc, col0:col0 + 128], in_=ps_t[:, kc, :], func=AF.Copy,
                                                   scale=gain_sb[:, kc:kc + 1]),
             reads=["ps_t", gain_key], writes=[dst_key])


def build_fox(P, TT):
    nc, s = P.nc, P.s
    NTL = TT // 128
    NQC = TT // 512
    x = P.din("x", [TT, D])
    gl = P.din("gl", [128, 8])
    wA = P.din("wA", [2, D, 512])
    wB = P.din("wB", [2, D, 260])
    bfl = P.din("bfl", [128, 8])
    qg = P.din("qg", [128, 64])
    kg = P.din("kg", [128, 64])
    oT_l = P.io["oT_l"]

    ident = make_ident(P, BF16, "ident")
    identf = make_ident(P, F32, "identf")
    gl_sb = P.sb("gl_sb", [128, 8], F32)
    bfl_sb = P.sb("bfl_sb", [128, 8], F32)
    qg_sb = P.sb("qg_sb", [128, 64], F32)
    kg_sb = P.sb("kg_sb", [128, 64], F32)
    for t_, d_, k_ in ((gl_sb, gl, "gl"), (bfl_sb, bfl, "bfl"), (qg_sb, qg, "qg"), (kg_sb, kg, "kg")):
        s.dma("sp", lambda q, t_=t_, d_=d_: q.dma_start(out=t_[:], in_=d_), writes=[k_])
    G8 = P.sb("G8", [128, 8, 64], F32)
    for h in range(4):
        s.op("pool", lambda e, h=h: e.tensor_scalar(out=G8[:, h, :], in0=qg_sb[:], scalar1=0.125, scalar2=None, op0=ALU.mult),
             reads=["qg"], writes=["G8"])
        s.op("pool", lambda e, h=h: e.tensor_copy(out=G8[:, 4 + h, :], in_=kg_sb[:]), reads=["kg"], writes=["G8"])
    ones_bf = P.sb("ones_bf", [128, 64], BF16)
    s.op("pool", lambda e: e.memset(ones_bf[:], 1.0), writes=["ones_bf"])
    onesf = P.sb("onesf", [128, 128], F32)
    s.op("pool", lambda e: e.memset(onesf[:], 1.0), writes=["onesf"])
    tri = P.sb("tri", [128, 128], F32)
    s.op("pool", lambda e: e.memset(tri[:], 1.0), writes=["tri"])
    s.op("pool", lambda e: e.affine_select(out=tri[:], in_=tri[:], pattern=[[1, 128]], compare_op=ALU.is_ge, fill=0.0,
                                           base=0, channel_multiplier=-1), reads=["tri"], writes=["tri"])
    selneg = P.sb("selneg", [4, 4, 128], F32)
    s.op("pool", lambda e: e.memset(selneg[:], -1.0), writes=["selneg"])
    s.op("pool", lambda e: e.affine_select(out=selneg[:], in_=selneg[:], pattern=[[1, 4], [0, 128]], compare_op=ALU.is_equal,
                                           fill=0.0, base=0, channel_multiplier=-1), reads=["selneg"], writes=["selneg"])

    KT = P.sb("KT", [128, 2, TT], BF16)
    V = P.sb("V", [128, NTL, 4, 128], BF16)
    s.op("pool", lambda e: e.memset(V[:], 1.0), writes=["V"])
    wA_bf = P.sb("wA_bf", [128, 8, 512], BF16)
    wB_bf = P.sb("wB_bf", [128, 8, 260], BF16)
    wstage = [P.sb(f"wstage{i}", [128, 512], F32) for i in range(2)]
    x_sb = [P.sb(f"x_sb{i}", [128, D], F32) for i in range(2)]
    xs = [P.sb(f"xs{i}", [128, D], BF16) for i in range(2)]
    junk = P.sb("junk", [128, D], BF16)
    stat = P.sb("stat", [128, 8], F32)
    xnT = P.sb("xnT", [128, 8, 512], BF16)
    sq = P.sb("sq", [128, 512], F32)
    ss8 = P.sb("ss8", [128, 24], F32)
    t1 = P.sb("t1", [128, 8, 64], F32)
    qkn = P.sb("qkn", [128, 8, 64], BF16)
    QT = P.sb("QT", [128, 2, 512], BF16)
    zf = P.sb("zf", [128, 16], F32)
    Ltok = P.sb("Ltok", [128, NTL, 4], F32)
    R = P.sb("R", [128, 4], F32)
    LT = P.sb("LT", [4, 512], F32)
    nLb = [P.sb(f"nLb{h}", [128, 512], F32) for h in range(4)]
    tmp = [P.sb(f"tmp{i}", [128, 512], F32) for i in range(4)]
    E = [P.sb(f"E{i}", [128, 512], BF16) for i in range(5)]
    rden = P.sb("rden", [64, 512], F32)
    oTh = [P.sb(f"oTh{i}", [64, 512], BF16) for i in range(2)]

    ps_qk = P.ps("ps_qk", [128, 512])
    ps_t = P.ps("ps_t", [128, 8, 128], BF16)
    ps_s = [P.ps(f"ps_s{i}", [128, 512]) for i in range(4)]
    ps_n = P.ps("ps_n", [128, 512])
    ps_m = P.ps("ps_m", [128, 512])
    ps_vf = ps_m[:, 252:512]

    x_v = x.rearrange("(n p) d -> n p d", p=128)
    pair_ctr = 0
    bgw = list(P.io.get("bg_work", []))
    per_it = (len(bgw) + 2 * NQC - 3) // max(2 * NQC - 2, 1)
    for hp in range(2):
        for kc in range(8):
            i = kc % 2
            s.dma("sp", lambda q, i=i, kc=kc, hp=hp: q.dma_start(out=wstage[i][:, 0:512], in_=wA[hp, kc * 128:(kc + 1) * 128, :]),
                  writes=[f"wstage{i}"])
            s.op("pool", lambda e, i=i, kc=kc: e.tensor_copy(out=wA_bf[:, kc, :], in_=wstage[i][:, 0:512]),
                 reads=[f"wstage{i}"], writes=["wA_bf"])
        for kc in range(8):
            i = kc % 2
            s.dma("sp", lambda q, i=i, kc=kc, hp=hp: q.dma_start(out=wstage[i][:, 0:260], in_=wB[hp, kc * 128:(kc + 1) * 128, :]),
                  writes=[f"wstage{i}"])
            s.op("pool", lambda e, i=i, kc=kc: e.tensor_copy(out=wB_bf[:, kc, :], in_=wstage[i][:, 0:260]),
                 reads=[f"wstage{i}"], writes=["wB_bf"])
        s.op("pool", lambda e: e.memset(R[:], 0.0), writes=["R"])
        for qc in range(NQC):
            for i in range(4):
                tg = qc * 4 + i
                par = tg % 2
                s.dma("sp", lambda q, par=par, tg=tg: q.dma_start(out=x_sb[par][:], in_=x_v[tg]), writes=[f"x_sb{par}"])
                norm_transpose_g(P, x_sb[par][:], f"x_sb{par}", xnT, "xnT", i * 128, gl_sb, "gl", xs[par], f"xs{par}",
                                 junk, stat, ps_t, ident)
            for i in range(4):
                tg = qc * 4 + i
                tc_ = slice(i * 128, (i + 1) * 128)
                for kc in range(8):
                    s.op("pe", lambda e, kc=kc, tc_=tc_: e.matmul(ps_qk[:], lhsT=xnT[:, kc, tc_], rhs=wA_bf[:, kc, :],
                                                                  start=(kc == 0), stop=(kc == 7)),
                         reads=["xnT", "wA_bf"], writes=["ps_qk"])
                for kc in range(8):
                    s.op("pe", lambda e, kc=kc, tc_=tc_: e.matmul(ps_vf[:, 0:260], lhsT=xnT[:, kc, tc_], rhs=wB_bf[:, kc, :],
                                                                  start=(kc == 0), stop=(kc == 7)),
                         reads=["xnT", "wB_bf"], writes=["ps_m"])
                s.op("act", lambda e: e.activation(out=sq[:], in_=ps_qk[:], func=AF.Square), reads=["ps_qk"], writes=["sq"])
                s.op("dve", lambda e: e.tensor_reduce(out=ss8[:, 0:8], in_=sq[:].rearrange("p (h d) -> p h d", h=8),
                                                      axis=AX.X, op=ALU.add), reads=["sq"], writes=["ss8a"])
                rms_rstd(s, P, ss8[:, 0:8], ss8[:, 16:24], DH, "ss8a", "ss8c", ss8[:, 8:16], "ss8b")
                s.op("dve", lambda e: e.tensor_tensor(out=t1[:], in0=ps_qk[:].rearrange("p (h d) -> p h d", h=8),
                                                      in1=ss8[:, 16:24].unsqueeze(2).to_broadcast([128, 8, 64]), op=ALU.mult),
                     reads=["ps_qk", "ss8c"], writes=["t1"])
                s.op("pool", lambda e: e.tensor_tensor(out=qkn[:], in0=t1[:], in1=G8[:], op=ALU.mult),
                     reads=["t1", "G8"], writes=["qkn"])
                for pr in range(4):
                    s.op("pe", lambda e, pr=pr: e.transpose(out=ps_t[:, pr, :], in_=qkn[:, 2 * pr:2 * pr + 2, :].rearrange("p h d -> p (h d)"),
                                                            identity=ident[:]), reads=["qkn", "ident"], writes=["ps_t"])
                s.op("act", lambda e, tc_=tc_: e.activation(out=QT[:, :, tc_], in_=ps_t[:, 0:2, :], func=AF.Copy),
                     reads=["ps_t"], writes=["QT"])
                s.op("act", lambda e, tg=tg: e.activation(out=KT[:, :, tg * 128:(tg + 1) * 128], in_=ps_t[:, 2:4, :], func=AF.Copy),
                     reads=["ps_t"], writes=["KT"])
                s.op("act", lambda e, tg=tg: e.activation(out=V[:, tg, :, 0:64], in_=ps_vf[:, 0:256].rearrange("p (h d) -> p h d", h=4), func=AF.Copy),
                     reads=["ps_m"], writes=["V"])
                s.op("dve", lambda e, hp=hp: e.tensor_tensor(out=zf[:, 0:4], in0=ps_vf[:, 256:260], in1=bfl_sb[:, hp * 4:hp * 4 + 4], op=ALU.add),
                     reads=["ps_m", "bfl"], writes=["zf0"])
                s.op("act", lambda e: e.activation(out=zf[:, 4:8], in_=zf[:, 0:4], func=AF.Exp, scale=-1.0), reads=["zf0"], writes=["zf1"])
                s.op("act", lambda e: e.activation(out=zf[:, 8:12], in_=zf[:, 4:8], func=AF.Ln, bias=1.0), reads=["zf1"], writes=["zf2"])
                s.op("pe", lambda e: e.matmul(ps_m[:, 0:4], lhsT=tri[:], rhs=zf[:, 8:12], start=True, stop=True),
                     reads=["tri", "zf2"], writes=["ps_m"])
                s.op("pe", lambda e: e.matmul(ps_m[:, 4:8], lhsT=onesf[:], rhs=zf[:, 8:12], start=True, stop=True),
                     reads=["onesf", "zf2"], writes=["ps_m"])
                s.op("dve", lambda e, tg=tg: e.tensor_tensor(out=Ltok[:, tg, :], in0=ps_m[:, 0:4], in1=R[:], op=ALU.add),
                     reads=["ps_m", "R"], writes=[f"Ltok{tg}"])
                s.op("dve", lambda e: e.tensor_tensor(out=R[:], in0=ps_m[:, 4:8], in1=R[:], op=ALU.add),
                     reads=["ps_m", "R"], writes=["R"])
                s.op("pe", lambda e, tg=tg: e.transpose(out=ps_m[0:4, 16:144], in_=Ltok[:, tg, :], identity=identf[:]),
                     reads=[f"Ltok{tg}", "identf"], writes=["ps_m"])
                s.op("act", lambda e, tc_=tc_: e.activation(out=LT[:, tc_], in_=ps_m[0:4, 16:144], func=AF.Copy),
                     reads=["ps_m"], writes=["LT"])
            for h in range(4):
                s.op("pe", lambda e, h=h: e.matmul(ps_m[:], lhsT=selneg[:, h, :], rhs=LT[:], start=True, stop=True),
                     reads=["selneg", "LT"], writes=["ps_m"])
                s.op("act", lambda e, h=h: e.activation(out=nLb[h][:], in_=ps_m[:], func=AF.Copy), reads=["ps_m"], writes=[f"nLb{h}"])
            for _ in range(per_it):
                if bgw:
                    bgw.pop(0)()
            nkt = 4 * qc + 4
            items = [(h, kt) for h in range(4) for kt in range(nkt)]
            LOOK = 3
            binfo = {}

            def stage_a(ix):
                nonlocal pair_ctr
                h, kt = items[ix]
                pr, hb = h // 2, (h % 2) * 64
                r = kt - 4 * qc
                cs = 128 * r if r > 0 else 0
                cols = slice(cs, 512)
                pi, ti, ei = pair_ctr % 4, pair_ctr % 4, pair_ctr % 5
                pair_ctr += 1
                pss, tm, Et = ps_s[pi], tmp[ti], E[ei]
                binfo[ix] = (Et, ei, cols)
                s.op("pe", lambda e: e.matmul(pss[:, cols], lhsT=KT[hb:hb + 64, pr, kt * 128:(kt + 1) * 128], rhs=QT[hb:hb + 64, pr, cols],
                                              start=True, stop=True), reads=["KT", "QT"], writes=[f"ps_s{pi}"])
                s.op("dve", lambda e: e.tensor_tensor(out=tm[:, cols], in0=pss[:, cols], in1=nLb[h][:, cols], op=ALU.add),
                     reads=[f"ps_s{pi}", f"nLb{h}"], writes=[f"tmp{ti}"])
                if r >= 0:
                    s.op("pool", lambda e: e.affine_select(out=tm[:, cs:cs + 128], in_=tm[:, cs:cs + 128], pattern=[[1, 128]],
                                                           compare_op=ALU.is_ge, fill=fill_reg(e, -30000.0), base=0, channel_multiplier=-1),
                         reads=[f"tmp{ti}"], writes=[f"tmp{ti}"])
                s.op("act", lambda e: e.activation(out=Et[:, cols], in_=tm[:, cols], func=AF.Exp, bias=Ltok[:, kt, h:h + 1]),
                     reads=[f"tmp{ti}", f"Ltok{kt}"], writes=[f"E{ei}"])

            def stage_b(ix):
                h, kt = items[ix]
                Et, ei, cols = binfo.pop(ix)
                first, last, qc_ = (kt == 0), (kt == nkt - 1), qc
                s.op("pe", lambda e: e.matmul(ps_n[:, cols], lhsT=V[:, kt, h, :], rhs=Et[:, cols], start=first, stop=last),
                     reads=[f"E{ei}", "V"], writes=["ps_n"])
                if last:
                    oi = h % 2
                    s.op("act", lambda e: e.activation(out=rden[:], in_=ps_n[64:128, :], func=AF.Copy), reads=["ps_n"], writes=["rden"])
                    s.op("dve", lambda e: e.reciprocal(out=rden[:], in_=rden[:]), reads=["rden"], writes=["rden"])
                    s.op("dve", lambda e: e.tensor_tensor(out=oTh[oi][:], in0=ps_n[0:64, :], in1=rden[:], op=ALU.mult),
                         reads=["ps_n", "rden"], writes=[f"oTh{oi}"])
                    hg = hp * 4 + h
                    s.dma("sp", lambda q: q.dma_start(out=oT_l[hg // 2][(hg % 2) * 64:(hg % 2 + 1) * 64, qc_ * 512:(qc_ + 1) * 512], in_=oTh[oi][:]),
                          reads=[f"oTh{oi}"], final=True)

            for ix in range(len(items) + LOOK):
                if ix < len(items):
                    stage_a(ix)
                if ix >= LOOK:
                    stage_b(ix - LOOK)
    while bgw:
        bgw.pop(0)()
    return P.finish()


def rope_tm(s, eng2, src, src_key, dst, dst_key, nh, cos_ap, sin_ap, tabkey, rt, rtkey):
    cb = cos_ap.unsqueeze(1).to_broadcast([128, nh, 8])
    sb_ = sin_ap.unsqueeze(1).to_broadcast([128, nh, 8])
    n8 = nh * 8

    def v(i):
        return rt[:, i, 0:n8].rearrange("p (h d) -> p h d", h=nh)

    x1 = src[:, :, 0:8]
    x2 = src[:, :, 8:16]
    s.op("dve", lambda e: e.tensor_copy(out=dst, in_=src), reads=[src_key], writes=[dst_key])
    s.op("dve", lambda e: e.tensor_tensor(out=v(0), in0=x1, in1=cb, op=ALU.mult), reads=[src_key, tabkey], writes=[rtkey + "0"])
    s.op("dve", lambda e: e.tensor_tensor(out=v(1), in0=x2, in1=sb_, op=ALU.mult), reads=[src_key, tabkey], writes=[rtkey + "1"])
    s.op("dve", lambda e: e.tensor_tensor(out=dst[:, :, 0:8], in0=v(0), in1=v(1), op=ALU.subtract),
         reads=[rtkey + "0", rtkey + "1"], writes=[dst_key])
    s.op("dve", lambda e: e.tensor_tensor(out=v(2), in0=x2, in1=cb, op=ALU.mult), reads=[src_key, tabkey], writes=[rtkey + "2"])
    s.op("dve", lambda e: e.tensor_tensor(out=v(3), in0=x1, in1=sb_, op=ALU.mult), reads=[src_key, tabkey], writes=[rtkey + "3"])
    s.op("dve", lambda e: e.tensor_tensor(out=dst[:, :, 8:16], in0=v(2), in1=v(3), op=ALU.add),
         reads=[rtkey + "2", rtkey + "3"], writes=[dst_key])


def sincos_tab(P, posf, n, invf_sb, cosT, sinT, name):
    s = P.s
    ang = P.sb(name + "_ang", [128, n, 8], F32)
    ki = P.sb(name + "_ki", [128, n, 8], I32)
    kf = P.sb(name + "_kf", [128, n, 8], F32)
    TWO_PI = 2.0 * np.pi
    for i in range(8):
        s.op("dve", lambda e, i=i: e.tensor_scalar(out=ang[:, :, i], in0=posf, scalar1=invf_sb[:, i:i + 1], scalar2=None, op0=ALU.mult),
             reads=[name + "_posf", "invf"], writes=[name + "_ang"])
    s.op("dve", lambda e: e.tensor_scalar(out=kf[:], in0=ang[:], scalar1=1.0 / TWO_PI, scalar2=None, op0=ALU.mult),
         reads=[name + "_ang"], writes=[name + "_kf"])
    s.op("dve", lambda e: e.tensor_copy(out=ki[:], in_=kf[:]), reads=[name + "_kf"], writes=[name + "_ki"])
    s.op("dve", lambda e: e.tensor_copy(out=kf[:], in_=ki[:]), reads=[name + "_ki"], writes=[name + "_kf"])
    s.op("dve", lambda e: e.scalar_tensor_tensor(out=ang[:], in0=kf[:], scalar=-TWO_PI, in1=ang[:], op0=ALU.mult, op1=ALU.add),
         reads=[name + "_kf", name + "_ang"], writes=[name + "_ang"])
    s.op("dve", lambda e: e.tensor_scalar(out=kf[:], in0=ang[:], scalar1=float(np.pi), scalar2=None, op0=ALU.is_gt),
         reads=[name + "_ang"], writes=[name + "_kf"])
    s.op("dve", lambda e: e.scalar_tensor_tensor(out=ang[:], in0=kf[:], scalar=-TWO_PI, in1=ang[:], op0=ALU.mult, op1=ALU.add),
         reads=[name + "_kf", name + "_ang"], writes=[name + "_ang"])
    s.op("dve", lambda e: e.tensor_scalar(out=kf[:], in0=ang[:], scalar1=-float(np.pi), scalar2=None, op0=ALU.is_lt),
         reads=[name + "_ang"], writes=[name + "_kf"])
    s.op("dve", lambda e: e.scalar_tensor_tensor(out=ang[:], in0=kf[:], scalar=TWO_PI, in1=ang[:], op0=ALU.mult, op1=ALU.add),
         reads=[name + "_kf", name + "_ang"], writes=[name + "_ang"])
    s.op("dve", lambda e: e.tensor_scalar(out=ang[:], in0=ang[:], scalar1=3.1415925, scalar2=-3.1415925, op0=ALU.min, op1=ALU.max),
         reads=[name + "_ang"], writes=[name + "_ang"])
    s.op("act", lambda e: e.activation(out=sinT[:], in_=ang[:], func=AF.Sin), reads=[name + "_ang"], writes=[name + "_tab"])
    s.op("dve", lambda e: e.tensor_scalar(out=kf[:], in0=ang[:], scalar1=-1.0, scalar2=None, op0=ALU.mult),
         reads=[name + "_ang"], writes=[name + "_kf"])
    s.op("dve", lambda e: e.tensor_tensor(out=kf[:], in0=kf[:], in1=ang[:], op=ALU.max),
         reads=[name + "_ang", name + "_kf"], writes=[name + "_kf"])
    s.op("dve", lambda e: e.tensor_scalar(out=kf[:], in0=kf[:], scalar1=-1.0, scalar2=float(np.pi / 2), op0=ALU.mult, op1=ALU.add),
         reads=[name + "_kf"], writes=[name + "_kf"])
    s.op("act", lambda e: e.activation(out=cosT[:], in_=kf[:], func=AF.Sin), reads=[name + "_kf"], writes=[name + "_tab"])


def build_nsa(P, TT):
    nc, s = P.nc, P.s
    NTL = TT // 128
    NQC = TT // 512
    NCB = TT // 16
    NCT = NCB // 128
    NVB = (TT - 32) // 16 + 1
    xnT_all_l = P.io["xnT_all_l"]
    msk = P.din("msk", [128, 2])
    wq = P.din("wq", [D, 512])
    wg = P.din("wg", [D, 24])
    bgl = P.din("bgl", [128, 24])
    qg = P.din("qg", [128, 64])
    kv_all_l = P.io["kv_all_l"]
    pos = P.din("pos", [128, NTL], I32)
    pose = P.din("pose", [128, NCT], I32)
    kgains = P.din("kgains", [128, 3, 64])
    invf = P.din("invf", [128, 8])
    peT = P.din("peT", [128, 32])
    w1 = P.din("w1", [128, 32, 256])
    w2 = P.din("w2", [128, 2, 2, 64])
    oT_l = P.io["oT_l"]

    ident = make_ident(P, BF16, "ident")
    identf = make_ident(P, F32, "identf")
    small = {}
    for nm, ap_, shp, dt in (("msk", msk, [128, 2], F32), ("bgl", bgl, [128, 24], F32), ("qg", qg, [128, 64], F32),
                             ("kgains", kgains, [128, 3, 64], F32), ("invf", invf, [128, 8], F32),
                             ("pos", pos, [128, NTL], I32), ("pose", pose, [128, NCT], I32), ("peT", peT, [128, 32], F32),
                             ("w2", w2, [128, 2, 2, 64], F32)):
        t_ = P.sb(nm + "_sb", shp, dt)
        s.dma("sp", lambda q, t_=t_, ap_=ap_: q.dma_start(out=t_[:], in_=ap_), writes=[nm])
        small[nm] = t_
    msk_sb, bgl_sb, qg_sb, kg_sb, invf_sb = small["msk"], small["bgl"], small["qg"], small["kgains"], small["invf"]
    G8 = P.sb("G8", [128, 8, 64], F32)
    for h in range(8):
        s.op("pool", lambda e, h=h: e.tensor_scalar(out=G8[:, h, :], in0=qg_sb[:], scalar1=0.125, scalar2=None, op0=ALU.mult),
             reads=["qg"], writes=["G8"])
    ones_bf = P.sb("ones_bf", [128, 128], BF16)
    s.op("pool", lambda e: e.memset(ones_bf[:], 1.0), writes=["ones_bf"])
    posf = P.sb("posf", [128, NTL], F32)
    s.op("dve", lambda e: e.tensor_copy(out=posf[:], in_=small["pos"][:]), reads=["pos"], writes=["tq_posf"])
    cosT = P.sb("cosT", [128, NTL, 8], F32)
    sinT = P.sb("sinT", [128, NTL, 8], F32)
    sincos_tab(P, posf[:], NTL, invf_sb, cosT, sinT, "tq")
    posef = P.sb("posef", [128, NCT], F32)
    s.op("dve", lambda e: e.tensor_copy(out=posef[:], in_=small["pose"][:]), reads=["pose"], writes=["te_posf"])
    cosE = P.sb("cosE", [128, NCT, 8], F32)
    sinE = P.sb("sinE", [128, NCT, 8], F32)
    sincos_tab(P, posef[:], NCT, invf_sb, cosE, sinE, "te")
    Gx = P.sb("Gx", [128, TT], BF16)
    s.op("pool", lambda e: e.memset(Gx[:], 1.0), writes=["Gx"])
    s.op("pool", lambda e: e.affine_select(out=Gx[:], in_=Gx[:], pattern=[[1, TT]], compare_op=ALU.is_ge, fill=0.0,
                                           base=0, channel_multiplier=-64), reads=["Gx"], writes=["Gx"])
    s.op("pool", lambda e: e.affine_select(out=Gx[:], in_=Gx[:], pattern=[[-1, TT]], compare_op=ALU.is_ge, fill=0.0,
                                           base=63, channel_multiplier=64), reads=["Gx"], writes=["Gx"])
    ovl = P.sb("ovl", [128, NCT, 128], BF16)
    oa = P.sb("oa", [128, NCT, 128], F32)
    ob = P.sb("ob", [128, NCT, 128], F32)
    oc = P.sb("oc", [128, NCT, 128], F32)
    s.op("pool", lambda e: e.iota(out=oa[:], pattern=[[2048, NCT], [0, 128]], base=0, channel_multiplier=16,
                                  allow_small_or_imprecise_dtypes=True), writes=["oa"])
    s.op("pool", lambda e: e.iota(out=ob[:], pattern=[[0, NCT], [64, 128]], base=0, channel_multiplier=0,
                                  allow_small_or_imprecise_dtypes=True), writes=["ob"])
    s.op("dve", lambda e: e.tensor_tensor(out=oc[:], in0=oa[:], in1=ob[:], op=ALU.max), reads=["oa", "ob"], writes=["oc"])
    s.op("dve", lambda e: e.tensor_scalar(out=oa[:], in0=oa[:], scalar1=32.0, scalar2=None, op0=ALU.add), reads=["oa"], writes=["oa"])
    s.op("dve", lambda e: e.tensor_scalar(out=ob[:], in0=ob[:], scalar1=64.0, scalar2=None, op0=ALU.add), reads=["ob"], writes=["ob"])
    s.op("dve", lambda e: e.tensor_tensor(out=oa[:], in0=oa[:], in1=ob[:], op=ALU.min), reads=["oa", "ob"], writes=["oa"])
    s.op("dve", lambda e: e.tensor_tensor(out=oa[:], in0=oa[:], in1=oc[:], op=ALU.subtract), reads=["oa", "oc"], writes=["oa"])
    s.op("dve", lambda e: e.tensor_scalar(out=ovl[:], in0=oa[:], scalar1=0.0, scalar2=1.0 / 32, op0=ALU.max, op1=ALU.mult),
         reads=["oa"], writes=["ovl"])
    selg = P.sb("selg", [24, 24, 64], F32)
    s.op("pool", lambda e: e.memset(selg[:], 1.0), writes=["selg"])
    s.op("pool", lambda e: e.affine_select(out=selg[:], in_=selg[:], pattern=[[1, 24], [0, 64]], compare_op=ALU.is_equal,
                                           fill=0.0, base=0, channel_multiplier=-1), reads=["selg"], writes=["selg"])

    ksT = P.sb("ksT", [128, TT], BF16)
    kwT = P.sb("kwT", [128, TT], BF16)
    VS = P.sb("VS", [128, NTL, 64], BF16)
    VW = P.sb("VW", [128, NTL, 64], BF16)
    kcT = P.sb("kcT", [128, NCB], BF16)
    VC = P.sb("VC", [128, NCT, 64], BF16)
    shared2 = P.sb("shared2", [128, 16384], BF16)
    rawT = shared2[:, 0:TT]
    w1_sb = shared2[:, 8192:16384].rearrange("p (l n) -> p l n", l=32)
    w2_sb = P.sb("w2_bf", [128, 2, 2, 64], BF16)
    peT_bf = P.sb("peT_bf", [128, 32], BF16)
    hidT = P.sb("hidT", [128, 2, NCB], BF16)
    wstage = [P.sb(f"wstage{i}", [128, 1024], F32) for i in range(2)]
    kv_sb = [P.sb(f"kv_sb{i}", [128, 6, 64], F32) for i in range(2)]
    kv_b = [P.sb(f"kv_b{i}", [128, 6, 64], F32) for i in range(2)]
    xs = [shared2[:, 4096 + 1024 * i:4096 + 1024 * (i + 1)] for i in range(2)]
    junk = shared2[:, 6144:7168]
    stat = P.sb("stat", [128, 8], F32)
    xnT = shared2[:, 0:4096].rearrange("p (k t) -> p k t", k=8)
    wq_bf = P.sb("wq_bf", [128, 8, 512], BF16)
    wg_bf = P.sb("wg_bf", [128, 8, 24], BF16)
    sq = P.sb("sq", [128, 512], F32)
    ss8 = P.sb("ss8", [128, 24], F32)
    t1 = P.sb("t1", [128, 8, 64], F32)
    t2 = P.sb("t2", [128, 8, 64], F32)
    rt = P.sb("rt", [128, 4, 64], F32)
    qkn = P.sb("qkn", [128, 8, 64], BF16)
    kdup = P.sb("kdup", [128, 2, 2, 64], BF16)
    QT = shared2[:, 7168:9216].rearrange("p (k t) -> p k t", k=4)
    zg = P.sb("zg", [128, 3, 24], F32)
    gT = P.sb("gT", [24, 512], F32)
    Ebuf = [shared2[:, 9216 + 512 * i:9216 + 512 * (i + 1)] for i in range(4)]
    Em = [shared2[:, 11264 + 512 * i:11264 + 512 * (i + 1)] for i in range(3)]
    rdenb = P.sb("rdenb", [128, 512], F32)
    rden = P.sb("rden", [64, 512], F32)
    fgate = P.sb("fgate", [64, 512], F32)
    contrib = P.sb("contrib", [64, 512], F32)
    shared1 = P.sb("shared1", [128, 8, 512], F32)
    oacc = [shared1[0:64, h, :] for h in range(8)]
    oTh = [P.sb(f"oTh{i}", [64, 512], BF16) for i in range(2)]
    impT = P.sb("impT", [128, 512], F32)
    sc = P.sb("sc", [128, 128], F32)
    sc2 = P.sb("sc2", [128, 128], F32)
    m8 = P.sb("m8", [128, 16], F32)
    selb = P.sb("selb", [128, 128], BF16)
    selT = P.sb("selT", [128, 512], BF16)
    gx = shared1
    biasH = P.sb("biasH", [128, 4], F32)

    ps_s = [P.ps(f"ps_s{i}", [128, 512]) for i in range(3)]
    ps_M = [P.ps(f"ps_M{i}", [128, 512]) for i in range(2)]
    ps_t = ps_s[2][:].bitcast(BF16).rearrange("p (k t) -> p k t", k=8)
    ps_n = P.ps("ps_n", [64, 512])
    ps_d = P.ps("ps_d", [128, 512])
    ps_m = P.ps("ps_m", [128, 512])

    for kc in range(8):
        i = kc % 2
        s.dma("sp", lambda q, i=i, kc=kc: q.dma_start(out=wstage[i][:, 0:512], in_=wq[kc * 128:(kc + 1) * 128, :]), writes=[f"wstage{i}"])
        s.op("pool", lambda e, i=i, kc=kc: e.tensor_copy(out=wq_bf[:, kc, :], in_=wstage[i][:, 0:512]), reads=[f"wstage{i}"], writes=["wq_bf"])
    for kc in range(8):
        i = kc % 2
        s.dma("sp", lambda q, i=i, kc=kc: q.dma_start(out=wstage[i][:, 0:24], in_=wg[kc * 128:(kc + 1) * 128, :]), writes=[f"wstage{i}"])
        s.op("pool", lambda e, i=i, kc=kc: e.tensor_copy(out=wg_bf[:, kc, :], in_=wstage[i][:, 0:24]), reads=[f"wstage{i}"], writes=["wg_bf"])
    for l4 in range(8):
        i = l4 % 2
        s.dma("sp", lambda q, i=i, l4=l4: q.dma_start(out=wstage[i][:, 0:1024].rearrange("p (l n) -> p l n", l=4), in_=w1[:, l4 * 4:(l4 + 1) * 4, :]),
              writes=[f"wstage{i}"])
        s.op("pool", lambda e, i=i, l4=l4: e.tensor_copy(out=w1_sb[:, l4 * 4:(l4 + 1) * 4, :],
                                                       in_=wstage[i][:, 0:1024].rearrange("p (l n) -> p l n", l=4)),
             reads=[f"wstage{i}"], writes=["w1"])
    s.op("pool", lambda e: e.tensor_copy(out=w2_sb[:], in_=small["w2"][:]), reads=["w2"], writes=["w2b"])
    s.op("pool", lambda e: e.tensor_copy(out=peT_bf[:], in_=small["peT"][:]), reads=["peT"], writes=["peTb"])
    s.op("pool", lambda e: e.memset(hidT[:], 0.0), writes=["hidT"])

    def head_norm(src_ps, src_key, nslots, gain_ap, gain_key, dst, dst_key):
        w = nslots * 64
        s.op("act", lambda e: e.activation(out=sq[:, 0:w], in_=src_ps, func=AF.Square), reads=[src_key], writes=["sq"])
        s.op("dve", lambda e: e.tensor_reduce(out=ss8[:, 0:nslots], in_=sq[:, 0:w].rearrange("p (h d) -> p h d", h=nslots),
                                              axis=AX.X, op=ALU.add), reads=["sq"], writes=["ss8a"])
        rms_rstd(s, P, ss8[:, 0:nslots], ss8[:, 16:16 + nslots], DH, "ss8a", "ss8c", ss8[:, 8:8 + nslots], "ss8b")
        s.op("dve", lambda e: e.tensor_tensor(out=t1[:, 0:nslots, :], in0=src_ps.rearrange("p (h d) -> p h d", h=nslots),
                                              in1=ss8[:, 16:16 + nslots].unsqueeze(2).to_broadcast([128, nslots, 64]), op=ALU.mult),
             reads=[src_key, "ss8c"], writes=["t1"])
        s.op("pool", lambda e: e.tensor_tensor(out=dst, in0=t1[:, 0:nslots, :], in1=gain_ap, op=ALU.mult),
             reads=["t1", gain_key], writes=[dst_key])

    HT = NTL // 2
    QT_ = HT // 4

    def kv_src(g, rk, tl):
        r0_ = rk * (QT_ * 128) + (tl % QT_) * 128
        return kv_all_l[g * 4 + tl // QT_][r0_:r0_ + 128, :]
    for tg in range(NTL):
        par = tg % 2
        kvt = kv_sb[par]
        rk, tl = tg // HT, tg % HT
        kvb = kv_b[par]
        s.dma("sp", lambda q, kvt=kvt, rk=rk, tl=tl: q.dma_start(out=kvt[:].rearrange("p a d -> p (a d)"), in_=kv_src(0, rk, tl)),
              writes=[f"kv_sb{par}"])
        s.dma("sp", lambda q, kvb=kvb, rk=rk, tl=tl: q.dma_start(out=kvb[:].rearrange("p a d -> p (a d)"), in_=kv_src(1, rk, tl)),
              writes=[f"kv_b{par}"])
        s.op("dve", lambda e, kvt=kvt: e.tensor_scalar(out=kvt[:], in0=kvt[:], scalar1=msk_sb[:, 0:1], scalar2=None, op0=ALU.mult),
             reads=[f"kv_sb{par}", "msk"], writes=[f"kv_sb{par}"])
        s.op("dve", lambda e, kvt=kvt, kvb=kvb: e.scalar_tensor_tensor(out=kvt[:], in0=kvb[:], scalar=msk_sb[:, 1:2], in1=kvt[:],
                                                                     op0=ALU.mult, op1=ALU.add),
             reads=[f"kv_sb{par}", f"kv_b{par}", "msk"], writes=[f"kv_sb{par}"])
        for wi, part, gi in ((0, 2, 1), (1, 4, 2)):
            head_norm(kvt[:, part, :], f"kv_sb{par}", 1, kg_sb[:, gi:gi + 1, :], "kgains", t2[:, 0:1, :], "t2")
            rope_tm(s, None, t2[:, 0:1, :], "t2", kdup[:, wi, 0:1, :], f"kdup{wi}", 1, cosT[:, tg, :], sinT[:, tg, :], "tq_tab", rt, "rt")
            s.op("pool", lambda e, wi=wi: e.tensor_copy(out=kdup[:, wi, 1, :], in_=kdup[:, wi, 0, :]), reads=[f"kdup{wi}"], writes=[f"kdup{wi}"])
            s.op("pe", lambda e, wi=wi: e.transpose(out=ps_t[:, wi, :], in_=kdup[:, wi, :, :].rearrange("p a d -> p (a d)"), identity=ident[:]),
                 reads=[f"kdup{wi}", "ident"], writes=["ps_s2"])
        s.op("act", lambda e, tg=tg: e.activation(out=ksT[:, tg * 128:(tg + 1) * 128], in_=ps_t[:, 0, :], func=AF.Copy), reads=["ps_s2"], writes=["ksT"])
        s.op("act", lambda e, tg=tg: e.activation(out=kwT[:, tg * 128:(tg + 1) * 128], in_=ps_t[:, 1, :], func=AF.Copy), reads=["ps_s2"], writes=["kwT"])
        s.op("pool", lambda e, kvt=kvt, tg=tg: e.tensor_copy(out=VS[:, tg, :], in_=kvt[:, 3, :]), reads=[f"kv_sb{par}"], writes=["VS"])
        s.op("pool", lambda e, kvt=kvt, tg=tg: e.tensor_copy(out=VW[:, tg, :], in_=kvt[:, 5, :]), reads=[f"kv_sb{par}"], writes=["VW"])
        s.op("pool", lambda e, kvt=kvt: e.tensor_copy(out=qkn[:, 0:2, :], in_=kvt[:, 0:2, :]), reads=[f"kv_sb{par}"], writes=["qkn"])
        s.op("pe", lambda e: e.transpose(out=ps_t[:, 2, :], in_=qkn[:, 0:2, :].rearrange("p a d -> p (a d)"), identity=ident[:]),
             reads=["qkn", "ident"], writes=["ps_s2"])
        s.op("act", lambda e, tg=tg: e.activation(out=rawT[:, tg * 128:(tg + 1) * 128], in_=ps_t[:, 2, :], func=AF.Copy), reads=["ps_s2"], writes=["rawT"])

    for which in range(2):
        pb = which * 64
        for hc in range(2):
            for l in range(32):
                s.op("pe", lambda e, l=l, hc=hc, pb=pb, which=which: e.matmul(
                    ps_m[:, which * 2 + hc:which * 2 + hc + 1], lhsT=w1_sb[pb:pb + 64, l, hc * 128:(hc + 1) * 128],
                    rhs=peT_bf[pb:pb + 64, l:l + 1], start=(l == 0), stop=(l == 31)), reads=["w1", "peTb"], writes=["ps_m"])
            s.op("act", lambda e, which=which, hc=hc: e.activation(out=biasH[:, which * 2 + hc:which * 2 + hc + 1],
                                                                   in_=ps_m[:, which * 2 + hc:which * 2 + hc + 1], func=AF.Copy),
                 reads=["ps_m"], writes=["biasH"])
        for hc in range(2):
            pss = ps_s[hc]
            for l in range(32):
                s.op("pe", lambda e, l=l, hc=hc, pb=pb, pss=pss: e.matmul(
                    pss[:, 0:NVB], lhsT=w1_sb[pb:pb + 64, l, hc * 128:(hc + 1) * 128],
                    rhs=rawT[pb:pb + 64, l:l + 16 * (NVB - 1) + 1:16], start=(l == 0), stop=(l == 31)),
                    reads=["w1", "rawT"], writes=[f"ps_s{hc}"])
            xg, x2g, ug, thg = gx[:, 0, 0:NVB], gx[:, 1, 0:NVB], gx[:, 2, 0:NVB], gx[:, 3, 0:NVB]
            s.op("act", lambda e, pss=pss, which=which, hc=hc, xg=xg: e.activation(
                out=xg, in_=pss[:, 0:NVB], func=AF.Identity, bias=biasH[:, which * 2 + hc:which * 2 + hc + 1]),
                reads=[f"ps_s{hc}", "biasH"], writes=["gx0"])
            s.op("dve", lambda e, xg=xg, x2g=x2g: e.tensor_tensor(out=x2g, in0=xg, in1=xg, op=ALU.mult), reads=["gx0"], writes=["gx1"])
            s.op("dve", lambda e, x2g=x2g: e.tensor_scalar(out=x2g, in0=x2g, scalar1=0.044715, scalar2=1.0, op0=ALU.mult, op1=ALU.add),
                 reads=["gx1"], writes=["gx1"])
            s.op("dve", lambda e, xg=xg, x2g=x2g, ug=ug: e.tensor_tensor(out=ug, in0=x2g, in1=xg, op=ALU.mult), reads=["gx0", "gx1"], writes=["gx2"])
            s.op("act", lambda e, ug=ug, thg=thg: e.activation(out=thg, in_=ug, func=AF.Tanh, scale=0.7978845608028654), reads=["gx2"], writes=["gx3"])
            s.op("dve", lambda e, thg=thg: e.tensor_scalar(out=thg, in0=thg, scalar1=0.5, scalar2=0.5, op0=ALU.mult, op1=ALU.add),
                 reads=["gx3"], writes=["gx3"])
            s.op("dve", lambda e, thg=thg, xg=xg, hc=hc: e.tensor_tensor(out=hidT[:, hc, 0:NVB], in0=thg, in1=xg, op=ALU.mult),
                 reads=["gx3", "gx0"], writes=["hidT"])
        for ct in range(NCT):
            for hc in range(2):
                s.op("pe", lambda e, ct=ct, hc=hc, which=which: e.matmul(
                    ps_m[:, 64:128], lhsT=hidT[:, hc, ct * 128:(ct + 1) * 128], rhs=w2_sb[:, which, hc, :],
                    start=(hc == 0), stop=(hc == 1)), reads=["hidT", "w2b"], writes=["ps_m"])
            if which == 0:
                head_norm(ps_m[:, 64:128], "ps_m", 1, kg_sb[:, 0:1, :], "kgains", t2[:, 0:1, :], "t2")
                rope_tm(s, None, t2[:, 0:1, :], "t2", kdup[:, 0, 0:1, :], "kdup0", 1, cosE[:, ct, :], sinE[:, ct, :], "te_tab", rt, "rt")
                s.op("pool", lambda e: e.tensor_copy(out=kdup[:, 0, 1, :], in_=kdup[:, 0, 0, :]), reads=["kdup0"], writes=["kdup0"])
                s.op("pe", lambda e: e.transpose(out=ps_t[:, 0, :], in_=kdup[:, 0, :, :].rearrange("p a d -> p (a d)"), identity=ident[:]),
                     reads=["kdup0", "ident"], writes=["ps_s2"])
                s.op("act", lambda e, ct=ct: e.activation(out=kcT[:, ct * 128:(ct + 1) * 128], in_=ps_t[:, 0, :], func=AF.Copy),
                     reads=["ps_s2"], writes=["kcT"])
            else:
                s.op("act", lambda e, ct=ct: e.activation(out=VC[:, ct, :], in_=ps_m[:, 64:128], func=AF.Copy), reads=["ps_m"], writes=["VC"])

    s.op("pool", lambda e: e.memset(biasH[:, 0:1], 0.0), writes=["biasH", "rawT", "w1", "gx0", "gx1", "gx2", "gx3", "xnT", "xs0", "xs1", "junk", "QT"]
         + [f"E{i}" for i in range(4)] + [f"Em{i}" for i in range(3)] + [f"oacc{h}" for h in range(8)])
    pc = [0]

    binfo = {}

    def attn_a(ix, h, kT, ktile, kkey, cols, mask_fn, sel_kt=None):
        pr, hb = h // 2, (h % 2) * 64
        pi = pc[0] % 3
        mi = pc[0] % 2
        ei = pc[0] % 3
        pc[0] += 1
        pss, Et = ps_s[pi], Em[ei]
        binfo[ix] = (Et, ei)
        s.op("pe", lambda e: e.matmul(pss[:, cols], lhsT=kT[hb:hb + 64, ktile * 128:(ktile + 1) * 128], rhs=QT[hb:hb + 64, pr, cols],
                                      start=True, stop=True), reads=[kkey, "QT"], writes=[f"ps_s{pi}"])
        s.op("act", lambda e: e.activation(out=Et[:, cols], in_=pss[:, cols], func=AF.Exp), reads=[f"ps_s{pi}"], writes=[f"Em{ei}"])
        if sel_kt is not None:
            psM = ps_M[mi]
            s.op("pe", lambda e: e.matmul(psM[:, cols], lhsT=Gx[:, sel_kt * 128:(sel_kt + 1) * 128], rhs=selT[:, cols], start=True, stop=True),
                 reads=["Gx", "selT"], writes=[f"ps_M{mi}"])
            s.op("dve", lambda e: e.tensor_tensor(out=Et[:, cols], in0=Et[:, cols], in1=psM[:, cols], op=ALU.mult),
                 reads=[f"Em{ei}", f"ps_M{mi}"], writes=[f"Em{ei}"])
        if mask_fn is not None:
            mask_fn(Et, f"Em{ei}")

    def attn_b(ix, ktile, Vt, vkey, cols, first, last):
        Et, ei = binfo.pop(ix)
        s.op("pe", lambda e: e.matmul(ps_n[:, cols], lhsT=Vt[:, ktile, :], rhs=Et[:, cols], start=first, stop=last),
             reads=[f"Em{ei}", vkey], writes=["ps_n"])
        s.op("pe", lambda e: e.matmul(ps_d[0:64, cols], lhsT=ones_bf[:, 0:64], rhs=Et[:, cols], start=first, stop=last),
             reads=[f"Em{ei}", "ones_bf"], writes=["ps_d"])

    def finish_branch(h, br, first_branch, den_ap, den_key):
        idx = br * 8 + h
        s.op("dve", lambda e: e.tensor_scalar(out=rden[:], in0=den_ap, scalar1=1e-30, scalar2=None, op0=ALU.max), reads=[den_key], writes=["rden"])
        s.op("dve", lambda e: e.reciprocal(out=rden[:], in_=rden[:]), reads=["rden"], writes=["rden"])
        s.op("pe", lambda e: e.matmul(ps_m[0:64, :], lhsT=selg[:, idx, :], rhs=gT[:], start=True, stop=True), reads=["selg", "gT"], writes=["ps_m"])
        s.op("dve", lambda e: e.tensor_tensor(out=fgate[:], in0=ps_m[0:64, :], in1=rden[:], op=ALU.mult), reads=["ps_m", "rden"], writes=["fgate"])
        if first_branch:
            s.op("dve", lambda e: e.tensor_tensor(out=oacc[h][:], in0=ps_n[:], in1=fgate[:], op=ALU.mult), reads=["ps_n", "fgate"], writes=[f"oacc{h}"])
        else:
            s.op("dve", lambda e: e.tensor_tensor(out=contrib[:], in0=ps_n[:], in1=fgate[:], op=ALU.mult), reads=["ps_n", "fgate"], writes=["contrib"])
            s.op("pool", lambda e: e.tensor_tensor(out=oacc[h][:], in0=oacc[h][:], in1=contrib[:], op=ALU.add),
                 reads=[f"oacc{h}", "contrib"], writes=[f"oacc{h}"])

    def causal_mask(cs):
        def f(Et, ekey):
            s.op("pool", lambda e: e.affine_select(out=Et[:, cs:cs + 128], in_=Et[:, cs:cs + 128], pattern=[[1, 128]], compare_op=ALU.is_ge,
                                                   fill=0.0, base=0, channel_multiplier=-1), reads=[ekey], writes=[ekey])
        return f

    def winlow_mask(cs):
        def f(Et, ekey):
            s.op("pool", lambda e: e.affine_select(out=Et[:, cs:cs + 128], in_=Et[:, cs:cs + 128], pattern=[[-1, 128]], compare_op=ALU.is_ge,
                                                   fill=0.0, base=-1, channel_multiplier=1), reads=[ekey], writes=[ekey])
        return f

    for qc in range(NQC):
        t0 = qc * 512
        qh = NQC // 2
        for kc in range(8):
            rb = (qc // qh) * 256 + (kc % 2) * 128
            s.dma("sp", lambda q, qc=qc, qh=qh, kc=kc, rb=rb: q.dma_start(
                out=xnT[:, kc, :], in_=xnT_all_l[kc // 2][rb:rb + 128, (qc % qh) * 512:(qc % qh + 1) * 512]), writes=["xnT"])
        for i in range(4):
            tg = qc * 4 + i
            tc_ = slice(i * 128, (i + 1) * 128)
            for kc in range(8):
                s.op("pe", lambda e, kc=kc, tc_=tc_: e.matmul(ps_s[0][:], lhsT=xnT[:, kc, tc_], rhs=wq_bf[:, kc, :], start=(kc == 0), stop=(kc == 7)),
                     reads=["xnT", "wq_bf"], writes=["ps_s0"])
            for kc in range(8):
                s.op("pe", lambda e, kc=kc, tc_=tc_: e.matmul(ps_s[1][:, 0:24], lhsT=xnT[:, kc, tc_], rhs=wg_bf[:, kc, :], start=(kc == 0), stop=(kc == 7)),
                     reads=["xnT", "wg_bf"], writes=["ps_s1"])
            head_norm(ps_s[0][:], "ps_s0", 8, G8[:], "G8", t2[:], "t2")
            rope_tm(s, None, t2[:], "t2", qkn[:], "qkn", 8, cosT[:, tg, :], sinT[:, tg, :], "tq_tab", rt, "rt")
            for pr in range(4):
                s.op("pe", lambda e, pr=pr: e.transpose(out=ps_t[:, pr, :], in_=qkn[:, 2 * pr:2 * pr + 2, :].rearrange("p h d -> p (h d)"),
                                                        identity=ident[:]), reads=["qkn", "ident"], writes=["ps_s2"])
            s.op("act", lambda e, tc_=tc_: e.activation(out=QT[:, :, tc_], in_=ps_t[:, 0:4, :], func=AF.Copy), reads=["ps_s2"], writes=["QT"])
            s.op("dve", lambda e: e.tensor_tensor(out=zg[:, 0, :], in0=ps_s[1][:, 0:24], in1=bgl_sb[:], op=ALU.add), reads=["ps_s1", "bgl"], writes=["zg0"])
            s.op("act", lambda e: e.activation(out=zg[:, 1, :], in_=zg[:, 0, :], func=AF.Exp, scale=-1.0), reads=["zg0"], writes=["zg1"])
            s.op("dve", lambda e: e.tensor_scalar(out=zg[:, 1, :], in0=zg[:, 1, :], scalar1=1.0, scalar2=None, op0=ALU.add), reads=["zg1"], writes=["zg1"])
            s.op("dve", lambda e: e.reciprocal(out=zg[:, 2, :], in_=zg[:, 1, :]), reads=["zg1"], writes=["zg2"])
            s.op("pe", lambda e: e.transpose(out=ps_m[0:24, 0:128], in_=zg[:, 2, :], identity=identf[:]), reads=["zg2", "identf"], writes=["ps_m"])
            s.op("act", lambda e, tc_=tc_: e.activation(out=gT[:, tc_], in_=ps_m[0:24, 0:128], func=AF.Copy), reads=["ps_m"], writes=["gT"])
        kts = [kt for kt in range(NCT) if 2048 * kt + 31 <= t0 + 511]
        nimp = len(kts) * 8
        impi = 0
        for h in range(8):
            pr, hb = h // 2, (h % 2) * 64
            for ii, kt in enumerate(kts):
                pi = pc[0] % 2
                pc[0] += 1
                pss, Et = ps_s[pi], Ebuf[ii]
                s.op("pe", lambda e, pss=pss, kt=kt, pr=pr, hb=hb: e.matmul(pss[:], lhsT=kcT[hb:hb + 64, kt * 128:(kt + 1) * 128], rhs=QT[hb:hb + 64, pr, :],
                                                                        start=True, stop=True), reads=["kcT", "QT"], writes=[f"ps_s{pi}"])
                s.op("act", lambda e, pss=pss, Et=Et: e.activation(out=Et[:], in_=pss[:], func=AF.Exp), reads=[f"ps_s{pi}"], writes=[f"E{ii}"])
                if not (2048 * kt + 2063 <= t0):
                    s.op("pool", lambda e, Et=Et, kt=kt, t0=t0: e.affine_select(out=Et[:], in_=Et[:], pattern=[[1, 512]], compare_op=ALU.is_ge, fill=0.0,
                                                                       base=t0 - 2048 * kt - 31, channel_multiplier=-16),
                         reads=[f"E{ii}"], writes=[f"E{ii}"])
                s.op("pe", lambda e, Et=Et, kt=kt, ii=ii, nk=len(kts): e.matmul(ps_n[:], lhsT=VC[:, kt, :], rhs=Et[:], start=(ii == 0), stop=(ii == nk - 1)),
                     reads=[f"E{ii}", "VC"], writes=["ps_n"])
                s.op("pe", lambda e, Et=Et, ii=ii, nk=len(kts): e.matmul(ps_d[:], lhsT=ones_bf[:], rhs=Et[:], start=(ii == 0), stop=(ii == nk - 1)),
                     reads=[f"E{ii}", "ones_bf"], writes=["ps_d"])
            s.op("dve", lambda e: e.tensor_scalar(out=rdenb[:], in0=ps_d[:], scalar1=1e-30, scalar2=None, op0=ALU.max), reads=["ps_d"], writes=["rdenb"])
            s.op("dve", lambda e: e.reciprocal(out=rdenb[:], in_=rdenb[:]), reads=["rdenb"], writes=["rdenb"])
            for ii, kt in enumerate(kts):
                Et = Ebuf[ii]
                s.op("dve", lambda e, Et=Et: e.tensor_tensor(out=Et[:], in0=Et[:], in1=rdenb[:], op=ALU.mult), reads=[f"E{ii}", "rdenb"], writes=[f"E{ii}"])
                s.op("pe", lambda e, Et=Et, kt=kt, impi=impi, nimp=nimp: e.matmul(ps_M[0][:], lhsT=ovl[:, kt, :], rhs=Et[:], start=(impi == 0), stop=(impi == nimp - 1)),
                     reads=[f"E{ii}", "ovl"], writes=["ps_M0"])
                impi += 1
            finish_branch(h, 0, True, ps_d[0:64, :], "ps_d")
        s.op("act", lambda e: e.activation(out=impT[:], in_=ps_M[0][:], func=AF.Copy), reads=["ps_M0"], writes=["impT"])
        for i in range(4):
            tb = t0 + 128 * i
            s.op("pe", lambda e, i=i: e.transpose(out=ps_m[:, 128:256], in_=impT[:, i * 128:(i + 1) * 128], identity=identf[:]),
                 reads=["impT", "identf"], writes=["ps_m"])
            s.op("act", lambda e: e.activation(out=sc[:], in_=ps_m[:, 128:256], func=AF.Copy), reads=["ps_m"], writes=["sc"])
            s.op("pool", lambda e, tb=tb: e.affine_select(out=sc[:], in_=sc[:], pattern=[[-64, 128]], compare_op=ALU.is_ge,
                                                          fill=fill_reg(e, 1e6), base=tb - 128, channel_multiplier=1), reads=["sc"], writes=["sc"])
            s.op("pool", lambda e, tb=tb: e.affine_select(out=sc[:], in_=sc[:], pattern=[[-64, 128]], compare_op=ALU.is_ge,
                                                          fill=fill_reg(e, -1e30), base=tb, channel_multiplier=1), reads=["sc"], writes=["sc"])
            s.op("pool", lambda e: e.memset(sc[:, 0:1], 1e6), reads=["sc"], writes=["sc"])
            s.op("dve", lambda e: e.max(out=m8[:, 0:8], in_=sc[:]), reads=["sc"], writes=["m8a"])
            s.op("dve", lambda e: e.match_replace(out=sc2[:], in_to_replace=m8[:, 0:8], in_values=sc[:], imm_value=-3e38),
                 reads=["sc", "m8a"], writes=["sc2"])
            s.op("dve", lambda e: e.max(out=m8[:, 8:16], in_=sc2[:]), reads=["sc2"], writes=["m8b"])
            s.op("dve", lambda e: e.tensor_scalar(out=selb[:], in0=sc[:], scalar1=m8[:, 15:16], scalar2=None, op0=ALU.is_ge),
                 reads=["sc", "m8b"], writes=["selb"])
            s.op("pe", lambda e: e.transpose(out=ps_t[:, 0, :], in_=selb[:], identity=ident[:]), reads=["selb", "ident"], writes=["ps_s2"])
            s.op("act", lambda e, i=i: e.activation(out=selT[:, i * 128:(i + 1) * 128], in_=ps_t[:, 0, :], func=AF.Copy), reads=["ps_s2"], writes=["selT"])
        items = []
        nkt = 4 * qc + 4
        for h in range(8):
            for kt in range(nkt):
                r = kt - 4 * qc
                cs = 128 * r if r > 0 else 0
                items.append(("pair", h, ksT, "ksT", VS, "VS", kt, slice(cs, 512), kt == 0, kt == nkt - 1,
                              causal_mask(cs) if r >= 0 else None, kt))
            items.append(("fin", h, 1))
            kt_lo = max(0, 4 * qc - 4)
            kt_first = 4 * qc - 1 if qc >= 1 else 0
            worder = [kt_first] + [k_ for k_ in range(kt_lo, nkt) if k_ != kt_first]
            for wi_, kt in enumerate(worder):
                r = kt - 4 * qc
                if r >= 0:
                    cs = 128 * r
                    cols, mf = slice(cs, 512), causal_mask(cs)
                else:
                    ce = 128 * (r + 5)
                    cols, mf = slice(0, ce), winlow_mask(ce - 128)
                items.append(("pair", h, kwT, "kwT", VW, "VW", kt, cols, wi_ == 0, wi_ == len(worder) - 1, mf, None))
            items.append(("fin", h, 2))
            items.append(("out", h, qc))
        LOOK = 2

        def do_a(ix):
            it = items[ix]
            if it[0] == "pair":
                _, h, kT, kkey, Vt, vkey, kt, cols, first, last, mf, selkt = it
                attn_a(ix, h, kT, kt, kkey, cols, mf, sel_kt=selkt)

        def do_b(ix):
            it = items[ix]
            if it[0] == "pair":
                _, h, kT, kkey, Vt, vkey, kt, cols, first, last, mf, selkt = it
                attn_b(ix, kt, Vt, vkey, cols, first, last)
            elif it[0] == "fin":
                finish_branch(it[1], it[2], False, ps_d[0:64, :], "ps_d")
            else:
                h, qc_ = it[1], it[2]
                oi = h % 2
                s.op("act", lambda e: e.activation(out=oTh[oi][:], in_=oacc[h][:], func=AF.Copy), reads=[f"oacc{h}"], writes=[f"oTh{oi}"])
                s.dma("sp", lambda q: q.dma_start(out=oT_l[h // 2][(h % 2) * 64:(h % 2 + 1) * 64, qc_ * 512:(qc_ + 1) * 512], in_=oTh[oi][:]),
                      reads=[f"oTh{oi}"], final=True)

        for ix in range(len(items) + LOOK):
            if ix < len(items):
                do_a(ix)
            if ix >= LOOK:
                do_b(ix - LOOK)
    return P.finish()


def _lay_cw(conv_w, conv_b):
    a = np.concatenate([conv_w, conv_b[None]], 0)
    a = a.reshape(4, NFC, 128).transpose(2, 1, 0)
    return np.ascontiguousarray(a.reshape(128, NFC * 4)).astype(np.float32)


def _lay_g(g):
    return np.ascontiguousarray(np.asarray(g, np.float32).reshape(8, 128).T)


def _rep(v, n=128):
    v = np.asarray(v, np.float32)
    return np.ascontiguousarray(np.broadcast_to(v[None], (n,) + v.shape)).astype(np.float32)


def _fox_inputs(z, b, hh):
    w_in = z["a_w_in"][0]
    wA = np.zeros((2, D, 512), np.float32)
    wB = np.zeros((2, D, 260), np.float32)
    for hp in range(2):
        h0 = hh * 8 + hp * 4
        wA[hp, :, 0:256] = w_in[:, h0 * 64:(h0 + 4) * 64]
        wA[hp, :, 256:512] = w_in[:, 1024 + h0 * 64:1024 + (h0 + 4) * 64]
        wB[hp, :, 0:256] = w_in[:, 2048 + h0 * 64:2048 + (h0 + 4) * 64]
        wB[hp, :, 256:260] = w_in[:, 3072 + h0:3072 + h0 + 4]
    return {"x": np.ascontiguousarray(z["x"][b]), "gl": _lay_g(z["a_norm"][0]), "wA": wA, "wB": wB,
            "bfl": _rep(z["a_b_f"][0][hh * 8:hh * 8 + 8]), "qg": _rep(z["a_q_gain"][0]), "kg": _rep(z["a_k_gain"][0])}


_ROPE_INV = (500000.0 ** (-np.arange(8, dtype=np.float32) * (2.0 / 16))).astype(np.float32)


def _nsa_inputs(z, h1_b, kvp_b, pos_b, g, TT=T):
    w_in = z["b_w_in"][0]
    NTL = TT // 128
    NCB = TT // 16
    NCT = NCB // 128
    wq = np.ascontiguousarray(w_in[:, g * 512:(g + 1) * 512])
    gcols = [1024 + br * 16 + g * 8 + hl for br in range(3) for hl in range(8)]
    wg = np.ascontiguousarray(w_in[:, gcols])
    bgl = _rep(z["b_b_gate"][0][[c - 1024 for c in gcols]])
    kv = None if kvp_b is None else np.ascontiguousarray(kvp_b.reshape(TT, 6, 2, 64)[:, :, g, :].reshape(TT, 384))
    pos = np.ascontiguousarray(pos_b.reshape(NTL, 128).T).astype(np.int32)
    ends = np.minimum(np.arange(NCB) * 16 + 31, TT - 1)
    pose = np.ascontiguousarray(pos_b[ends].reshape(NCT, 128).T).astype(np.int32)
    kgains = np.ascontiguousarray(np.broadcast_to(np.stack([z["kc_gain"], z["ks_gain"], z["kw_gain"]])[None], (128, 3, 64))).astype(np.float32)
    peT = np.concatenate([z["kc_pe"].T, z["vc_pe"].T], 0).astype(np.float32)
    w1 = np.concatenate([z["kc_w1"].reshape(32, 64, 256).transpose(1, 0, 2), z["vc_w1"].reshape(32, 64, 256).transpose(1, 0, 2)], 0)
    w2 = np.stack([z["kc_w2"].reshape(2, 128, 64).transpose(1, 0, 2), z["vc_w2"].reshape(2, 128, 64).transpose(1, 0, 2)], 1)
    return {"h1": None if h1_b is None else np.ascontiguousarray(h1_b), "gl": _lay_g(z["b_norm"][0]), "wq": wq, "wg": wg, "bgl": bgl, "qg": _rep(z["b_q_gain"][0]),
            "kvp": kv, "pos": pos, "pose": pose, "kgains": kgains, "invf": _rep(_ROPE_INV), "peT": np.ascontiguousarray(peT),
            "w1": np.ascontiguousarray(w1.astype(np.float32)), "w2": np.ascontiguousarray(w2.astype(np.float32))}


def make_precast_work(P, ios):
    s = P.s
    stg = [P.sb(f"pc_stage{i}", [128, 1408], F32) for i in range(2)]
    cbf = [P.sb(f"pc_cb{i}", [128, 1408], BF16) for i in range(2)]
    ctr = [0]
    work = []

    def piece(src_ap, n, dst_ap, view=None):
        def emit():
            i = ctr[0] % 2
            ctr[0] += 1
            st_t, cb = stg[i], cbf[i]
            s.dma("sp", lambda q: q.dma_start(out=st_t[:, 0:n], in_=src_ap), writes=[f"pc_stage{i}"])
            s.op("pool", lambda e: e.tensor_copy(out=cb[:, 0:n], in_=st_t[:, 0:n]), reads=[f"pc_stage{i}"], writes=[f"pc_cb{i}"])
            srcv = cb[:, 0:n] if view is None else view(cb[:, 0:n])
            s.dma("sp", lambda q: q.dma_start(out=dst_ap, in_=srcv), reads=[f"pc_cb{i}"], final=True)
        work.append(emit)

    for io_ in ios:
        w_out, w_up, w_dn = io_["w_out"], io_["w_up"], io_["w_dn"]
        for kc in range(8):
            piece(w_out[kc * 128:(kc + 1) * 128, :], 1024, io_["wout_s"][kc * 128:(kc + 1) * 128, :])
        for kc in range(8):
            for cb_ in range(4):
                piece(w_up[kc * 128:(kc + 1) * 128, cb_ * 1408:(cb_ + 1) * 1408], 1408, io_["wup_s"][:, cb_ * 11:(cb_ + 1) * 11, kc, :],
                      view=lambda a: a.rearrange("p (f n) -> p f n", f=11))
        for j in range(22):
            piece(w_dn[j * 128:(j + 1) * 128, :], 1024, io_["wdn_s"][j * 128:(j + 1) * 128, :])
        if "kv_w" in io_:
            for kc in range(8):
                piece(io_["kv_w"][kc * 128:(kc + 1) * 128, :], 768, io_["kvw_s"][kc * 128:(kc + 1) * 128, :])
        io_["precast_done"] = True
    return work


GROUPS = [[0, 1], [2, 3], [4, 5], [6, 7]]
NT_FFN = 4096 + 128


def _cc_block(nc, tag, pairs):
    with nc.cleanup_on_exit():
        sem = nc.alloc_semaphore(name=tag + "cc")
        with nc.Block() as block:
            @block.gpsimd
            def _(g):
                for i, (a, b) in enumerate(pairs):
                    g.collective_compute("AllGather", ALU.bypass, replica_groups=GROUPS,
                                         ins=[a.ap().opt()], outs=[b.ap().opt()]).then_inc(sem)
                    g.wait_ge(sem, i + 1)


def _dump_block(nc, tag, src, dst):
    with nc.cleanup_on_exit():
        sem = nc.alloc_semaphore(name=tag + "dump")
        with nc.Block() as block:
            @block.sync
            def _(q):
                q.dma_start(out=dst.ap(), in_=src.ap()).then_inc(sem, 16)
                q.wait_ge(sem, 16)


def build_fused(upto=99):
    nc = bass.Bass("TRN2", target_bir_lowering=False)
    ext = {}

    def ein(name, shape, dt=F32):
        ext[name] = nc.dram_tensor(name, list(shape), dt, kind="ExternalInput")
        return ext[name].ap()

    NTL, NCT = T // 128, T // 16 // 128
    ioA = {"x": ein("A_x", [T, D]), "gl": ein("A_gl", [128, 8]), "wA": ein("A_wA", [2, D, 512]), "wB": ein("A_wB", [2, D, 260]),
           "bfl": ein("A_bfl", [128, 8]), "qg": ein("A_qg", [128, 64]), "kg": ein("A_kg", [128, 64])}
    msk = ein("msk", [128, 2])
    ioB = {"msk": msk, "hin": ein("B_hin", [NT_FFN, D]), "w_out": ein("B_w_out", [D, D]), "gl": ein("B_gl", [128, 8]),
           "w_up": ein("B_w_up", [D, 2 * DFF]), "cw": ein("B_cw", [128, NFC * 4]), "w_dn": ein("B_w_dn", [DFF, D]),
           "kvg": ein("B_kvg", [128, 8]), "kv_w": ein("B_kv_w", [D, 768]), "bg": ein("B_bg", [128, 8])}
    ioC = {"msk": msk, "wq": ein("C_wq", [D, 512]), "wg": ein("C_wg", [D, 24]), "bgl": ein("C_bgl", [128, 24]), "qg": ein("C_qg", [128, 64]),
           "pos": ein("C_pos", [128, NTL], I32), "pose": ein("C_pose", [128, NCT], I32), "kgains": ein("C_kgains", [128, 3, 64]),
           "invf": ein("C_invf", [128, 8]), "peT": ein("C_peT", [128, 32]), "w1": ein("C_w1", [128, 32, 256]), "w2": ein("C_w2", [128, 2, 2, 64])}
    ioD = {"msk": msk, "w_out": ein("D_w_out", [D, D]), "gl": ein("D_gl", [128, 8]), "w_up": ein("D_w_up", [D, 2 * DFF]),
           "cw": ein("D_cw", [128, NFC * 4]), "w_dn": ein("D_w_dn", [DFF, D])}
    out = nc.dram_tensor("out", [4096, D], F32, kind="ExternalOutput")
    oT1_my = [nc.dram_tensor(f"oT1_my{k}", [128, T], BF16) for k in range(4)]
    oT1_all = [nc.dram_tensor(f"oT1_all{k}", [256, T], BF16) for k in range(4)]
    h1_my = nc.dram_tensor("h1_my", [4096, D], F32)
    hl_send = nc.dram_tensor("hl_send", [128, D], F32)
    hl_all = nc.dram_tensor("hl_all", [256, D], F32)
    xnT_my = [nc.dram_tensor(f"xnT_my{k}", [256, 4096], BF16) for k in range(4)]
    xnT_all = [nc.dram_tensor(f"xnT_all{k}", [512, 4096], BF16) for k in range(4)]
    kv_send = [nc.dram_tensor(f"kv_send{k}", [1024, 384], F32) for k in range(8)]
    kv_all = [nc.dram_tensor(f"kv_all{k}", [2048, 384], F32) for k in range(8)]
    oT2_my = [nc.dram_tensor(f"oT2_my{k}", [128, T], BF16) for k in range(4)]
    oT2_all = [nc.dram_tensor(f"oT2_all{k}", [256, T], BF16) for k in range(4)]
    aps = lambda l: [t_.ap() for t_ in l]

    for pre, io_ in (("B", ioB), ("D", ioD)):
        io_["wout_s"] = nc.dram_tensor(pre + "_wout_s", [D, D], BF16).ap()
        io_["wup_s"] = nc.dram_tensor(pre + "_wup_s", [128, NFC, 8, 128], BF16).ap()
        io_["wdn_s"] = nc.dram_tensor(pre + "_wdn_s", [DFF, D], BF16).ap()
    ioB["kvw_s"] = nc.dram_tensor("B_kvw_s", [D, 768], BF16).ap()
    ioA["oT_l"] = aps(oT1_my)
    ioB.update({"oT_all_l": aps(oT1_all), "hout": h1_my.ap(), "kv_send_l": aps(kv_send), "xnT_my_l": aps(xnT_my), "hl_send": hl_send.ap()})
    ioC.update({"xnT_all_l": aps(xnT_all), "kv_all_l": aps(kv_all), "oT_l": aps(oT2_my)})
    ioD.update({"oT_all_l": aps(oT2_all), "h_my": h1_my.ap(), "hl_all": hl_all.ap(), "hout": out.ap()})

    with nc.cleanup_on_exit():
        PA = Prog(nc, "A_", ioA)
        ioA["bg_work"] = make_precast_work(PA, [ioB, ioD])
        build_fox(PA, T)
    if upto == 0:
        dbg = nc.dram_tensor("dbg", [128, T], BF16, kind="ExternalOutput")
        _dump_block(nc, "d0_", oT1_my[0], dbg)
        return nc
    _cc_block(nc, "e1_", list(zip(oT1_my, oT1_all)))
    if upto == 1:
        dbg = nc.dram_tensor("dbg", [256, T], BF16, kind="ExternalOutput")
        _dump_block(nc, "d1_", oT1_all[3], dbg)
        return nc
    with nc.cleanup_on_exit():
        build_ffn(Prog(nc, "B_", ioB), True, NT_FFN, False)
    if upto == 2:
        dbg = nc.dram_tensor("dbg", [4096, D], F32, kind="ExternalOutput")
        _dump_block(nc, "d2_", h1_my, dbg)
        return nc
    _cc_block(nc, "e2_", [(hl_send, hl_all)] + list(zip(xnT_my, xnT_all)) + list(zip(kv_send, kv_all)))
    with nc.cleanup_on_exit():
        build_nsa(Prog(nc, "C_", ioC), T)
    _cc_block(nc, "e3_", list(zip(oT2_my, oT2_all)))
    with nc.cleanup_on_exit():
        build_ffn(Prog(nc, "D_", ioD), False, NT_FFN, True)
    return nc


def _core_inputs(z, c):
    b, r = c // 2, c % 2
    d = {}
    for k, v in _fox_inputs(z, b, r).items():
        d["A_" + k] = v
    m = np.zeros((128, 2), np.float32)
    m[:, r] = 1.0
    d["msk"] = m
    hin = np.zeros((NT_FFN, D), np.float32)
    if r == 0:
        hin[128:] = z["x"][b, 0:4096]
    else:
        hin[:] = z["x"][b, 4096 - 128:8192]
    d["B_hin"] = hin
    for L, pre, w_out in ((0, "B_", z["a_w_out"][0]), (1, "D_", z["b_w_out"][0])):
        d[pre + "w_out"] = np.ascontiguousarray(w_out)
        d[pre + "gl"] = _lay_g(z["f_norm"][L])
        d[pre + "w_up"] = np.ascontiguousarray(z["f_w_up"][L])
        d[pre + "cw"] = _lay_cw(z["f_conv_w"][L], z["f_conv_b"][L])
        d[pre + "w_dn"] = np.ascontiguousarray(z["f_w_down"][L])
    d["B_kvg"] = _lay_g(z["kv_norm"])
    d["B_kv_w"] = np.ascontiguousarray(z["kv_w"])
    d["B_bg"] = _lay_g(z["b_norm"][0])
    ni = _nsa_inputs(z, None, None, z["positions"][b], r)
    for k in ("wq", "wg", "bgl", "qg", "pos", "pose", "kgains", "invf", "peT", "w1", "w2"):
        d["C_" + k] = ni[k]
    return d


def kernel(**inputs):
    z = {k: np.asarray(v) for k, v in inputs.items()}
    B = z["x"].shape[0]
    cores = list(range(8))
    nc = build_fused()
    res = run_bass_kernel_spmd(nc, [_core_inputs(z, c) for c in cores], core_ids=cores)
    out = np.stack([np.concatenate([res.results[2 * b]["out"], res.results[2 * b + 1]["out"]], 0) for b in range(B)], 0)
    return out.astype(np.float32)
```

```python
import contextlib
import numpy as np
import ml_dtypes
import concourse.bass as bass
import concourse.mybir as mybir
from concourse.bass_utils import run_bass_kernel_spmd

F32 = mybir.dt.float32
BF16 = mybir.dt.bfloat16
I32 = mybir.dt.int32
AF = mybir.ActivationFunctionType
ALU = mybir.AluOpType
AX = mybir.AxisListType

EPOCH = 8192
NDMASEM = 24

D = 1024
T = 8192
NH = 16
DH = 64
DFF = 2816
NFC = 44
EPS = 1e-6


class Sched:
    ENGS = ("pe", "act", "dve", "pool", "sp")

    def __init__(self, nc, tag=""):
        self.nc = nc
        self.tag = tag
        self.ops = {e: [] for e in self.ENGS}
        self.cnt = {e: 0 for e in self.ENGS}
        self.dcnt = {e: 0 for e in self.ENGS}
        self.lastw = {}
        self.readers = {}
        self.known = {e: {} for e in self.ENGS}
        self.sems = {}
        self.final_tokens = []

    def _tok_compute(self, eng):
        i = self.cnt[eng]
        self.cnt[eng] += 1
        return (("c", eng, i // EPOCH), i % EPOCH + 1)

    def _tok_dma(self, q):
        i = self.dcnt[q]
        self.dcnt[q] += 1
        return (("d", q, i % NDMASEM), 16 * (i // NDMASEM + 1))

    def _need(self, eng, waits, tok):
        sk, v = tok
        if self.known[eng].get(sk, 0) >= v:
            return
        waits[sk] = max(waits.get(sk, 0), v)

    def _deps(self, eng, reads, writes, is_dma):
        waits = {}

        def same_pe(t):
            return eng == "pe" and (not is_dma) and t[0][0] == "c" and t[0][1] == "pe"

        for r in reads:
            t = self.lastw.get(r)
            if t is not None and not same_pe(t):
                self._need(eng, waits, t)
        for w in writes:
            t = self.lastw.get(w)
            if t is not None and not same_pe(t):
                self._need(eng, waits, t)
            for t in self.readers.get(w, ()):
                if (not is_dma) and t[0][0] == "c" and t[0][1] == eng:
                    continue
                self._need(eng, waits, t)
        for sk, v in waits.items():
            self.known[eng][sk] = v
        return waits

    def _commit(self, tok, reads, writes):
        for r in reads:
            self.readers.setdefault(r, []).append(tok)
        for w in writes:
            self.lastw[w] = tok
            self.readers[w] = []

    def op(self, eng, emit, reads=(), writes=()):
        waits = self._deps(eng, reads, writes, False)
        tok = self._tok_compute(eng)
        self.ops[eng].append((waits, emit, tok))
        self._commit(tok, reads, writes)
        return tok

    def dma(self, q, emit, reads=(), writes=(), final=False):
        waits = self._deps(q, reads, writes, True)
        tok = self._tok_dma(q)
        sk, v = tok
        if v > 16 and self.known[q].get(sk, 0) < v - 16:
            waits[sk] = max(waits.get(sk, 0), v - 16)
            self.known[q][sk] = v - 16
        self.ops[q].append((waits, emit, tok))
        self._commit(tok, reads, writes)
        if final:
            self.final_tokens.append(tok)
        return tok

    def emit_all(self):
        nc = self.nc
        semkeys = set()
        for e in self.ENGS:
            for waits, emit, tok in self.ops[e]:
                semkeys.add(tok[0])
                semkeys.update(waits.keys())
        semkeys = sorted(semkeys, key=str)
        with contextlib.ExitStack() as st:
            for sk in semkeys:
                self.sems[sk] = nc.alloc_semaphore(name=self.tag + "s_" + "_".join(str(x) for x in sk))
            block = st.enter_context(nc.Block())
            sems = self.sems

            def run(engname, eng):
                for waits, emit, tok in self.ops[engname]:
                    for sk, v in waits.items():
                        eng.wait_ge(sems[sk], v)
                    ins = emit(eng)
                    ins.then_inc(sems[tok[0]], 16 if tok[0][0] == "d" else 1)
                if engname == "sp":
                    for sk, v in self.final_tokens:
                        eng.wait_ge(sems[sk], v)

            @block.tensor
            def _(eng):
                run("pe", eng)

            @block.scalar
            def _(eng):
                run("act", eng)

            @block.vector
            def _(eng):
                run("dve", eng)

            @block.gpsimd
            def _(eng):
                run("pool", eng)

            @block.sync
            def _(eng):
                run("sp", eng)


class Prog:
    def __init__(self, nc, tag, io):
        self.nc = nc
        self.tag = tag
        self.io = io
        self.st = contextlib.ExitStack()
        self.s = Sched(self.nc, tag)

    def din(self, name, shape, dt=F32):
        ap = self.io[name]
        assert list(ap.shape) == list(shape), (name, ap.shape, shape)
        return ap

    dout = din

    def sb(self, name, shape, dt):
        return self.st.enter_context(self.nc.sbuf_tensor(self.tag + name, list(shape), dt))

    def ps(self, name, shape, dt=F32):
        return self.st.enter_context(self.nc.psum_tensor(self.tag + name, list(shape), dt))

    def finish(self):
        import os
        if os.environ.get("KDEBUG"):
            print("phase", self.tag, "sbuf remaining", self.nc.sbuf_bytes_remaining, flush=True)
        self.s.emit_all()
        self.st.close()
        return self.nc


def make_ident(P, dt, name):
    s = P.s
    idf = P.sb(name + "_f", [128, 128], F32)
    ident = P.sb(name, [128, 128], dt)
    s.op("pool", lambda e: e.memset(idf[:], 1.0), writes=[name + "_f"])
    s.op("pool", lambda e: e.affine_select(out=idf[:], in_=idf[:], pattern=[[-1, 128]],
                                           compare_op=ALU.is_equal, fill=0.0, base=0, channel_multiplier=1),
         reads=[name + "_f"], writes=[name + "_f"])
    s.op("pool", lambda e: e.tensor_copy(out=ident[:], in_=idf[:]), reads=[name + "_f"], writes=[name])
    return ident


def build_ffn(P, with_kv, NT, hin_int):
    nc, s = P.nc, P.s
    NTO = NT - 128
    msk = P.din("msk", [128, 2])
    if hin_int:
        h_my = P.din("h_my", [NTO, D])
        hl_all = P.din("hl_all", [256, D])
    else:
        hin = P.din("hin", [NT, D])
    oT_l = P.io["oT_all_l"]
    w_out = P.din("w_out", [D, D])
    gl = P.din("gl", [128, 8])
    w_up = P.din("w_up", [D, 2 * DFF])
    cw = P.din("cw", [128, NFC * 4])
    w_dn = P.din("w_dn", [DFF, D])
    hout = P.dout("hout", [NTO, D])
    if with_kv:
        kvg = P.din("kvg", [128, 8])
        kv_w = P.din("kv_w", [D, 768])
        bg = P.din("bg", [128, 8])
        kv_send_l = P.io["kv_send_l"]
        xnT_my_l = P.io["xnT_my_l"]
        hl_send = P.dout("hl_send", [128, D])

    W = 512
    ident = make_ident(P, BF16, "ident")
    gl_sb = P.sb("gl_sb", [128, 8], F32)
    cw_sb = P.sb("cw_sb", [128, NFC * 4], F32)
    s.dma("sp", lambda q: q.dma_start(out=gl_sb[:], in_=gl), writes=["gl"])
    s.dma("sp", lambda q: q.dma_start(out=cw_sb[:], in_=cw), writes=["cw"])
    msk_sb = P.sb("msk_sb", [128, 2], F32)
    s.dma("sp", lambda q: q.dma_start(out=msk_sb[:], in_=msk), writes=["msk"])
    if with_kv:
        kvg_sb = P.sb("kvg_sb", [128, 8], F32)
        s.dma("sp", lambda q: q.dma_start(out=kvg_sb[:], in_=kvg), writes=["kvg"])
        bg_sb = P.sb("bg_sb", [128, 8], F32)
        s.dma("sp", lambda q: q.dma_start(out=bg_sb[:], in_=bg), writes=["bg"])

    oT_sb = P.sb("oT_sb", [128, 8, W], BF16)
    oT_a = P.sb("oT_a", [128, 8, W], BF16)
    wbig = P.sb("wbig", [128, 22 * 1024], BF16)
    wstage = [P.sb(f"wstage{i}", [128, 2048], F32) for i in range(2)]
    hin_sb = [P.sb(f"hin_sb{i}", [128, D], F32) for i in range(2)]
    hmid = P.sb("hmid", [128, 4, D], F32)
    junk = P.sb("junk", [128, D], BF16)
    stat = P.sb("stat", [128, 8], F32)
    xs = [P.sb(f"xs{i}", [128, D], BF16) for i in range(2)]
    hnT = P.sb("hnT", [128, 8, W], BF16)
    wup_bf = [P.sb(f"wup_bf{i}", [128, 8, 256], BF16) for i in range(2)]
    ubuf = [P.sb(f"ubuf{i}", [128, 2 + W], F32) for i in range(2)]
    halo = P.sb("halo", [128, NFC, 2], F32)
    ctmp = [P.sb(f"ctmp{i}", [128, W], F32) for i in range(4)]
    sg = P.sb("sg", [128, W], F32)
    gT = P.sb("gT", [128, 22, W], BF16)
    otile = [P.sb(f"otile{i}", [128, D], F32) for i in range(2)]
    if with_kv:
        hkT = P.sb("hkT", [128, 8, W], BF16)
        hbT = P.sb("hbT", [128, 8, W], BF16)
        kvtile = [P.sb(f"kvtile{i}", [128, 768], F32) for i in range(2)]

    ps_o = [P.ps(f"ps_o{i}", [128, 512]) for i in range(2)]
    ps_t = P.ps("ps_t", [128, 8, 128], BF16)
    ps_u = [P.ps(f"ps_u{i}", [128, 512]) for i in range(4)]

    s.op("pool", lambda e: e.memset(halo[:], 0.0), writes=["halo"])

    w_out_v = w_out.rearrange("(k p) n -> p k n", p=128)
    w_up_v = w_up.rearrange("(k p) n -> p k n", p=128)
    w_dn_v = w_dn.rearrange("(k p) n -> p k n", p=128)
    def oT_load(q, dst, gcol, Wc):
        ins = None
        for kc in range(8):
            ins = q.dma_start(out=dst[:, kc, 0:Wc], in_=oT_l[kc % 4][(kc // 4) * 128:(kc // 4 + 1) * 128, gcol:gcol + Wc])
        return ins
    if with_kv:
        kv_w_v = kv_w.rearrange("(k p) n -> p k n", p=128)

    stage_ctr = [0]
    wout_s, wup_s, wdn_s = P.io["wout_s"], P.io["wup_s"], P.io["wdn_s"]
    kvw_s = P.io["kvw_s"] if with_kv else None
    cbuf = [P.sb(f"cbuf{i}", [128, 2048], BF16) for i in range(3)]
    skeys = []
    cast_ctr = [0]

    def precast(src_ap, n, dst_ap, view=None):
        c = cast_ctr[0]
        cast_ctr[0] += 1
        i, ci = c % 2, c % 3
        st_t, cb = wstage[i], cbuf[ci]
        s.dma("sp", lambda q: q.dma_start(out=st_t[:, 0:n], in_=src_ap), writes=[f"wstage{i}"])
        eng = ("pool", "act", "dve")[c % 3]
        if eng == "act":
            s.op("act", lambda e: e.activation(out=cb[:, 0:n], in_=st_t[:, 0:n], func=AF.Copy), reads=[f"wstage{i}"], writes=[f"cbuf{ci}"])
        else:
            s.op(eng, lambda e: e.tensor_copy(out=cb[:, 0:n], in_=st_t[:, 0:n]), reads=[f"wstage{i}"], writes=[f"cbuf{ci}"])
        key = f"ws{c}"
        skeys.append(key)
        srcv = cb[:, 0:n] if view is None else view(cb[:, 0:n])
        s.dma("sp", lambda q: q.dma_start(out=dst_ap, in_=srcv), reads=[f"cbuf{ci}"], writes=[key])

    if not P.io.get("precast_done"):
        for kc in range(8):
            precast(w_out[kc * 128:(kc + 1) * 128, :], 1024, wout_s[kc * 128:(kc + 1) * 128, :])
        for kc in range(8):
            for cb_ in range(4):
                precast(w_up[kc * 128:(kc + 1) * 128, cb_ * 1408:(cb_ + 1) * 1408], 1408, wup_s[:, cb_ * 11:(cb_ + 1) * 11, kc, :],
                        view=lambda a: a.rearrange("p (f n) -> p f n", f=11))
        for j in range(22):
            precast(w_dn[j * 128:(j + 1) * 128, :], 1024, wdn_s[j * 128:(j + 1) * 128, :])
        if with_kv:
            for kc in range(8):
                precast(kv_w[kc * 128:(kc + 1) * 128, :], 768, kvw_s[kc * 128:(kc + 1) * 128, :])
    wout_sv = wout_s.rearrange("(k p) n -> p k n", p=128)
    wdn_sv = wdn_s.rearrange("(j p) n -> p j n", p=128)
    if with_kv:
        kvw_sv = kvw_s.rearrange("(k p) n -> p k n", p=128)

    def load_cast(dst_ap, src_ap, ncols, dstkey):
        i = stage_ctr[0] % 2
        stage_ctr[0] += 1
        st_t = wstage[i]
        s.dma("sp", lambda q: q.dma_start(out=st_t[:, 0:ncols], in_=src_ap), writes=[f"wstage{i}"])
        s.op("pool", lambda e: e.tensor_copy(out=dst_ap, in_=st_t[:, 0:ncols]), reads=[f"wstage{i}"], writes=[dstkey])

    def norm_transpose(src_tile, src_key, dstT, dst_key, col0, gain_sb, gain_key, par, second=None):
        x_s = xs[par]
        s.op("act", lambda e: e.activation(out=junk[:], in_=src_tile, func=AF.Square, accum_out=stat[:, 0:1]),
             reads=[src_key], writes=["junk", "stat0"])
        s.op("dve", lambda e: e.tensor_scalar(out=stat[:, 1:2], in0=stat[:, 0:1], scalar1=1.0 / D, scalar2=EPS,
                                               op0=ALU.mult, op1=ALU.add), reads=["stat0"], writes=["stat1"])
        s.op("act", lambda e: e.activation(out=stat[:, 2:3], in_=stat[:, 1:2], func=AF.Sqrt), reads=["stat1"], writes=["stat2"])
        s.op("dve", lambda e: e.reciprocal(out=stat[:, 3:4], in_=stat[:, 2:3]), reads=["stat2"], writes=["stat3"])
        s.op("dve", lambda e: e.tensor_scalar(out=x_s[:], in0=src_tile, scalar1=stat[:, 3:4], scalar2=None, op0=ALU.mult),
             reads=[src_key, "stat3"], writes=[f"xs{par}"])
        for kc in range(8):
            s.op("pe", lambda e, kc=kc: e.transpose(out=ps_t[:, kc, :], in_=x_s[:, kc * 128:(kc + 1) * 128], identity=ident[:]),
                 reads=[f"xs{par}", "ident"], writes=["ps_t"])
        for kc in range(8):
            s.op("act", lambda e, kc=kc: e.activation(out=dstT[:, kc, col0:col0 + 128], in_=ps_t[:, kc, :], func=AF.Copy,
                                                       scale=gain_sb[:, kc:kc + 1]),
                 reads=["ps_t", gain_key], writes=[dst_key])
        if second is not None:
            d2, k2, g2, gk2 = second
            for kc in range(8):
                s.op("act", lambda e, kc=kc: e.activation(out=d2[:, kc, col0:col0 + 128], in_=ps_t[:, kc, :], func=AF.Copy,
                                                           scale=g2[:, kc:kc + 1]),
                     reads=["ps_t", gk2], writes=[k2])

    chunks = [(0, 128)] + [(128 + 512 * i, 512) for i in range((NT - 128) // 512)]
    tile_ctr = 0
    for (c0, Wc) in chunks:
        nt = Wc // 128
        s.dma("sp", lambda q: q.dma_start(out=wbig[:, 0:8192].rearrange("p (k n) -> p k n", k=8), in_=wout_sv), reads=skeys, writes=["wbig"])
        g1 = 4096 - 128 + c0
        for kc in range(8):
            s.dma("sp", lambda q, g1=g1, Wc=Wc, kc=kc: q.dma_start(out=oT_sb[:, kc, 0:Wc],
                                                                 in_=oT_l[kc % 4][(kc // 4) * 128:(kc // 4 + 1) * 128, g1:g1 + Wc]), writes=["oT_sb"])
        s.op("dve", lambda e, Wc=Wc: e.tensor_scalar(out=oT_sb[:, :, 0:Wc], in0=oT_sb[:, :, 0:Wc], scalar1=msk_sb[:, 1:2], scalar2=None,
                                                     op0=ALU.mult), reads=["oT_sb", "msk"], writes=["oT_sb"])
        if c0 >= 128:
            g0 = c0 - 128
            for kc in range(8):
                s.dma("sp", lambda q, g0=g0, Wc=Wc, kc=kc: q.dma_start(out=oT_a[:, kc, 0:Wc],
                                                                     in_=oT_l[kc % 4][(kc // 4) * 128:(kc // 4 + 1) * 128, g0:g0 + Wc]), writes=["oT_a"])
            s.op("dve", lambda e, Wc=Wc: e.scalar_tensor_tensor(out=oT_sb[:, :, 0:Wc], in0=oT_a[:, :, 0:Wc], scalar=msk_sb[:, 0:1],
                                                                in1=oT_sb[:, :, 0:Wc], op0=ALU.mult, op1=ALU.add),
                 reads=["oT_a", "oT_sb", "msk"], writes=["oT_sb"])
        for i in range(nt):
            par = tile_ctr % 2
            tile_ctr += 1
            r0 = c0 + i * 128
            hs = hin_sb[par]
            if not hin_int:
                s.dma("sp", lambda q, hs=hs, r0=r0: q.dma_start(out=hs[:], in_=hin[r0:r0 + 128, :]), writes=[f"hin_sb{par}"])
            elif r0 < 128:
                s.dma("sp", lambda q, hs=hs: q.dma_start(out=hs[:], in_=hl_all[0:128, :]), writes=[f"hin_sb{par}"])
                s.op("dve", lambda e, hs=hs: e.tensor_scalar(out=hs[:], in0=hs[:], scalar1=msk_sb[:, 1:2], scalar2=None, op0=ALU.mult),
                     reads=[f"hin_sb{par}", "msk"], writes=[f"hin_sb{par}"])
            else:
                s.dma("sp", lambda q, hs=hs, r0=r0: q.dma_start(out=hs[:], in_=h_my[r0 - 128:r0, :]), writes=[f"hin_sb{par}"])
            for hf in range(2):
                for kc in range(8):
                    s.op("pe", lambda e, kc=kc, hf=hf, i=i: e.matmul(ps_o[hf][:], lhsT=oT_sb[:, kc, i * 128:(i + 1) * 128],
                                                                    rhs=wbig[:, kc * 1024 + hf * 512: kc * 1024 + hf * 512 + 512],
                                                                    start=(kc == 0), stop=(kc == 7)),
                         reads=["oT_sb", "wbig"], writes=[f"ps_o{hf}"])
                s.op("dve", lambda e, hf=hf, i=i, hs=hs: e.tensor_tensor(out=hmid[:, i, hf * 512:(hf + 1) * 512], in0=ps_o[hf][:],
                                                                          in1=hs[:, hf * 512:(hf + 1) * 512], op=ALU.add),
                     reads=[f"ps_o{hf}", f"hin_sb{par}"], writes=[f"hmid{i}"])
            norm_transpose(hmid[:, i, :], f"hmid{i}", hnT, "hnT", i * 128, gl_sb, "gl", par)
        for j0 in (0, 6, 12, 18):
            j1 = min(j0 + 6, 22)
            s.dma("sp", lambda q, j0=j0, j1=j1: q.dma_start(out=wbig[:, j0 * 1024:j1 * 1024].rearrange("p (j n) -> p j n", j=j1 - j0),
                                                          in_=wdn_sv[:, j0:j1, :]), reads=skeys, writes=["wbig"])
        for j in range(22):
            wb = wup_bf[j % 2]
            wkey = f"wup_bf{j % 2}"
            for part in range(2):
                s.dma("sp", lambda q, wb=wb, part=part, j=j: q.dma_start(out=wb[:, :, part * 128:(part + 1) * 128], in_=wup_s[:, part * 22 + j, :, :]),
                      reads=skeys, writes=[wkey + f"_{part}"])
            for part in range(2):
                idx = part * 22 + j
                pu = ps_u[(j % 2) * 2 + part]
                pukey = f"ps_u{(j % 2) * 2 + part}"
                ub = ubuf[part]
                ubk = f"ubuf{part}"
                for kc in range(8):
                    s.op("pe", lambda e, kc=kc, pu=pu, wb=wb, part=part, Wc=Wc: e.matmul(
                        pu[:, 0:Wc], lhsT=wb[:, kc, part * 128:(part + 1) * 128], rhs=hnT[:, kc, 0:Wc],
                        start=(kc == 0), stop=(kc == 7)),
                        reads=[wkey + f"_{part}", "hnT"], writes=[pukey])
                s.op("act", lambda e, ub=ub, pu=pu, Wc=Wc: e.activation(out=ub[:, 2:2 + Wc], in_=pu[:, 0:Wc], func=AF.Copy),
                     reads=[pukey], writes=[ubk])
                s.op("pool", lambda e, ub=ub, idx=idx: e.tensor_copy(out=ub[:, 0:2], in_=halo[:, idx, :]),
                     reads=["halo%d" % idx], writes=[ubk + "h"])
                s.op("pool", lambda e, ub=ub, idx=idx, Wc=Wc: e.tensor_copy(out=halo[:, idx, :], in_=ub[:, Wc:Wc + 2]),
                     reads=[ubk, ubk + "h"], writes=["halo%d" % idx])
                c1, c2, c3 = ctmp[part * 2], ctmp[part * 2 + 1], ctmp[part * 2]
                k1, k2 = f"ctmp{part * 2}", f"ctmp{part * 2 + 1}"
                s.op("dve", lambda e, ub=ub, c1=c1, idx=idx, Wc=Wc: e.tensor_scalar(
                    out=c1[:, 0:Wc], in0=ub[:, 2:2 + Wc], scalar1=cw_sb[:, idx * 4 + 2:idx * 4 + 3],
                    scalar2=cw_sb[:, idx * 4 + 3:idx * 4 + 4], op0=ALU.mult, op1=ALU.add),
                    reads=[ubk, "cw"], writes=[k1])
                s.op("dve", lambda e, ub=ub, c1=c1, c2=c2, idx=idx, Wc=Wc: e.scalar_tensor_tensor(
                    out=c2[:, 0:Wc], in0=ub[:, 1:1 + Wc], scalar=cw_sb[:, idx * 4 + 1:idx * 4 + 2], in1=c1[:, 0:Wc],
                    op0=ALU.mult, op1=ALU.add), reads=[ubk, ubk + "h", "cw", k1], writes=[k2])
                s.op("dve", lambda e, ub=ub, c2=c2, c3=c3, idx=idx, Wc=Wc: e.scalar_tensor_tensor(
                    out=c3[:, 0:Wc], in0=ub[:, 0:Wc], scalar=cw_sb[:, idx * 4:idx * 4 + 1], in1=c2[:, 0:Wc],
                    op0=ALU.mult, op1=ALU.add), reads=[ubk, ubk + "h", "cw", k2], writes=[k1])
            s.op("act", lambda e, Wc=Wc: e.activation(out=sg[:, 0:Wc], in_=ctmp[0][:, 0:Wc], func=AF.Silu),
                 reads=["ctmp0"], writes=["sg"])
            s.op("pool", lambda e, j=j, Wc=Wc: e.tensor_tensor(out=gT[:, j, 0:Wc], in0=sg[:, 0:Wc], in1=ctmp[2][:, 0:Wc], op=ALU.mult),
                 reads=["sg", "ctmp2"], writes=["gT"])
        for i in range(nt):
            par = tile_ctr % 2
            tile_ctr += 1
            r0 = c0 + i * 128
            ot = otile[par]
            for hf in range(2):
                for j in range(22):
                    s.op("pe", lambda e, j=j, hf=hf, i=i: e.matmul(ps_o[hf][:], lhsT=gT[:, j, i * 128:(i + 1) * 128],
                                                                  rhs=wbig[:, j * 1024 + hf * 512: j * 1024 + hf * 512 + 512],
                                                                  start=(j == 0), stop=(j == 21)),
                         reads=["gT", "wbig"], writes=[f"ps_o{hf}"])
                s.op("dve", lambda e, hf=hf, i=i, ot=ot: e.tensor_tensor(out=ot[:, hf * 512:(hf + 1) * 512], in0=ps_o[hf][:],
                                                                          in1=hmid[:, i, hf * 512:(hf + 1) * 512], op=ALU.add),
                     reads=[f"ps_o{hf}", f"hmid{i}"], writes=[f"otile{par}"])
            if r0 >= 128:
                s.dma("sp", lambda q, ot=ot, r0=r0: q.dma_start(out=hout[r0 - 128:r0, :], in_=ot[:]),
                      reads=[f"otile{par}"], final=True)
            if with_kv and r0 == NT - 128:
                s.dma("sp", lambda q, ot=ot: q.dma_start(out=hl_send[:, :], in_=ot[:]), reads=[f"otile{par}"], final=True)
            if with_kv and r0 >= 128:
                norm_transpose(ot[:], f"otile{par}", hkT, "hkT", i * 128, kvg_sb, "kvg", par, second=(hbT, "hbT", bg_sb, "bg"))
        if with_kv and c0 >= 128:
            for kc in range(8):
                s.dma("sp", lambda q, c0=c0, Wc=Wc, kc=kc: q.dma_start(
                    out=xnT_my_l[kc // 2][(kc % 2) * 128:(kc % 2 + 1) * 128, c0 - 128:c0 - 128 + Wc], in_=hbT[:, kc, 0:Wc]),
                    reads=["hbT"], final=True)
            s.dma("sp", lambda q: q.dma_start(out=wbig[:, 0:6144].rearrange("p (k n) -> p k n", k=8), in_=kvw_sv), reads=skeys, writes=["wbig"])
            for i in range(nt):
                par = tile_ctr % 2
                tile_ctr += 1
                r0 = c0 + i * 128
                kt = kvtile[par]
                for hf in range(2):
                    for kc in range(8):
                        s.op("pe", lambda e, kc=kc, hf=hf, i=i: e.matmul(ps_o[hf][:, 0:384], lhsT=hkT[:, kc, i * 128:(i + 1) * 128],
                                                                        rhs=wbig[:, kc * 768 + hf * 384: kc * 768 + hf * 384 + 384],
                                                                        start=(kc == 0), stop=(kc == 7)),
                             reads=["hkT", "wbig"], writes=[f"ps_o{hf}"])
                    s.op("act", lambda e, hf=hf, kt=kt: e.activation(out=kt[:, hf * 384:(hf + 1) * 384], in_=ps_o[hf][:, 0:384], func=AF.Copy),
                         reads=[f"ps_o{hf}"], writes=[f"kvtile{par}"])
                for g in range(2):
                    tk = r0 - 128
                    s.dma("sp", lambda q, kt=kt, tk=tk, g=g: q.dma_start(
                        out=kv_send_l[g * 4 + tk // 1024][tk % 1024:tk % 1024 + 128, :].rearrange("p (a d) -> p a d", a=6),
                        in_=kt[:].rearrange("p (a g d) -> p a g d", a=6, g=2)[:, :, g, :]), reads=[f"kvtile{par}"], final=True)
    return P.finish()


_FILL_REGS = {}


def fill_reg(e, val):
    key = (id(e), val)
    if key not in _FILL_REGS:
        _FILL_REGS[key] = e.to_reg(val)
    return _FILL_REGS[key]


def rms_rstd(s, P, ss_ap, out_ap, n, rkey, wkey, tmp_ap, tkey):
    s.op("act", lambda e: e.activation(out=tmp_ap, in_=ss_ap, func=AF.Ln, scale=1.0 / n, bias=EPS), reads=[rkey], writes=[tkey])
    s.op("act", lambda e: e.activation(out=out_ap, in_=tmp_ap, func=AF.Exp, scale=-0.5), reads=[tkey], writes=[wkey])


def norm_transpose_g(P, src_tile, src_key, dstT, dst_key, col0, gain_sb, gain_key, xs_t, xs_key, junk, stat, ps_t, ident):
    s = P.s
    s.op("act", lambda e: e.activation(out=junk[:], in_=src_tile, func=AF.Square, accum_out=stat[:, 0:1]),
         reads=[src_key], writes=["junk", "stat0"])
    rms_rstd(s, P, stat[:, 0:1], stat[:, 3:4], D, "stat0", "stat3", stat[:, 1:2], "stat1")
    s.op("dve", lambda e: e.tensor_scalar(out=xs_t[:], in0=src_tile, scalar1=stat[:, 3:4], scalar2=None, op0=ALU.mult),
         reads=[src_key, "stat3"], writes=[xs_key])
    for kc in range(8):
        s.op("pe", lambda e, kc=kc: e.transpose(out=ps_t[:, kc, :], in_=xs_t[:, kc * 128:(kc + 1) * 128], identity=ident[:]),
             reads=[xs_key, "ident"], writes=["ps_t"])
    for kc in range(8):
        s.op("act", lambda e, kc=kc: e.activation(out=dstT[:, kc, col0:col0 + 128], in_=ps_t[:, kc, :], func=AF.Copy,
                                                   scale=gain_sb[:, kc:kc + 1]),
             reads=["ps_t", gain_key], writes=[dst_key])


def build_fox(P, TT):
    nc, s = P.nc, P.s
    NTL = TT // 128
    NQC = TT // 512
    x = P.din("x", [TT, D])
    gl = P.din("gl", [128, 8])
    wA = P.din("wA", [2, D, 512])
    wB = P.din("wB", [2, D, 260])
    bfl = P.din("bfl", [128, 8])
    qg = P.din("qg", [128, 64])
    kg = P.din("kg", [128, 64])
    oT_l = P.io["oT_l"]

    ident = make_ident(P, BF16, "ident")
    identf = make_ident(P, F32, "identf")
    gl_sb = P.sb("gl_sb", [128, 8], F32)
    bfl_sb = P.sb("bfl_sb", [128, 8], F32)
    qg_sb = P.sb("qg_sb", [128, 64], F32)
    kg_sb = P.sb("kg_sb", [128, 64], F32)
    for t_, d_, k_ in ((gl_sb, gl, "gl"), (bfl_sb, bfl, "bfl"), (qg_sb, qg, "qg"), (kg_sb, kg, "kg")):
        s.dma("sp", lambda q, t_=t_, d_=d_: q.dma_start(out=t_[:], in_=d_), writes=[k_])
    G8 = P.sb("G8", [128, 8, 64], F32)
    for h in range(4):
        s.op("pool", lambda e, h=h: e.tensor_scalar(out=G8[:, h, :], in0=qg_sb[:], scalar1=0.125, scalar2=None, op0=ALU.mult),
             reads=["qg"], writes=["G8"])
        s.op("pool", lambda e, h=h: e.tensor_copy(out=G8[:, 4 + h, :], in_=kg_sb[:]), reads=["kg"], writes=["G8"])
    ones_bf = P.sb("ones_bf", [128, 64], BF16)
    s.op("pool", lambda e: e.memset(ones_bf[:], 1.0), writes=["ones_bf"])
    onesf = P.sb("onesf", [128, 128], F32)
    s.op("pool", lambda e: e.memset(onesf[:], 1.0), writes=["onesf"])
    tri = P.sb("tri", [128, 128], F32)
    s.op("pool", lambda e: e.memset(tri[:], 1.0), writes=["tri"])
    s.op("pool", lambda e: e.affine_select(out=tri[:], in_=tri[:], pattern=[[1, 128]], compare_op=ALU.is_ge, fill=0.0,
                                           base=0, channel_multiplier=-1), reads=["tri"], writes=["tri"])
    selneg = P.sb("selneg", [4, 4, 128], F32)
    s.op("pool", lambda e: e.memset(selneg[:], -1.0), writes=["selneg"])
    s.op("pool", lambda e: e.affine_select(out=selneg[:], in_=selneg[:], pattern=[[1, 4], [0, 128]], compare_op=ALU.is_equal,
                                           fill=0.0, base=0, channel_multiplier=-1), reads=["selneg"], writes=["selneg"])

    KT = P.sb("KT", [128, 2, TT], BF16)
    V = P.sb("V", [128, NTL, 4, 128], BF16)
    s.op("pool", lambda e: e.memset(V[:], 1.0), writes=["V"])
    wA_bf = P.sb("wA_bf", [128, 8, 512], BF16)
    wB_bf = P.sb("wB_bf", [128, 8, 260], BF16)
    wstage = [P.sb(f"wstage{i}", [128, 512], F32) for i in range(2)]
    x_sb = [P.sb(f"x_sb{i}", [128, D], F32) for i in range(2)]
    xs = [P.sb(f"xs{i}", [128, D], BF16) for i in range(2)]
    junk = P.sb("junk", [128, D], BF16)
    stat = P.sb("stat", [128, 8], F32)
    xnT = P.sb("xnT", [128, 8, 512], BF16)
    sq = P.sb("sq", [128, 512], F32)
    ss8 = P.sb("ss8", [128, 24], F32)
    t1 = P.sb("t1", [128, 8, 64], F32)
    qkn = P.sb("qkn", [128, 8, 64], BF16)
    QT = P.sb("QT", [128, 4, 512], BF16)
    s.op("pool", lambda e: e.memset(QT[:], 0.0), writes=["QT"])
    zf = P.sb("zf", [128, 16], F32)
    Ltok = P.sb("Ltok", [128, NTL, 4], F32)
    R = P.sb("R", [128, 4], F32)
    LT = P.sb("LT", [4, 512], F32)
    nLb = [P.sb(f"nLb{h}", [128, 512], F32) for h in range(4)]
    tmp = [P.sb(f"tmp{i}", [128, 512], F32) for i in range(4)]
    E = [P.sb(f"E{i}", [128, 512], BF16) for i in range(5)]
    rden = P.sb("rden", [64, 512], F32)
    oTh = [P.sb(f"oTh{i}", [64, 512], BF16) for i in range(2)]

    ps_qk = P.ps("ps_qk", [128, 512])
    ps_t = P.ps("ps_t", [128, 8, 128], BF16)
    ps_s = [P.ps(f"ps_s{i}", [128, 512]) for i in range(4)]
    ps_n = P.ps("ps_n", [128, 512])
    ps_m = P.ps("ps_m", [128, 512])
    ps_vf = ps_m[:, 252:512]

    x_v = x.rearrange("(n p) d -> n p d", p=128)
    pair_ctr = 0
    bgw = list(P.io.get("bg_work", []))
    per_it = (len(bgw) + 2 * NQC - 3) // max(2 * NQC - 2, 1)
    for hp in range(2):
        for kc in range(8):
            i = kc % 2
            s.dma("sp", lambda q, i=i, kc=kc, hp=hp: q.dma_start(out=wstage[i][:, 0:512], in_=wA[hp, kc * 128:(kc + 1) * 128, :]),
                  writes=[f"wstage{i}"])
            s.op("pool", lambda e, i=i, kc=kc: e.tensor_copy(out=wA_bf[:, kc, :], in_=wstage[i][:, 0:512]),
                 reads=[f"wstage{i}"], writes=["wA_bf"])
        for kc in range(8):
            i = kc % 2
            s.dma("sp", lambda q, i=i, kc=kc, hp=hp: q.dma_start(out=wstage[i][:, 0:260], in_=wB[hp, kc * 128:(kc + 1) * 128, :]),
                  writes=[f"wstage{i}"])
            s.op("pool", lambda e, i=i, kc=kc: e.tensor_copy(out=wB_bf[:, kc, :], in_=wstage[i][:, 0:260]),
                 reads=[f"wstage{i}"], writes=["wB_bf"])
        s.op("pool", lambda e: e.memset(R[:], 0.0), writes=["R"])
        for qc in range(NQC):
            for i in range(4):
                tg = qc * 4 + i
                par = tg % 2
                s.dma("sp", lambda q, par=par, tg=tg: q.dma_start(out=x_sb[par][:], in_=x_v[tg]), writes=[f"x_sb{par}"])
                norm_transpose_g(P, x_sb[par][:], f"x_sb{par}", xnT, "xnT", i * 128, gl_sb, "gl", xs[par], f"xs{par}",
                                 junk, stat, ps_t, ident)
            for i in range(4):
                tg = qc * 4 + i
                tc_ = slice(i * 128, (i + 1) * 128)
                for kc in range(8):
                    s.op("pe", lambda e, kc=kc, tc_=tc_: e.matmul(ps_qk[:], lhsT=xnT[:, kc, tc_], rhs=wA_bf[:, kc, :],
                                                                  start=(kc == 0), stop=(kc == 7)),
                         reads=["xnT", "wA_bf"], writes=["ps_qk"])
                for kc in range(8):
                    s.op("pe", lambda e, kc=kc, tc_=tc_: e.matmul(ps_vf[:, 0:260], lhsT=xnT[:, kc, tc_], rhs=wB_bf[:, kc, :],
                                                                  start=(kc == 0), stop=(kc == 7)),
                         reads=["xnT", "wB_bf"], writes=["ps_m"])
                s.op("act", lambda e: e.activation(out=sq[:], in_=ps_qk[:], func=AF.Square), reads=["ps_qk"], writes=["sq"])
                s.op("dve", lambda e: e.tensor_reduce(out=ss8[:, 0:8], in_=sq[:].rearrange("p (h d) -> p h d", h=8),
                                                      axis=AX.X, op=ALU.add), reads=["sq"], writes=["ss8a"])
                rms_rstd(s, P, ss8[:, 0:8], ss8[:, 16:24], DH, "ss8a", "ss8c", ss8[:, 8:16], "ss8b")
                s.op("dve", lambda e: e.tensor_tensor(out=t1[:], in0=ps_qk[:].rearrange("p (h d) -> p h d", h=8),
                                                      in1=ss8[:, 16:24].unsqueeze(2).to_broadcast([128, 8, 64]), op=ALU.mult),
                     reads=["ps_qk", "ss8c"], writes=["t1"])
                s.op("pool", lambda e: e.tensor_tensor(out=qkn[:], in0=t1[:], in1=G8[:], op=ALU.mult),
                     reads=["t1", "G8"], writes=["qkn"])
                for pr in range(4):
                    s.op("pe", lambda e, pr=pr: e.transpose(out=ps_t[:, pr, :], in_=qkn[:, 2 * pr:2 * pr + 2, :].rearrange("p h d -> p (h d)"),
                                                            identity=ident[:]), reads=["qkn", "ident"], writes=["ps_t"])
                for pr_ in range(2):
                    for hh_ in range(2):
                        s.op("act", lambda e, tc_=tc_, pr_=pr_, hh_=hh_: e.activation(
                            out=QT[hh_ * 64:(hh_ + 1) * 64, 2 * pr_ + hh_, tc_], in_=ps_t[hh_ * 64:(hh_ + 1) * 64, pr_, :], func=AF.Copy),
                            reads=["ps_t"], writes=["QT"])
                s.op("act", lambda e, tg=tg: e.activation(out=KT[:, :, tg * 128:(tg + 1) * 128], in_=ps_t[:, 2:4, :], func=AF.Copy),
                     reads=["ps_t"], writes=["KT"])
                s.op("act", lambda e, tg=tg: e.activation(out=V[:, tg, :, 0:64], in_=ps_vf[:, 0:256].rearrange("p (h d) -> p h d", h=4), func=AF.Copy),
                     reads=["ps_m"], writes=["V"])
                s.op("dve", lambda e, hp=hp: e.tensor_tensor(out=zf[:, 0:4], in0=ps_vf[:, 256:260], in1=bfl_sb[:, hp * 4:hp * 4 + 4], op=ALU.add),
                     reads=["ps_m", "bfl"], writes=["zf0"])
                s.op("act", lambda e: e.activation(out=zf[:, 4:8], in_=zf[:, 0:4], func=AF.Exp, scale=-1.0), reads=["zf0"], writes=["zf1"])
                s.op("act", lambda e: e.activation(out=zf[:, 8:12], in_=zf[:, 4:8], func=AF.Ln, bias=1.0), reads=["zf1"], writes=["zf2"])
                s.op("pe", lambda e: e.matmul(ps_m[:, 0:4], lhsT=tri[:], rhs=zf[:, 8:12], start=True, stop=True),
                     reads=["tri", "zf2"], writes=["ps_m"])
                s.op("pe", lambda e: e.matmul(ps_m[:, 4:8], lhsT=onesf[:], rhs=zf[:, 8:12], start=True, stop=True),
                     reads=["onesf", "zf2"], writes=["ps_m"])
                s.op("dve", lambda e, tg=tg: e.tensor_tensor(out=Ltok[:, tg, :], in0=ps_m[:, 0:4], in1=R[:], op=ALU.add),
                     reads=["ps_m", "R"], writes=[f"Ltok{tg}"])
                s.op("dve", lambda e: e.tensor_tensor(out=R[:], in0=ps_m[:, 4:8], in1=R[:], op=ALU.add),
                     reads=["ps_m", "R"], writes=["R"])
                s.op("pe", lambda e, tg=tg: e.transpose(out=ps_m[0:4, 16:144], in_=Ltok[:, tg, :], identity=identf[:]),
                     reads=[f"Ltok{tg}", "identf"], writes=["ps_m"])
                s.op("act", lambda e, tc_=tc_: e.activation(out=LT[:, tc_], in_=ps_m[0:4, 16:144], func=AF.Copy),
                     reads=["ps_m"], writes=["LT"])
            for h in range(4):
                s.op("pe", lambda e, h=h: e.matmul(ps_m[:], lhsT=selneg[:, h, :], rhs=LT[:], start=True, stop=True),
                     reads=["selneg", "LT"], writes=["ps_m"])
                s.op("act", lambda e, h=h: e.activation(out=nLb[h][:], in_=ps_m[:], func=AF.Copy), reads=["ps_m"], writes=[f"nLb{h}"])
            for _ in range(per_it):
                if bgw:
                    bgw.pop(0)()
            nkt = 4 * qc + 4
            items = [(h, kt) for h in range(4) for kt in range(nkt)]
            LOOK = 3
            binfo = {}

            def stage_a(ix):
                nonlocal pair_ctr
                h, kt = items[ix]
                pr, hb = h // 2, (h % 2) * 64
                r = kt - 4 * qc
                cs = 128 * r if r > 0 else 0
                cols = slice(cs, 512)
                pi, ti, ei = pair_ctr % 4, pair_ctr % 4, pair_ctr % 5
                pair_ctr += 1
                pss, tm, Et = ps_s[pi], tmp[ti], E[ei]
                binfo[ix] = (Et, ei, cols)
                s.op("pe", lambda e: e.matmul(pss[:, cols], lhsT=KT[:, pr, kt * 128:(kt + 1) * 128], rhs=QT[:, h, cols],
                                              start=True, stop=True), reads=["KT", "QT"], writes=[f"ps_s{pi}"])
                s.op("dve", lambda e: e.tensor_tensor(out=tm[:, cols], in0=pss[:, cols], in1=nLb[h][:, cols], op=ALU.add),
                     reads=[f"ps_s{pi}", f"nLb{h}"], writes=[f"tmp{ti}"])
                if r >= 0:
                    s.op("pool", lambda e: e.affine_select(out=tm[:, cs:cs + 128], in_=tm[:, cs:cs + 128], pattern=[[1, 128]],
                                                           compare_op=ALU.is_ge, fill=fill_reg(e, -30000.0), base=0, channel_multiplier=-1),
                         reads=[f"tmp{ti}"], writes=[f"tmp{ti}"])
                s.op("act", lambda e: e.activation(out=Et[:, cols], in_=tm[:, cols], func=AF.Exp, bias=Ltok[:, kt, h:h + 1]),
                     reads=[f"tmp{ti}", f"Ltok{kt}"], writes=[f"E{ei}"])

            def stage_b(ix):
                h, kt = items[ix]
                Et, ei, cols = binfo.pop(ix)
                first, last, qc_ = (kt == 0), (kt == nkt - 1), qc
                s.op("pe", lambda e: e.matmul(ps_n[:, cols], lhsT=V[:, kt, h, :], rhs=Et[:, cols], start=first, stop=last),
                     reads=[f"E{ei}", "V"], writes=["ps_n"])
                if last:
                    oi = h % 2
                    s.op("act", lambda e: e.activation(out=rden[:], in_=ps_n[64:128, :], func=AF.Copy), reads=["ps_n"], writes=["rden"])
                    s.op("dve", lambda e: e.reciprocal(out=rden[:], in_=rden[:]), reads=["rden"], writes=["rden"])
                    s.op("dve", lambda e: e.tensor_tensor(out=oTh[oi][:], in0=ps_n[0:64, :], in1=rden[:], op=ALU.mult),
                         reads=["ps_n", "rden"], writes=[f"oTh{oi}"])
                    hg = hp * 4 + h
                    s.dma("sp", lambda q: q.dma_start(out=oT_l[hg // 2][(hg % 2) * 64:(hg % 2 + 1) * 64, qc_ * 512:(qc_ + 1) * 512], in_=oTh[oi][:]),
                          reads=[f"oTh{oi}"], final=True)

            for ix in range(len(items) + LOOK):
                if ix < len(items):
                    stage_a(ix)
                if ix >= LOOK:
                    stage_b(ix - LOOK)
    while bgw:
        bgw.pop(0)()
    return P.finish()


def rope_tm(s, eng2, src, src_key, dst, dst_key, nh, cos_ap, sin_ap, tabkey, rt, rtkey):
    cb = cos_ap.unsqueeze(1).to_broadcast([128, nh, 8])
    sb_ = sin_ap.unsqueeze(1).to_broadcast([128, nh, 8])
    n8 = nh * 8

    def v(i):
        return rt[:, i, 0:n8].rearrange("p (h d) -> p h d", h=nh)

    x1 = src[:, :, 0:8]
    x2 = src[:, :, 8:16]
    s.op("dve", lambda e: e.tensor_copy(out=dst, in_=src), reads=[src_key], writes=[dst_key])
    s.op("dve", lambda e: e.tensor_tensor(out=v(0), in0=x1, in1=cb, op=ALU.mult), reads=[src_key, tabkey], writes=[rtkey + "0"])
    s.op("dve", lambda e: e.tensor_tensor(out=v(1), in0=x2, in1=sb_, op=ALU.mult), reads=[src_key, tabkey], writes=[rtkey + "1"])
    s.op("dve", lambda e: e.tensor_tensor(out=dst[:, :, 0:8], in0=v(0), in1=v(1), op=ALU.subtract),
         reads=[rtkey + "0", rtkey + "1"], writes=[dst_key])
    s.op("dve", lambda e: e.tensor_tensor(out=v(2), in0=x2, in1=cb, op=ALU.mult), reads=[src_key, tabkey], writes=[rtkey + "2"])
    s.op("dve", lambda e: e.tensor_tensor(out=v(3), in0=x1, in1=sb_, op=ALU.mult), reads=[src_key, tabkey], writes=[rtkey + "3"])
    s.op("dve", lambda e: e.tensor_tensor(out=dst[:, :, 8:16], in0=v(2), in1=v(3), op=ALU.add),
         reads=[rtkey + "2", rtkey + "3"], writes=[dst_key])


def sincos_tab(P, posf, n, invf_sb, cosT, sinT, name):
    s = P.s
    ang = P.sb(name + "_ang", [128, n, 8], F32)
    ki = P.sb(name + "_ki", [128, n, 8], I32)
    kf = P.sb(name + "_kf", [128, n, 8], F32)
    TWO_PI = 2.0 * np.pi
    for i in range(8):
        s.op("dve", lambda e, i=i: e.tensor_scalar(out=ang[:, :, i], in0=posf, scalar1=invf_sb[:, i:i + 1], scalar2=None, op0=ALU.mult),
             reads=[name + "_posf", "invf"], writes=[name + "_ang"])
    s.op("dve", lambda e: e.tensor_scalar(out=kf[:], in0=ang[:], scalar1=1.0 / TWO_PI, scalar2=None, op0=ALU.mult),
         reads=[name + "_ang"], writes=[name + "_kf"])
    s.op("dve", lambda e: e.tensor_copy(out=ki[:], in_=kf[:]), reads=[name + "_kf"], writes=[name + "_ki"])
    s.op("dve", lambda e: e.tensor_copy(out=kf[:], in_=ki[:]), reads=[name + "_ki"], writes=[name + "_kf"])
    s.op("dve", lambda e: e.scalar_tensor_tensor(out=ang[:], in0=kf[:], scalar=-TWO_PI, in1=ang[:], op0=ALU.mult, op1=ALU.add),
         reads=[name + "_kf", name + "_ang"], writes=[name + "_ang"])
    s.op("dve", lambda e: e.tensor_scalar(out=kf[:], in0=ang[:], scalar1=float(np.pi), scalar2=None, op0=ALU.is_gt),
         reads=[name + "_ang"], writes=[name + "_kf"])
    s.op("dve", lambda e: e.scalar_tensor_tensor(out=ang[:], in0=kf[:], scalar=-TWO_PI, in1=ang[:], op0=ALU.mult, op1=ALU.add),
         reads=[name + "_kf", name + "_ang"], writes=[name + "_ang"])
    s.op("dve", lambda e: e.tensor_scalar(out=kf[:], in0=ang[:], scalar1=-float(np.pi), scalar2=None, op0=ALU.is_lt),
         reads=[name + "_ang"], writes=[name + "_kf"])
    s.op("dve", lambda e: e.scalar_tensor_tensor(out=ang[:], in0=kf[:], scalar=TWO_PI, in1=ang[:], op0=ALU.mult, op1=ALU.add),
         reads=[name + "_kf", name + "_ang"], writes=[name + "_ang"])
    s.op("dve", lambda e: e.tensor_scalar(out=ang[:], in0=ang[:], scalar1=3.1415925, scalar2=-3.1415925, op0=ALU.min, op1=ALU.max),
         reads=[name + "_ang"], writes=[name + "_ang"])
    s.op("act", lambda e: e.activation(out=sinT[:], in_=ang[:], func=AF.Sin), reads=[name + "_ang"], writes=[name + "_tab"])
    s.op("dve", lambda e: e.tensor_scalar(out=kf[:], in0=ang[:], scalar1=-1.0, scalar2=None, op0=ALU.mult),
         reads=[name + "_ang"], writes=[name + "_kf"])
    s.op("dve", lambda e: e.tensor_tensor(out=kf[:], in0=kf[:], in1=ang[:], op=ALU.max),
         reads=[name + "_ang", name + "_kf"], writes=[name + "_kf"])
    s.op("dve", lambda e: e.tensor_scalar(out=kf[:], in0=kf[:], scalar1=-1.0, scalar2=float(np.pi / 2), op0=ALU.mult, op1=ALU.add),
         reads=[name + "_kf"], writes=[name + "_kf"])
    s.op("act", lambda e: e.activation(out=cosT[:], in_=kf[:], func=AF.Sin), reads=[name + "_kf"], writes=[name + "_tab"])


def build_nsa(P, TT):
    nc, s = P.nc, P.s
    NTL = TT // 128
    NQC = TT // 512
    NCB = TT // 16
    NCT = NCB // 128
    NVB = (TT - 32) // 16 + 1
    xnT_all_l = P.io["xnT_all_l"]
    msk = P.din("msk", [128, 2])
    wq = P.din("wq", [D, 512])
    wg = P.din("wg", [D, 24])
    bgl = P.din("bgl", [128, 24])
    qg = P.din("qg", [128, 64])
    kv_all_l = P.io["kv_all_l"]
    pos = P.din("pos", [128, NTL], I32)
    pose = P.din("pose", [128, NCT], I32)
    kgains = P.din("kgains", [128, 3, 64])
    invf = P.din("invf", [128, 8])
    peT = P.din("peT", [128, 32])
    w1 = P.din("w1", [128, 32, 256])
    w2 = P.din("w2", [128, 2, 2, 64])
    oT_l = P.io["oT_l"]

    ident = make_ident(P, BF16, "ident")
    identf = make_ident(P, F32, "identf")
    small = {}
    for nm, ap_, shp, dt in (("msk", msk, [128, 2], F32), ("bgl", bgl, [128, 24], F32), ("qg", qg, [128, 64], F32),
                             ("kgains", kgains, [128, 3, 64], F32), ("invf", invf, [128, 8], F32),
                             ("pos", pos, [128, NTL], I32), ("pose", pose, [128, NCT], I32), ("peT", peT, [128, 32], F32),
                             ("w2", w2, [128, 2, 2, 64], F32)):
        t_ = P.sb(nm + "_sb", shp, dt)
        s.dma("sp", lambda q, t_=t_, ap_=ap_: q.dma_start(out=t_[:], in_=ap_), writes=[nm])
        small[nm] = t_
    msk_sb, bgl_sb, qg_sb, kg_sb, invf_sb = small["msk"], small["bgl"], small["qg"], small["kgains"], small["invf"]
    G8 = P.sb("G8", [128, 8, 64], F32)
    for h in range(8):
        s.op("pool", lambda e, h=h: e.tensor_scalar(out=G8[:, h, :], in0=qg_sb[:], scalar1=0.125, scalar2=None, op0=ALU.mult),
             reads=["qg"], writes=["G8"])
    ones_bf = P.sb("ones_bf", [128, 128], BF16)
    s.op("pool", lambda e: e.memset(ones_bf[:], 1.0), writes=["ones_bf"])
    posf = P.sb("posf", [128, NTL], F32)
    s.op("dve", lambda e: e.tensor_copy(out=posf[:], in_=small["pos"][:]), reads=["pos"], writes=["tq_posf"])
    cosT = P.sb("cosT", [128, NTL, 8], F32)
    sinT = P.sb("sinT", [128, NTL, 8], F32)
    sincos_tab(P, posf[:], NTL, invf_sb, cosT, sinT, "tq")
    posef = P.sb("posef", [128, NCT], F32)
    s.op("dve", lambda e: e.tensor_copy(out=posef[:], in_=small["pose"][:]), reads=["pose"], writes=["te_posf"])
    cosE = P.sb("cosE", [128, NCT, 8], F32)
    sinE = P.sb("sinE", [128, NCT, 8], F32)
    sincos_tab(P, posef[:], NCT, invf_sb, cosE, sinE, "te")
    Gx = P.sb("Gx", [128, TT], BF16)
    s.op("pool", lambda e: e.memset(Gx[:], 1.0), writes=["Gx"])
    s.op("pool", lambda e: e.affine_select(out=Gx[:], in_=Gx[:], pattern=[[1, TT]], compare_op=ALU.is_ge, fill=0.0,
                                           base=0, channel_multiplier=-64), reads=["Gx"], writes=["Gx"])
    s.op("pool", lambda e: e.affine_select(out=Gx[:], in_=Gx[:], pattern=[[-1, TT]], compare_op=ALU.is_ge, fill=0.0,
                                           base=63, channel_multiplier=64), reads=["Gx"], writes=["Gx"])
    ovl = P.sb("ovl", [128, NCT, 128], BF16)
    oa = P.sb("oa", [128, NCT, 128], F32)
    ob = P.sb("ob", [128, NCT, 128], F32)
    oc = P.sb("oc", [128, NCT, 128], F32)
    s.op("pool", lambda e: e.iota(out=oa[:], pattern=[[2048, NCT], [0, 128]], base=0, channel_multiplier=16,
                                  allow_small_or_imprecise_dtypes=True), writes=["oa"])
    s.op("pool", lambda e: e.iota(out=ob[:], pattern=[[0, NCT], [64, 128]], base=0, channel_multiplier=0,
                                  allow_small_or_imprecise_dtypes=True), writes=["ob"])
    s.op("dve", lambda e: e.tensor_tensor(out=oc[:], in0=oa[:], in1=ob[:], op=ALU.max), reads=["oa", "ob"], writes=["oc"])
    s.op("dve", lambda e: e.tensor_scalar(out=oa[:], in0=oa[:], scalar1=32.0, scalar2=None, op0=ALU.add), reads=["oa"], writes=["oa"])
    s.op("dve", lambda e: e.tensor_scalar(out=ob[:], in0=ob[:], scalar1=64.0, scalar2=None, op0=ALU.add), reads=["ob"], writes=["ob"])
    s.op("dve", lambda e: e.tensor_tensor(out=oa[:], in0=oa[:], in1=ob[:], op=ALU.min), reads=["oa", "ob"], writes=["oa"])
    s.op("dve", lambda e: e.tensor_tensor(out=oa[:], in0=oa[:], in1=oc[:], op=ALU.subtract), reads=["oa", "oc"], writes=["oa"])
    s.op("dve", lambda e: e.tensor_scalar(out=ovl[:], in0=oa[:], scalar1=0.0, scalar2=1.0 / 32, op0=ALU.max, op1=ALU.mult),
         reads=["oa"], writes=["ovl"])
    selg = P.sb("selg", [24, 24, 64], F32)
    s.op("pool", lambda e: e.memset(selg[:], 1.0), writes=["selg"])
    s.op("pool", lambda e: e.affine_select(out=selg[:], in_=selg[:], pattern=[[1, 24], [0, 64]], compare_op=ALU.is_equal,
                                           fill=0.0, base=0, channel_multiplier=-1), reads=["selg"], writes=["selg"])

    ksT = P.sb("ksT", [128, TT], BF16)
    kwT = P.sb("kwT", [128, TT], BF16)
    VS = P.sb("VS", [128, NTL, 64], BF16)
    VW = P.sb("VW", [128, NTL, 64], BF16)
    kcT = P.sb("kcT", [128, NCB], BF16)
    VC = P.sb("VC", [128, NCT, 64], BF16)
    shared2 = P.sb("shared2", [128, 16384], BF16)
    rawT = shared2[:, 0:TT]
    w1_sb = shared2[:, 8192:16384].rearrange("p (l n) -> p l n", l=32)
    w2_sb = P.sb("w2_bf", [128, 2, 2, 64], BF16)
    peT_bf = P.sb("peT_bf", [128, 32], BF16)
    hidT = P.sb("hidT", [128, 2, NCB], BF16)
    wstage = [P.sb(f"wstage{i}", [128, 1024], F32) for i in range(2)]
    kv_sb = [P.sb(f"kv_sb{i}", [128, 6, 64], F32) for i in range(2)]
    kv_b = [P.sb(f"kv_b{i}", [128, 6, 64], F32) for i in range(2)]
    xs = [shared2[:, 4096 + 1024 * i:4096 + 1024 * (i + 1)] for i in range(2)]
    junk = shared2[:, 6144:7168]
    stat = P.sb("stat", [128, 8], F32)
    xnT = shared2[:, 0:4096].rearrange("p (k t) -> p k t", k=8)
    wq_bf = P.sb("wq_bf", [128, 8, 512], BF16)
    wg_bf = P.sb("wg_bf", [128, 8, 24], BF16)
    sq = P.sb("sq", [128, 512], F32)
    ss8 = P.sb("ss8", [128, 24], F32)
    t1 = P.sb("t1", [128, 8, 64], F32)
    t2 = P.sb("t2", [128, 8, 64], F32)
    rt = P.sb("rt", [128, 4, 64], F32)
    qkn = P.sb("qkn", [128, 8, 64], BF16)
    kdup = P.sb("kdup", [128, 2, 2, 64], BF16)
    QT = P.sb("QTz", [128, 8, 512], BF16)
    zg = P.sb("zg", [128, 3, 24], F32)
    gT = P.sb("gT", [24, 512], F32)
    Ebuf = [shared2[:, 9216 + 512 * i:9216 + 512 * (i + 1)] for i in range(4)]
    Em = [shared2[:, 11264 + 512 * i:11264 + 512 * (i + 1)] for i in range(3)]
    rdenb = P.sb("rdenb", [128, 512], F32)
    rden = P.sb("rden", [64, 512], F32)
    fgate = P.sb("fgate", [64, 512], F32)
    contrib = P.sb("contrib", [64, 512], F32)
    shared1 = P.sb("shared1", [128, 8, 512], F32)
    oacc = [shared1[0:64, h, :] for h in range(8)]
    oTh = [P.sb(f"oTh{i}", [64, 512], BF16) for i in range(2)]
    impT = P.sb("impT", [128, 512], F32)
    sc = P.sb("sc", [128, 128], F32)
    sc2 = P.sb("sc2", [128, 128], F32)
    m8 = P.sb("m8", [128, 16], F32)
    selb = P.sb("selb", [128, 128], BF16)
    selT = P.sb("selT", [128, 512], BF16)
    gx = shared1
    biasH = P.sb("biasH", [128, 4], F32)

    ps_s = [P.ps(f"ps_s{i}", [128, 512]) for i in range(3)]
    ps_M = [P.ps(f"ps_M{i}", [128, 512]) for i in range(2)]
    ps_t = ps_s[2][:].bitcast(BF16).rearrange("p (k t) -> p k t", k=8)
    ps_n = P.ps("ps_n", [64, 512])
    ps_d = P.ps("ps_d", [128, 512])
    ps_m = P.ps("ps_m", [128, 512])

    for kc in range(8):
        i = kc % 2
        s.dma("sp", lambda q, i=i, kc=kc: q.dma_start(out=wstage[i][:, 0:512], in_=wq[kc * 128:(kc + 1) * 128, :]), writes=[f"wstage{i}"])
        s.op("pool", lambda e, i=i, kc=kc: e.tensor_copy(out=wq_bf[:, kc, :], in_=wstage[i][:, 0:512]), reads=[f"wstage{i}"], writes=["wq_bf"])
    for kc in range(8):
        i = kc % 2
        s.dma("sp", lambda q, i=i, kc=kc: q.dma_start(out=wstage[i][:, 0:24], in_=wg[kc * 128:(kc + 1) * 128, :]), writes=[f"wstage{i}"])
        s.op("pool", lambda e, i=i, kc=kc: e.tensor_copy(out=wg_bf[:, kc, :], in_=wstage[i][:, 0:24]), reads=[f"wstage{i}"], writes=["wg_bf"])
    for l4 in range(8):
        i = l4 % 2
        s.dma("sp", lambda q, i=i, l4=l4: q.dma_start(out=wstage[i][:, 0:1024].rearrange("p (l n) -> p l n", l=4), in_=w1[:, l4 * 4:(l4 + 1) * 4, :]),
              writes=[f"wstage{i}"])
        s.op("pool", lambda e, i=i, l4=l4: e.tensor_copy(out=w1_sb[:, l4 * 4:(l4 + 1) * 4, :],
                                                       in_=wstage[i][:, 0:1024].rearrange("p (l n) -> p l n", l=4)),
             reads=[f"wstage{i}"], writes=["w1"])
    s.op("pool", lambda e: e.tensor_copy(out=w2_sb[:], in_=small["w2"][:]), reads=["w2"], writes=["w2b"])
    s.op("pool", lambda e: e.tensor_copy(out=peT_bf[:], in_=small["peT"][:]), reads=["peT"], writes=["peTb"])
    s.op("pool", lambda e: e.memset(hidT[:], 0.0), writes=["hidT"])

    def head_norm(src_ps, src_key, nslots, gain_ap, gain_key, dst, dst_key):
        w = nslots * 64
        s.op("act", lambda e: e.activation(out=sq[:, 0:w], in_=src_ps, func=AF.Square), reads=[src_key], writes=["sq"])
        s.op("dve", lambda e: e.tensor_reduce(out=ss8[:, 0:nslots], in_=sq[:, 0:w].rearrange("p (h d) -> p h d", h=nslots),
                                              axis=AX.X, op=ALU.add), reads=["sq"], writes=["ss8a"])
        rms_rstd(s, P, ss8[:, 0:nslots], ss8[:, 16:16 + nslots], DH, "ss8a", "ss8c", ss8[:, 8:8 + nslots], "ss8b")
        s.op("dve", lambda e: e.tensor_tensor(out=t1[:, 0:nslots, :], in0=src_ps.rearrange("p (h d) -> p h d", h=nslots),
                                              in1=ss8[:, 16:16 + nslots].unsqueeze(2).to_broadcast([128, nslots, 64]), op=ALU.mult),
             reads=[src_key, "ss8c"], writes=["t1"])
        s.op("pool", lambda e: e.tensor_tensor(out=dst, in0=t1[:, 0:nslots, :], in1=gain_ap, op=ALU.mult),
             reads=["t1", gain_key], writes=[dst_key])

    HT = NTL // 2
    QT_ = HT // 4

    def kv_src(g, rk, tl):
        r0_ = rk * (QT_ * 128) + (tl % QT_) * 128
        return kv_all_l[g * 4 + tl // QT_][r0_:r0_ + 128, :]
    for tg in range(NTL):
        par = tg % 2
        kvt = kv_sb[par]
        rk, tl = tg // HT, tg % HT
        kvb = kv_b[par]
        s.dma("sp", lambda q, kvt=kvt, rk=rk, tl=tl: q.dma_start(out=kvt[:].rearrange("p a d -> p (a d)"), in_=kv_src(0, rk, tl)),
              writes=[f"kv_sb{par}"])
        s.dma("sp", lambda q, kvb=kvb, rk=rk, tl=tl: q.dma_start(out=kvb[:].rearrange("p a d -> p (a d)"), in_=kv_src(1, rk, tl)),
              writes=[f"kv_b{par}"])
        s.op("dve", lambda e, kvt=kvt: e.tensor_scalar(out=kvt[:], in0=kvt[:], scalar1=msk_sb[:, 0:1], scalar2=None, op0=ALU.mult),
             reads=[f"kv_sb{par}", "msk"], writes=[f"kv_sb{par}"])
        s.op("dve", lambda e, kvt=kvt, kvb=kvb: e.scalar_tensor_tensor(out=kvt[:], in0=kvb[:], scalar=msk_sb[:, 1:2], in1=kvt[:],
                                                                     op0=ALU.mult, op1=ALU.add),
             reads=[f"kv_sb{par}", f"kv_b{par}", "msk"], writes=[f"kv_sb{par}"])
        for wi, part, gi in ((0, 2, 1), (1, 4, 2)):
            head_norm(kvt[:, part, :], f"kv_sb{par}", 1, kg_sb[:, gi:gi + 1, :], "kgains", t2[:, 0:1, :], "t2")
            rope_tm(s, None, t2[:, 0:1, :], "t2", kdup[:, wi, 0:1, :], f"kdup{wi}", 1, cosT[:, tg, :], sinT[:, tg, :], "tq_tab", rt, "rt")
            s.op("pool", lambda e, wi=wi: e.tensor_copy(out=kdup[:, wi, 1, :], in_=kdup[:, wi, 0, :]), reads=[f"kdup{wi}"], writes=[f"kdup{wi}"])
            s.op("pe", lambda e, wi=wi: e.transpose(out=ps_t[:, wi, :], in_=kdup[:, wi, :, :].rearrange("p a d -> p (a d)"), identity=ident[:]),
                 reads=[f"kdup{wi}", "ident"], writes=["ps_s2"])
        s.op("act", lambda e, tg=tg: e.activation(out=ksT[:, tg * 128:(tg + 1) * 128], in_=ps_t[:, 0, :], func=AF.Copy), reads=["ps_s2"], writes=["ksT"])
        s.op("act", lambda e, tg=tg: e.activation(out=kwT[:, tg * 128:(tg + 1) * 128], in_=ps_t[:, 1, :], func=AF.Copy), reads=["ps_s2"], writes=["kwT"])
        s.op("pool", lambda e, kvt=kvt, tg=tg: e.tensor_copy(out=VS[:, tg, :], in_=kvt[:, 3, :]), reads=[f"kv_sb{par}"], writes=["VS"])
        s.op("pool", lambda e, kvt=kvt, tg=tg: e.tensor_copy(out=VW[:, tg, :], in_=kvt[:, 5, :]), reads=[f"kv_sb{par}"], writes=["VW"])
        s.op("pool", lambda e, kvt=kvt: e.tensor_copy(out=qkn[:, 0:2, :], in_=kvt[:, 0:2, :]), reads=[f"kv_sb{par}"], writes=["qkn"])
        s.op("pe", lambda e: e.transpose(out=ps_t[:, 2, :], in_=qkn[:, 0:2, :].rearrange("p a d -> p (a d)"), identity=ident[:]),
             reads=["qkn", "ident"], writes=["ps_s2"])
        s.op("act", lambda e, tg=tg: e.activation(out=rawT[:, tg * 128:(tg + 1) * 128], in_=ps_t[:, 2, :], func=AF.Copy), reads=["ps_s2"], writes=["rawT"])

    for which in range(2):
        pb = which * 64
        for hc in range(2):
            for l in range(32):
                s.op("pe", lambda e, l=l, hc=hc, pb=pb, which=which: e.matmul(
                    ps_m[:, which * 2 + hc:which * 2 + hc + 1], lhsT=w1_sb[pb:pb + 64, l, hc * 128:(hc + 1) * 128],
                    rhs=peT_bf[pb:pb + 64, l:l + 1], start=(l == 0), stop=(l == 31)), reads=["w1", "peTb"], writes=["ps_m"])
            s.op("act", lambda e, which=which, hc=hc: e.activation(out=biasH[:, which * 2 + hc:which * 2 + hc + 1],
                                                                   in_=ps_m[:, which * 2 + hc:which * 2 + hc + 1], func=AF.Copy),
                 reads=["ps_m"], writes=["biasH"])
        for hc in range(2):
            pss = ps_s[hc]
            for l in range(32):
                s.op("pe", lambda e, l=l, hc=hc, pb=pb, pss=pss: e.matmul(
                    pss[:, 0:NVB], lhsT=w1_sb[pb:pb + 64, l, hc * 128:(hc + 1) * 128],
                    rhs=rawT[pb:pb + 64, l:l + 16 * (NVB - 1) + 1:16], start=(l == 0), stop=(l == 31)),
                    reads=["w1", "rawT"], writes=[f"ps_s{hc}"])
            xg, x2g, ug, thg = gx[:, 0, 0:NVB], gx[:, 1, 0:NVB], gx[:, 2, 0:NVB], gx[:, 3, 0:NVB]
            s.op("act", lambda e, pss=pss, which=which, hc=hc, xg=xg: e.activation(
                out=xg, in_=pss[:, 0:NVB], func=AF.Identity, bias=biasH[:, which * 2 + hc:which * 2 + hc + 1]),
                reads=[f"ps_s{hc}", "biasH"], writes=["gx0"])
            s.op("dve", lambda e, xg=xg, x2g=x2g: e.tensor_tensor(out=x2g, in0=xg, in1=xg, op=ALU.mult), reads=["gx0"], writes=["gx1"])
            s.op("dve", lambda e, x2g=x2g: e.tensor_scalar(out=x2g, in0=x2g, scalar1=0.044715, scalar2=1.0, op0=ALU.mult, op1=ALU.add),
                 reads=["gx1"], writes=["gx1"])
            s.op("dve", lambda e, xg=xg, x2g=x2g, ug=ug: e.tensor_tensor(out=ug, in0=x2g, in1=xg, op=ALU.mult), reads=["gx0", "gx1"], writes=["gx2"])
            s.op("act", lambda e, ug=ug, thg=thg: e.activation(out=thg, in_=ug, func=AF.Tanh, scale=0.7978845608028654), reads=["gx2"], writes=["gx3"])
            s.op("dve", lambda e, thg=thg: e.tensor_scalar(out=thg, in0=thg, scalar1=0.5, scalar2=0.5, op0=ALU.mult, op1=ALU.add),
                 reads=["gx3"], writes=["gx3"])
            s.op("dve", lambda e, thg=thg, xg=xg, hc=hc: e.tensor_tensor(out=hidT[:, hc, 0:NVB], in0=thg, in1=xg, op=ALU.mult),
                 reads=["gx3", "gx0"], writes=["hidT"])
        for ct in range(NCT):
            for hc in range(2):
                s.op("pe", lambda e, ct=ct, hc=hc, which=which: e.matmul(
                    ps_m[:, 64:128], lhsT=hidT[:, hc, ct * 128:(ct + 1) * 128], rhs=w2_sb[:, which, hc, :],
                    start=(hc == 0), stop=(hc == 1)), reads=["hidT", "w2b"], writes=["ps_m"])
            if which == 0:
                head_norm(ps_m[:, 64:128], "ps_m", 1, kg_sb[:, 0:1, :], "kgains", t2[:, 0:1, :], "t2")
                rope_tm(s, None, t2[:, 0:1, :], "t2", kdup[:, 0, 0:1, :], "kdup0", 1, cosE[:, ct, :], sinE[:, ct, :], "te_tab", rt, "rt")
                s.op("pool", lambda e: e.tensor_copy(out=kdup[:, 0, 1, :], in_=kdup[:, 0, 0, :]), reads=["kdup0"], writes=["kdup0"])
                s.op("pe", lambda e: e.transpose(out=ps_t[:, 0, :], in_=kdup[:, 0, :, :].rearrange("p a d -> p (a d)"), identity=ident[:]),
                     reads=["kdup0", "ident"], writes=["ps_s2"])
                s.op("act", lambda e, ct=ct: e.activation(out=kcT[:, ct * 128:(ct + 1) * 128], in_=ps_t[:, 0, :], func=AF.Copy),
                     reads=["ps_s2"], writes=["kcT"])
            else:
                s.op("act", lambda e, ct=ct: e.activation(out=VC[:, ct, :], in_=ps_m[:, 64:128], func=AF.Copy), reads=["ps_m"], writes=["VC"])

    s.op("pool", lambda e: e.memset(QT[:], 0.0), writes=["QT"])
    s.op("pool", lambda e: e.memset(biasH[:, 0:1], 0.0), writes=["biasH", "rawT", "w1", "gx0", "gx1", "gx2", "gx3", "xnT", "xs0", "xs1", "junk", "QT"]
         + [f"E{i}" for i in range(4)] + [f"Em{i}" for i in range(3)] + [f"oacc{h}" for h in range(8)])
    pc = [0]

    binfo = {}

    def attn_a(ix, h, kT, ktile, kkey, cols, mask_fn, sel_kt=None):
        pr, hb = h // 2, (h % 2) * 64
        pi = pc[0] % 3
        mi = pc[0] % 2
        ei = pc[0] % 3
        pc[0] += 1
        pss, Et = ps_s[pi], Em[ei]
        binfo[ix] = (Et, ei)
        s.op("pe", lambda e: e.matmul(pss[:, cols], lhsT=kT[:, ktile * 128:(ktile + 1) * 128], rhs=QT[:, h, cols],
                                      start=True, stop=True), reads=[kkey, "QT"], writes=[f"ps_s{pi}"])
        s.op("act", lambda e: e.activation(out=Et[:, cols], in_=pss[:, cols], func=AF.Exp), reads=[f"ps_s{pi}"], writes=[f"Em{ei}"])
        if sel_kt is not None:
            psM = ps_M[mi]
            s.op("pe", lambda e: e.matmul(psM[:, cols], lhsT=Gx[:, sel_kt * 128:(sel_kt + 1) * 128], rhs=selT[:, cols], start=True, stop=True),
                 reads=["Gx", "selT"], writes=[f"ps_M{mi}"])
            s.op("dve", lambda e: e.tensor_tensor(out=Et[:, cols], in0=Et[:, cols], in1=psM[:, cols], op=ALU.mult),
                 reads=[f"Em{ei}", f"ps_M{mi}"], writes=[f"Em{ei}"])
        if mask_fn is not None:
            mask_fn(Et, f"Em{ei}")

    def attn_b(ix, ktile, Vt, vkey, cols, first, last):
        Et, ei = binfo.pop(ix)
        s.op("pe", lambda e: e.matmul(ps_n[:, cols], lhsT=Vt[:, ktile, :], rhs=Et[:, cols], start=first, stop=last),
             reads=[f"Em{ei}", vkey], writes=["ps_n"])
        s.op("pe", lambda e: e.matmul(ps_d[0:64, cols], lhsT=ones_bf[:, 0:64], rhs=Et[:, cols], start=first, stop=last),
             reads=[f"Em{ei}", "ones_bf"], writes=["ps_d"])

    def finish_branch(h, br, first_branch, den_ap, den_key):
        idx = br * 8 + h
        s.op("dve", lambda e: e.tensor_scalar(out=rden[:], in0=den_ap, scalar1=1e-30, scalar2=None, op0=ALU.max), reads=[den_key], writes=["rden"])
        s.op("dve", lambda e: e.reciprocal(out=rden[:], in_=rden[:]), reads=["rden"], writes=["rden"])
        s.op("pe", lambda e: e.matmul(ps_m[0:64, :], lhsT=selg[:, idx, :], rhs=gT[:], start=True, stop=True), reads=["selg", "gT"], writes=["ps_m"])
        s.op("dve", lambda e: e.tensor_tensor(out=fgate[:], in0=ps_m[0:64, :], in1=rden[:], op=ALU.mult), reads=["ps_m", "rden"], writes=["fgate"])
        if first_branch:
            s.op("dve", lambda e: e.tensor_tensor(out=oacc[h][:], in0=ps_n[:], in1=fgate[:], op=ALU.mult), reads=["ps_n", "fgate"], writes=[f"oacc{h}"])
        else:
            s.op("dve", lambda e: e.tensor_tensor(out=contrib[:], in0=ps_n[:], in1=fgate[:], op=ALU.mult), reads=["ps_n", "fgate"], writes=["contrib"])
            s.op("pool", lambda e: e.tensor_tensor(out=oacc[h][:], in0=oacc[h][:], in1=contrib[:], op=ALU.add),
                 reads=[f"oacc{h}", "contrib"], writes=[f"oacc{h}"])

    def causal_mask(cs):
        def f(Et, ekey):
            s.op("pool", lambda e: e.affine_select(out=Et[:, cs:cs + 128], in_=Et[:, cs:cs + 128], pattern=[[1, 128]], compare_op=ALU.is_ge,
                                                   fill=0.0, base=0, channel_multiplier=-1), reads=[ekey], writes=[ekey])
        return f

    def winlow_mask(cs):
        def f(Et, ekey):
            s.op("pool", lambda e: e.affine_select(out=Et[:, cs:cs + 128], in_=Et[:, cs:cs + 128], pattern=[[-1, 128]], compare_op=ALU.is_ge,
                                                   fill=0.0, base=-1, channel_multiplier=1), reads=[ekey], writes=[ekey])
        return f

    for qc in range(NQC):
        t0 = qc * 512
        qh = NQC // 2
        for kc in range(8):
            rb = (qc // qh) * 256 + (kc % 2) * 128
            s.dma("sp", lambda q, qc=qc, qh=qh, kc=kc, rb=rb: q.dma_start(
                out=xnT[:, kc, :], in_=xnT_all_l[kc // 2][rb:rb + 128, (qc % qh) * 512:(qc % qh + 1) * 512]), writes=["xnT"])
        for i in range(4):
            tg = qc * 4 + i
            tc_ = slice(i * 128, (i + 1) * 128)
            for kc in range(8):
                s.op("pe", lambda e, kc=kc, tc_=tc_: e.matmul(ps_s[0][:], lhsT=xnT[:, kc, tc_], rhs=wq_bf[:, kc, :], start=(kc == 0), stop=(kc == 7)),
                     reads=["xnT", "wq_bf"], writes=["ps_s0"])
            for kc in range(8):
                s.op("pe", lambda e, kc=kc, tc_=tc_: e.matmul(ps_s[1][:, 0:24], lhsT=xnT[:, kc, tc_], rhs=wg_bf[:, kc, :], start=(kc == 0), stop=(kc == 7)),
                     reads=["xnT", "wg_bf"], writes=["ps_s1"])
            head_norm(ps_s[0][:], "ps_s0", 8, G8[:], "G8", t2[:], "t2")
            rope_tm(s, None, t2[:], "t2", qkn[:], "qkn", 8, cosT[:, tg, :], sinT[:, tg, :], "tq_tab", rt, "rt")
            for pr in range(4):
                s.op("pe", lambda e, pr=pr: e.transpose(out=ps_t[:, pr, :], in_=qkn[:, 2 * pr:2 * pr + 2, :].rearrange("p h d -> p (h d)"),
                                                        identity=ident[:]), reads=["qkn", "ident"], writes=["ps_s2"])
            for pr_ in range(4):
                for hh_ in range(2):
                    s.op("act", lambda e, tc_=tc_, pr_=pr_, hh_=hh_: e.activation(
                        out=QT[hh_ * 64:(hh_ + 1) * 64, 2 * pr_ + hh_, tc_], in_=ps_t[hh_ * 64:(hh_ + 1) * 64, pr_, :], func=AF.Copy),
                        reads=["ps_s2"], writes=["QT"])
            s.op("dve", lambda e: e.tensor_tensor(out=zg[:, 0, :], in0=ps_s[1][:, 0:24], in1=bgl_sb[:], op=ALU.add), reads=["ps_s1", "bgl"], writes=["zg0"])
            s.op("act", lambda e: e.activation(out=zg[:, 1, :], in_=zg[:, 0, :], func=AF.Exp, scale=-1.0), reads=["zg0"], writes=["zg1"])
            s.op("dve", lambda e: e.tensor_scalar(out=zg[:, 1, :], in0=zg[:, 1, :], scalar1=1.0, scalar2=None, op0=ALU.add), reads=["zg1"], writes=["zg1"])
            s.op("dve", lambda e: e.reciprocal(out=zg[:, 2, :], in_=zg[:, 1, :]), reads=["zg1"], writes=["zg2"])
            s.op("pe", lambda e: e.transpose(out=ps_m[0:24, 0:128], in_=zg[:, 2, :], identity=identf[:]), reads=["zg2", "identf"], writes=["ps_m"])
            s.op("act", lambda e, tc_=tc_: e.activation(out=gT[:, tc_], in_=ps_m[0:24, 0:128], func=AF.Copy), reads=["ps_m"], writes=["gT"])
        kts = [kt for kt in range(NCT) if 2048 * kt + 31 <= t0 + 511]
        nimp = len(kts) * 8
        impi = 0
        for h in range(8):
            pr, hb = h // 2, (h % 2) * 64
            for ii, kt in enumerate(kts):
                pi = pc[0] % 2
                pc[0] += 1
                pss, Et = ps_s[pi], Ebuf[ii]
                s.op("pe", lambda e, pss=pss, kt=kt, h=h: e.matmul(pss[:], lhsT=kcT[:, kt * 128:(kt + 1) * 128], rhs=QT[:, h, :],
                                                                        start=True, stop=True), reads=["kcT", "QT"], writes=[f"ps_s{pi}"])
                s.op("act", lambda e, pss=pss, Et=Et: e.activation(out=Et[:], in_=pss[:], func=AF.Exp), reads=[f"ps_s{pi}"], writes=[f"E{ii}"])
                if not (2048 * kt + 2063 <= t0):
                    s.op("pool", lambda e, Et=Et, kt=kt, t0=t0: e.affine_select(out=Et[:], in_=Et[:], pattern=[[1, 512]], compare_op=ALU.is_ge, fill=0.0,
                                                                       base=t0 - 2048 * kt - 31, channel_multiplier=-16),
                         reads=[f"E{ii}"], writes=[f"E{ii}"])
                s.op("pe", lambda e, Et=Et, kt=kt, ii=ii, nk=len(kts): e.matmul(ps_n[:], lhsT=VC[:, kt, :], rhs=Et[:], start=(ii == 0), stop=(ii == nk - 1)),
                     reads=[f"E{ii}", "VC"], writes=["ps_n"])
                s.op("pe", lambda e, Et=Et, ii=ii, nk=len(kts): e.matmul(ps_d[:], lhsT=ones_bf[:], rhs=Et[:], start=(ii == 0), stop=(ii == nk - 1)),
                     reads=[f"E{ii}", "ones_bf"], writes=["ps_d"])
            s.op("dve", lambda e: e.tensor_scalar(out=rdenb[:], in0=ps_d[:], scalar1=1e-30, scalar2=None, op0=ALU.max), reads=["ps_d"], writes=["rdenb"])
            s.op("dve", lambda e: e.reciprocal(out=rdenb[:], in_=rdenb[:]), reads=["rdenb"], writes=["rdenb"])
            for ii, kt in enumerate(kts):
                Et = Ebuf[ii]
                s.op("dve", lambda e, Et=Et: e.tensor_tensor(out=Et[:], in0=Et[:], in1=rdenb[:], op=ALU.mult), reads=[f"E{ii}", "rdenb"], writes=[f"E{ii}"])
                s.op("pe", lambda e, Et=Et, kt=kt, impi=impi, nimp=nimp: e.matmul(ps_M[0][:], lhsT=ovl[:, kt, :], rhs=Et[:], start=(impi == 0), stop=(impi == nimp - 1)),
                     reads=[f"E{ii}", "ovl"], writes=["ps_M0"])
                impi += 1
            finish_branch(h, 0, True, ps_d[0:64, :], "ps_d")
        s.op("act", lambda e: e.activation(out=impT[:], in_=ps_M[0][:], func=AF.Copy), reads=["ps_M0"], writes=["impT"])
        for i in range(4):
            tb = t0 + 128 * i
            s.op("pe", lambda e, i=i: e.transpose(out=ps_m[:, 128:256], in_=impT[:, i * 128:(i + 1) * 128], identity=identf[:]),
                 reads=["impT", "identf"], writes=["ps_m"])
            s.op("act", lambda e: e.activation(out=sc[:], in_=ps_m[:, 128:256], func=AF.Copy), reads=["ps_m"], writes=["sc"])
            s.op("pool", lambda e, tb=tb: e.affine_select(out=sc[:], in_=sc[:], pattern=[[-64, 128]], compare_op=ALU.is_ge,
                                                          fill=fill_reg(e, 1e6), base=tb - 128, channel_multiplier=1), reads=["sc"], writes=["sc"])
            s.op("pool", lambda e, tb=tb: e.affine_select(out=sc[:], in_=sc[:], pattern=[[-64, 128]], compare_op=ALU.is_ge,
                                                          fill=fill_reg(e, -1e30), base=tb, channel_multiplier=1), reads=["sc"], writes=["sc"])
            s.op("pool", lambda e: e.memset(sc[:, 0:1], 1e6), reads=["sc"], writes=["sc"])
            s.op("dve", lambda e: e.max(out=m8[:, 0:8], in_=sc[:]), reads=["sc"], writes=["m8a"])
            s.op("dve", lambda e: e.match_replace(out=sc2[:], in_to_replace=m8[:, 0:8], in_values=sc[:], imm_value=-3e38),
                 reads=["sc", "m8a"], writes=["sc2"])
            s.op("dve", lambda e: e.max(out=m8[:, 8:16], in_=sc2[:]), reads=["sc2"], writes=["m8b"])
            s.op("dve", lambda e: e.tensor_scalar(out=selb[:], in0=sc[:], scalar1=m8[:, 15:16], scalar2=None, op0=ALU.is_ge),
                 reads=["sc", "m8b"], writes=["selb"])
            s.op("pe", lambda e: e.transpose(out=ps_t[:, 0, :], in_=selb[:], identity=ident[:]), reads=["selb", "ident"], writes=["ps_s2"])
            s.op("act", lambda e, i=i: e.activation(out=selT[:, i * 128:(i + 1) * 128], in_=ps_t[:, 0, :], func=AF.Copy), reads=["ps_s2"], writes=["selT"])
        items = []
        nkt = 4 * qc + 4
        for h in range(8):
            for kt in range(nkt):
                r = kt - 4 * qc
                cs = 128 * r if r > 0 else 0
                items.append(("pair", h, ksT, "ksT", VS, "VS", kt, slice(cs, 512), kt == 0, kt == nkt - 1,
                              causal_mask(cs) if r >= 0 else None, kt))
            items.append(("fin", h, 1))
            kt_lo = max(0, 4 * qc - 4)
            kt_first = 4 * qc - 1 if qc >= 1 else 0
            worder = [kt_first] + [k_ for k_ in range(kt_lo, nkt) if k_ != kt_first]
            for wi_, kt in enumerate(worder):
                r = kt - 4 * qc
                if r >= 0:
                    cs = 128 * r
                    cols, mf = slice(cs, 512), causal_mask(cs)
                else:
                    ce = 128 * (r + 5)
                    cols, mf = slice(0, ce), winlow_mask(ce - 128)
                items.append(("pair", h, kwT, "kwT", VW, "VW", kt, cols, wi_ == 0, wi_ == len(worder) - 1, mf, None))
            items.append(("fin", h, 2))
            items.append(("out", h, qc))
        LOOK = 2

        def do_a(ix):
            it = items[ix]
            if it[0] == "pair":
                _, h, kT, kkey, Vt, vkey, kt, cols, first, last, mf, selkt = it
                attn_a(ix, h, kT, kt, kkey, cols, mf, sel_kt=selkt)

        def do_b(ix):
            it = items[ix]
            if it[0] == "pair":
                _, h, kT, kkey, Vt, vkey, kt, cols, first, last, mf, selkt = it
                attn_b(ix, kt, Vt, vkey, cols, first, last)
            elif it[0] == "fin":
                finish_branch(it[1], it[2], False, ps_d[0:64, :], "ps_d")
            else:
                h, qc_ = it[1], it[2]
                oi = h % 2
                s.op("act", lambda e: e.activation(out=oTh[oi][:], in_=oacc[h][:], func=AF.Copy), reads=[f"oacc{h}"], writes=[f"oTh{oi}"])
                s.dma("sp", lambda q: q.dma_start(out=oT_l[h // 2][(h % 2) * 64:(h % 2 + 1) * 64, qc_ * 512:(qc_ + 1) * 512], in_=oTh[oi][:]),
                      reads=[f"oTh{oi}"], final=True)

        for ix in range(len(items) + LOOK):
            if ix < len(items):
                do_a(ix)
            if ix >= LOOK:
                do_b(ix - LOOK)
    return P.finish()


def _lay_cw(conv_w, conv_b):
    a = np.concatenate([conv_w, conv_b[None]], 0)
    a = a.reshape(4, NFC, 128).transpose(2, 1, 0)
    return np.ascontiguousarray(a.reshape(128, NFC * 4)).astype(np.float32)


def _lay_g(g):
    return np.ascontiguousarray(np.asarray(g, np.float32).reshape(8, 128).T)


def _rep(v, n=128):
    v = np.asarray(v, np.float32)
    return np.ascontiguousarray(np.broadcast_to(v[None], (n,) + v.shape)).astype(np.float32)


def _fox_inputs(z, b, hh):
    w_in = z["a_w_in"][0]
    wA = np.zeros((2, D, 512), np.float32)
    wB = np.zeros((2, D, 260), np.float32)
    for hp in range(2):
        h0 = hh * 8 + hp * 4
        wA[hp, :, 0:256] = w_in[:, h0 * 64:(h0 + 4) * 64]
        wA[hp, :, 256:512] = w_in[:, 1024 + h0 * 64:1024 + (h0 + 4) * 64]
        wB[hp, :, 0:256] = w_in[:, 2048 + h0 * 64:2048 + (h0 + 4) * 64]
        wB[hp, :, 256:260] = w_in[:, 3072 + h0:3072 + h0 + 4]
    return {"x": np.ascontiguousarray(z["x"][b]), "gl": _lay_g(z["a_norm"][0]), "wA": wA, "wB": wB,
            "bfl": _rep(z["a_b_f"][0][hh * 8:hh * 8 + 8]), "qg": _rep(z["a_q_gain"][0]), "kg": _rep(z["a_k_gain"][0])}


_ROPE_INV = (500000.0 ** (-np.arange(8, dtype=np.float32) * (2.0 / 16))).astype(np.float32)


def _nsa_inputs(z, h1_b, kvp_b, pos_b, g, TT=T):
    w_in = z["b_w_in"][0]
    NTL = TT // 128
    NCB = TT // 16
    NCT = NCB // 128
    wq = np.ascontiguousarray(w_in[:, g * 512:(g + 1) * 512])
    gcols = [1024 + br * 16 + g * 8 + hl for br in range(3) for hl in range(8)]
    wg = np.ascontiguousarray(w_in[:, gcols])
    bgl = _rep(z["b_b_gate"][0][[c - 1024 for c in gcols]])
    kv = None if kvp_b is None else np.ascontiguousarray(kvp_b.reshape(TT, 6, 2, 64)[:, :, g, :].reshape(TT, 384))
    pos = np.ascontiguousarray(pos_b.reshape(NTL, 128).T).astype(np.int32)
    ends = np.minimum(np.arange(NCB) * 16 + 31, TT - 1)
    pose = np.ascontiguousarray(pos_b[ends].reshape(NCT, 128).T).astype(np.int32)
    kgains = np.ascontiguousarray(np.broadcast_to(np.stack([z["kc_gain"], z["ks_gain"], z["kw_gain"]])[None], (128, 3, 64))).astype(np.float32)
    peT = np.concatenate([z["kc_pe"].T, z["vc_pe"].T], 0).astype(np.float32)
    w1 = np.concatenate([z["kc_w1"].reshape(32, 64, 256).transpose(1, 0, 2), z["vc_w1"].reshape(32, 64, 256).transpose(1, 0, 2)], 0)
    w2 = np.stack([z["kc_w2"].reshape(2, 128, 64).transpose(1, 0, 2), z["vc_w2"].reshape(2, 128, 64).transpose(1, 0, 2)], 1)
    return {"h1": None if h1_b is None else np.ascontiguousarray(h1_b), "gl": _lay_g(z["b_norm"][0]), "wq": wq, "wg": wg, "bgl": bgl, "qg": _rep(z["b_q_gain"][0]),
            "kvp": kv, "pos": pos, "pose": pose, "kgains": kgains, "invf": _rep(_ROPE_INV), "peT": np.ascontiguousarray(peT),
            "w1": np.ascontiguousarray(w1.astype(np.float32)), "w2": np.ascontiguousarray(w2.astype(np.float32))}


def make_precast_work(P, ios):
    s = P.s
    stg = [P.sb(f"pc_stage{i}", [128, 1408], F32) for i in range(2)]
    cbf = [P.sb(f"pc_cb{i}", [128, 1408], BF16) for i in range(2)]
    ctr = [0]
    work = []

    def piece(src_ap, n, dst_ap, view=None):
        def emit():
            i = ctr[0] % 2
            ctr[0] += 1
            st_t, cb = stg[i], cbf[i]
            s.dma("sp", lambda q: q.dma_start(out=st_t[:, 0:n], in_=src_ap), writes=[f"pc_stage{i}"])
            s.op("pool", lambda e: e.tensor_copy(out=cb[:, 0:n], in_=st_t[:, 0:n]), reads=[f"pc_stage{i}"], writes=[f"pc_cb{i}"])
            srcv = cb[:, 0:n] if view is None else view(cb[:, 0:n])
            s.dma("sp", lambda q: q.dma_start(out=dst_ap, in_=srcv), reads=[f"pc_cb{i}"], final=True)
        work.append(emit)

    for io_ in ios:
        w_out, w_up, w_dn = io_["w_out"], io_["w_up"], io_["w_dn"]
        for kc in range(8):
            piece(w_out[kc * 128:(kc + 1) * 128, :], 1024, io_["wout_s"][kc * 128:(kc + 1) * 128, :])
        for kc in range(8):
            for cb_ in range(4):
                piece(w_up[kc * 128:(kc + 1) * 128, cb_ * 1408:(cb_ + 1) * 1408], 1408, io_["wup_s"][:, cb_ * 11:(cb_ + 1) * 11, kc, :],
                      view=lambda a: a.rearrange("p (f n) -> p f n", f=11))
        for j in range(22):
            piece(w_dn[j * 128:(j + 1) * 128, :], 1024, io_["wdn_s"][j * 128:(j + 1) * 128, :])
        if "kv_w" in io_:
            for kc in range(8):
                piece(io_["kv_w"][kc * 128:(kc + 1) * 128, :], 768, io_["kvw_s"][kc * 128:(kc + 1) * 128, :])
        io_["precast_done"] = True
    return work


GROUPS = [[0, 1], [2, 3], [4, 5], [6, 7]]
NT_FFN = 4096 + 128


def _cc_block(nc, tag, pairs):
    with nc.cleanup_on_exit():
        sem = nc.alloc_semaphore(name=tag + "cc")
        with nc.Block() as block:
            @block.gpsimd
            def _(g):
                for i, (a, b) in enumerate(pairs):
                    g.collective_compute("AllGather", ALU.bypass, replica_groups=GROUPS,
                                         ins=[a.ap().opt()], outs=[b.ap().opt()]).then_inc(sem)
                    g.wait_ge(sem, i + 1)


def _dump_block(nc, tag, src, dst):
    with nc.cleanup_on_exit():
        sem = nc.alloc_semaphore(name=tag + "dump")
        with nc.Block() as block:
            @block.sync
            def _(q):
                q.dma_start(out=dst.ap(), in_=src.ap()).then_inc(sem, 16)
                q.wait_ge(sem, 16)


def build_fused(upto=99):
    nc = bass.Bass("TRN2", target_bir_lowering=False)
    ext = {}

    def ein(name, shape, dt=F32):
        ext[name] = nc.dram_tensor(name, list(shape), dt, kind="ExternalInput")
        return ext[name].ap()

    NTL, NCT = T // 128, T // 16 // 128
    ioA = {"x": ein("A_x", [T, D]), "gl": ein("A_gl", [128, 8]), "wA": ein("A_wA", [2, D, 512]), "wB": ein("A_wB", [2, D, 260]),
           "bfl": ein("A_bfl", [128, 8]), "qg": ein("A_qg", [128, 64]), "kg": ein("A_kg", [128, 64])}
    msk = ein("msk", [128, 2])
    ioB = {"msk": msk, "hin": ein("B_hin", [NT_FFN, D]), "w_out": ein("B_w_out", [D, D]), "gl": ein("B_gl", [128, 8]),
           "w_up": ein("B_w_up", [D, 2 * DFF]), "cw": ein("B_cw", [128, NFC * 4]), "w_dn": ein("B_w_dn", [DFF, D]),
           "kvg": ein("B_kvg", [128, 8]), "kv_w": ein("B_kv_w", [D, 768]), "bg": ein("B_bg", [128, 8])}
    ioC = {"msk": msk, "wq": ein("C_wq", [D, 512]), "wg": ein("C_wg", [D, 24]), "bgl": ein("C_bgl", [128, 24]), "qg": ein("C_qg", [128, 64]),
           "pos": ein("C_pos", [128, NTL], I32), "pose": ein("C_pose", [128, NCT], I32), "kgains": ein("C_kgains", [128, 3, 64]),
           "invf": ein("C_invf", [128, 8]), "peT": ein("C_peT", [128, 32]), "w1": ein("C_w1", [128, 32, 256]), "w2": ein("C_w2", [128, 2, 2, 64])}
    ioD = {"msk": msk, "w_out": ein("D_w_out", [D, D]), "gl": ein("D_gl", [128, 8]), "w_up": ein("D_w_up", [D, 2 * DFF]),
           "cw": ein("D_cw", [128, NFC * 4]), "w_dn": ein("D_w_dn", [DFF, D])}
    out = nc.dram_tensor("out", [4096, D], F32, kind="ExternalOutput")
    oT1_my = [nc.dram_tensor(f"oT1_my{k}", [128, T], BF16) for k in range(4)]
    oT1_all = [nc.dram_tensor(f"oT1_all{k}", [256, T], BF16) for k in range(4)]
    h1_my = nc.dram_tensor("h1_my", [4096, D], F32)
    hl_send = nc.dram_tensor("hl_send", [128, D], F32)
    hl_all = nc.dram_tensor("hl_all", [256, D], F32)
    xnT_my = [nc.dram_tensor(f"xnT_my{k}", [256, 4096], BF16) for k in range(4)]
    xnT_all = [nc.dram_tensor(f"xnT_all{k}", [512, 4096], BF16) for k in range(4)]
    kv_send = [nc.dram_tensor(f"kv_send{k}", [1024, 384], F32) for k in range(8)]
    kv_all = [nc.dram_tensor(f"kv_all{k}", [2048, 384], F32) for k in range(8)]
    oT2_my = [nc.dram_tensor(f"oT2_my{k}", [128, T], BF16) for k in range(4)]
    oT2_all = [nc.dram_tensor(f"oT2_all{k}", [256, T], BF16) for k in range(4)]
    aps = lambda l: [t_.ap() for t_ in l]

    for pre, io_ in (("B", ioB), ("D", ioD)):
        io_["wout_s"] = nc.dram_tensor(pre + "_wout_s", [D, D], BF16).ap()
        io_["wup_s"] = nc.dram_tensor(pre + "_wup_s", [128, NFC, 8, 128], BF16).ap()
        io_["wdn_s"] = nc.dram_tensor(pre + "_wdn_s", [DFF, D], BF16).ap()
    ioB["kvw_s"] = nc.dram_tensor("B_kvw_s", [D, 768], BF16).ap()
    ioA["oT_l"] = aps(oT1_my)
    ioB.update({"oT_all_l": aps(oT1_all), "hout": h1_my.ap(), "kv_send_l": aps(kv_send), "xnT_my_l": aps(xnT_my), "hl_send": hl_send.ap()})
    ioC.update({"xnT_all_l": aps(xnT_all), "kv_all_l": aps(kv_all), "oT_l": aps(oT2_my)})
    ioD.update({"oT_all_l": aps(oT2_all), "h_my": h1_my.ap(), "hl_all": hl_all.ap(), "hout": out.ap()})

    with nc.cleanup_on_exit():
        PA = Prog(nc, "A_", ioA)
        ioA["bg_work"] = make_precast_work(PA, [ioB, ioD])
        build_fox(PA, T)
    if upto == 0:
        dbg = nc.dram_tensor("dbg", [128, T], BF16, kind="ExternalOutput")
        _dump_block(nc, "d0_", oT1_my[0], dbg)
        return nc
    _cc_block(nc, "e1_", list(zip(oT1_my, oT1_all)))
    if upto == 1:
        dbg = nc.dram_tensor("dbg", [256, T], BF16, kind="ExternalOutput")
        _dump_block(nc, "d1_", oT1_all[3], dbg)
        return nc
    with nc.cleanup_on_exit():
        build_ffn(Prog(nc, "B_", ioB), True, NT_FFN, False)
    if upto == 2:
        dbg = nc.dram_tensor("dbg", [4096, D], F32, kind="ExternalOutput")
        _dump_block(nc, "d2_", h1_my, dbg)
        return nc
    _cc_block(nc, "e2_", [(hl_send, hl_all)] + list(zip(xnT_my, xnT_all)) + list(zip(kv_send, kv_all)))
    with nc.cleanup_on_exit():
        build_nsa(Prog(nc, "C_", ioC), T)
    _cc_block(nc, "e3_", list(zip(oT2_my, oT2_all)))
    with nc.cleanup_on_exit():
        build_ffn(Prog(nc, "D_", ioD), False, NT_FFN, True)
    return nc


def _core_inputs(z, c):
    b, r = c // 2, c % 2
    d = {}
    for k, v in _fox_inputs(z, b, r).items():
        d["A_" + k] = v
    m = np.zeros((128, 2), np.float32)
    m[:, r] = 1.0
    d["msk"] = m
    hin = np.zeros((NT_FFN, D), np.float32)
    if r == 0:
        hin[128:] = z["x"][b, 0:4096]
    else:
        hin[:] = z["x"][b, 4096 - 128:8192]
    d["B_hin"] = hin
    for L, pre, w_out in ((0, "B_", z["a_w_out"][0]), (1, "D_", z["b_w_out"][0])):
        d[pre + "w_out"] = np.ascontiguousarray(w_out)
        d[pre + "gl"] = _lay_g(z["f_norm"][L])
        d[pre + "w_up"] = np.ascontiguousarray(z["f_w_up"][L])
        d[pre + "cw"] = _lay_cw(z["f_conv_w"][L], z["f_conv_b"][L])
        d[pre + "w_dn"] = np.ascontiguousarray(z["f_w_down"][L])
    d["B_kvg"] = _lay_g(z["kv_norm"])
    d["B_kv_w"] = np.ascontiguousarray(z["kv_w"])
    d["B_bg"] = _lay_g(z["b_norm"][0])
    ni = _nsa_inputs(z, None, None, z["positions"][b], r)
    for k in ("wq", "wg", "bgl", "qg", "pos", "pose", "kgains", "invf", "peT", "w1", "w2"):
        d["C_" + k] = ni[k]
    return d


def kernel(**inputs):
    z = {k: np.asarray(v) for k, v in inputs.items()}
    B = z["x"].shape[0]
    cores = list(range(8))
    nc = build_fused()
    res = run_bass_kernel_spmd(nc, [_core_inputs(z, c) for c in cores], core_ids=cores)
    out = np.stack([np.concatenate([res.results[2 * b]["out"], res.results[2 * b + 1]["out"]], 0) for b in range(B)], 0)
    return out.astype(np.float32)
```
